# Optimizing a Trainium2 kernel written in Bass

```python
import math
import jax
import jax.numpy as jnp
from jax import lax
import numpy as np

D_MODEL = 1024
BATCH = 32
SEQ = 2048
DEPTH = 2

GRID_W = 64
CTX_LEN = 256
ROPE_BASE = 10000.0
LN_EPS = 1e-6
RMS_EPS = 1e-6

RET_HEADS = 4
RET_HEAD_DIM = 128
RET_WIDTH = RET_HEADS * RET_HEAD_DIM
RET_CHUNK = 128

MLA_HEADS = 4
MLA_Q_RANK = 384
MLA_KV_RANK = 256
MLA_NOPE = 128
MLA_ROPE = 64
MLA_V = 128
MLA_WIDTH = MLA_HEADS * MLA_V
ATTN_BLOCK = 128

LRU_WIDTH = 512
LRU_BLOCKS = 8
LRU_BLOCK = LRU_WIDTH // LRU_BLOCKS
LRU_C = 8.0
LRU_CONV = 4

SSD_HEADS = 8
SSD_HEAD_DIM = 64
SSD_WIDTH = SSD_HEADS * SSD_HEAD_DIM
SSD_GROUPS = 2
SSD_STATE = 128
SSD_CONV = 4
SSD_CHUNK = 128
SSD_XBC = SSD_WIDTH + 2 * SSD_GROUPS * SSD_STATE

N_BRANCH = 4
BRANCH_WIDTH = 512
D_FF = 2816
FFN_CONV = 3

DEEPNORM_ALPHA = (2 * DEPTH) ** 0.25
DEEPNORM_BETA = (8 * DEPTH) ** -0.25

IN_SPLITS = (N_BRANCH * D_MODEL, RET_WIDTH, RET_WIDTH, RET_WIDTH, RET_WIDTH, MLA_Q_RANK, MLA_KV_RANK, MLA_ROPE, LRU_WIDTH, LRU_WIDTH, SSD_WIDTH, SSD_XBC, 2 * SSD_HEADS)
IN_WIDTH = sum(IN_SPLITS)

kernel_name = 'hybrid_bidir_diffusion_trunk'


def layer_norm(x):
    xf = x.astype(jnp.float32)
    mu = jnp.mean(xf, axis=-1, keepdims=True)
    var = jnp.mean(jnp.square(xf - mu), axis=-1, keepdims=True)
    return ((xf - mu) * lax.rsqrt(var + LN_EPS)).astype(x.dtype)


def rms_norm(x, w):
    xf = x.astype(jnp.float32)
    y = xf * lax.rsqrt(jnp.mean(jnp.square(xf), axis=-1, keepdims=True) + RMS_EPS)
    return (y * w.astype(jnp.float32)).astype(x.dtype)


def flip(t):
    return jnp.flip(t, axis=1)


def centred_conv(x, w, b):
    k, ch = w.shape
    left = (k - 1) // 2
    y = lax.conv_general_dilated(x, w[:, None, :].astype(x.dtype), window_strides=(1,), padding=[(left, k - 1 - left)], dimension_numbers=('NWC', 'WIO', 'NWC'), feature_group_count=ch)
    return y + b.astype(x.dtype)


def axial_rope(rows, dim):
    r, col = jnp.meshgrid(jnp.arange(rows, dtype=jnp.float32), jnp.arange(GRID_W, dtype=jnp.float32), indexing='ij')
    quarter = dim // 4
    inv = ROPE_BASE ** (-jnp.arange(quarter, dtype=jnp.float32) / quarter)
    ang = jnp.concatenate([r.reshape(-1, 1) * inv, col.reshape(-1, 1) * inv], axis=-1)
    return jnp.cos(ang), jnp.sin(ang)


def apply_rope(x, cos, sin):
    half = x.shape[-1] // 2
    x1, x2 = x[..., :half], x[..., half:]
    cos = cos[:, None, :].astype(x.dtype)
    sin = sin[:, None, :].astype(x.dtype)
    return jnp.concatenate([x1 * cos - x2 * sin, x1 * sin + x2 * cos], axis=-1)


def modulate(h, shift, scale):
    return layer_norm(h) * (1.0 + scale) + shift


def post_norm(x, delta, w, b):
    return layer_norm(DEEPNORM_ALPHA * x + delta) * w + b


def retention_scan(q, k, v, log_g, s0):
    b, t, h, dk = q.shape
    dv = v.shape[-1]
    n, cs = t // RET_CHUNK, RET_CHUNK
    f32 = jnp.float32
    qc = q.reshape(b, n, cs, h, dk).astype(f32)
    kc = k.reshape(b, n, cs, h, dk).astype(f32)
    vc = v.reshape(b, n, cs, h, dv).astype(f32)
    pos = jnp.arange(cs, dtype=f32)
    diff = pos[:, None] - pos[None, :]
    decay = jnp.where((diff >= 0)[..., None], jnp.exp(jnp.maximum(diff, 0.0)[..., None] * log_g), 0.0)
    scores = jnp.einsum('bnihd,bnjhd->bnhij', qc, kc) * decay.transpose(2, 0, 1)
    y = jnp.einsum('bnhij,bnjhe->bnihe', scores, vc)
    q_dec = jnp.exp((pos + 1.0)[:, None] * log_g)
    k_dec = jnp.exp((cs - 1.0 - pos)[:, None] * log_g)
    chunk_kv = jnp.einsum('bnjhd,bnjhe->nbhde', kc * k_dec[:, :, None], vc)
    chunk_dec = jnp.exp(cs * log_g)[:, None, None]

    def step(s, kv):
        return s * chunk_dec + kv, s

    s_last, s_prev = lax.scan(step, s0.astype(f32), chunk_kv)
    y = y + jnp.einsum('bnihd,nbhde->bnihe', qc * q_dec[:, :, None], s_prev)
    return y.reshape(b, t, h, dv), s_last


def retention_branch(ctx_in, lat_in, rope, decay_logit, gn_w, gn_b):
    cos, sin = rope
    log_g = jax.nn.log_sigmoid(decay_logit.astype(jnp.float32))
    scale = RET_HEAD_DIM ** -0.5

    def heads(t):
        return t.reshape(t.shape[0], t.shape[1], RET_HEADS, RET_HEAD_DIM)

    qc, kc, vc, gc = ctx_in
    qx, kx, vx, gx = lat_in
    qc, kc, vc = heads(qc), heads(kc) * scale, heads(vc)
    qx, kx, vx = apply_rope(heads(qx), cos, sin), apply_rope(heads(kx), cos, sin) * scale, heads(vx)
    s0 = jnp.zeros((qc.shape[0], RET_HEADS, RET_HEAD_DIM, RET_HEAD_DIM), jnp.float32)
    yc_f, sc_f = retention_scan(qc, kc, vc, log_g[0], s0)
    yc_b, sc_b = retention_scan(flip(qc), flip(kc), flip(vc), log_g[1], s0)
    yx_f, _ = retention_scan(qx, kx, vx, log_g[0], sc_f)
    yx_b, _ = retention_scan(flip(qx), flip(kx), flip(vx), log_g[1], sc_b)

    def finish(y, g):
        y = layer_norm(y).reshape(g.shape) * gn_w + gn_b
        return (jax.nn.silu(g) * y.astype(g.dtype)).astype(g.dtype)

    return finish(yc_f + flip(yc_b), gc), finish(yx_f + flip(yx_b), gx)


def softmax_attend(q, k, v, scale):
    s = jnp.einsum('bqhd,bkhd->bhqk', q, k, preferred_element_type=jnp.float32) * scale
    p = jax.nn.softmax(s, axis=-1).astype(v.dtype)
    return jnp.einsum('bhqk,bkhd->bqhd', p, v)


def blocked_attend(q, k, v, scale):
    b, t, h, d = q.shape
    nb = t // ATTN_BLOCK
    qb = q.reshape(b, nb, ATTN_BLOCK, h, d).swapaxes(0, 1)
    out = lax.map(lambda qq: softmax_attend(qq, k, v, scale), qb)
    return out.swapaxes(0, 1).reshape(b, t, h, v.shape[-1])


def mla_branch(ctx_in, lat_in, rope, q_norm_w, w_uq, kv_norm_w, w_ukv):
    cos, sin = rope

    def qkv(cq, ckv, kr, rotate):
        b, t, _ = cq.shape
        q = (rms_norm(cq, q_norm_w) @ w_uq).reshape(b, t, MLA_HEADS, MLA_NOPE + MLA_ROPE)
        kv = (rms_norm(ckv, kv_norm_w) @ w_ukv).reshape(b, t, MLA_HEADS, MLA_NOPE + MLA_V)
        q_nope, q_rope = q[..., :MLA_NOPE], q[..., MLA_NOPE:]
        k_nope, v = kv[..., :MLA_NOPE], kv[..., MLA_NOPE:]
        k_rope = kr[:, :, None, :]
        if rotate:
            q_rope = apply_rope(q_rope, cos, sin)
            k_rope = apply_rope(k_rope, cos, sin)
        k_rope = jnp.broadcast_to(k_rope, (b, t, MLA_HEADS, MLA_ROPE))
        return jnp.concatenate([q_nope, q_rope], -1), jnp.concatenate([k_nope, k_rope], -1), v

    qc, kc, vc = qkv(*ctx_in, rotate=False)
    qx, kx, vx = qkv(*lat_in, rotate=True)
    scale = (MLA_NOPE + MLA_ROPE) ** -0.5
    oc = softmax_attend(qc, kc, vc, scale)
    ox = blocked_attend(qx, jnp.concatenate([kc, kx], 1), jnp.concatenate([vc, vx], 1), scale)
    return oc.reshape(oc.shape[0], oc.shape[1], MLA_WIDTH), ox.reshape(ox.shape[0], ox.shape[1], MLA_WIDTH)


def linear_scan(a, b, h0):
    def combine(l, r):
        return l[0] * r[0], r[0] * l[1] + r[1]

    a_cum, h = lax.associative_scan(combine, (a, b), axis=1)
    h = h + a_cum * h0[:, None, :]
    return h, h[:, -1]


def rglru_branch(ctx_in, lat_in, conv_w, conv_b, gate_w, gate_b, lam):
    f32 = jnp.float32

    def gates(u, d):
        b, t, w = u.shape
        ub = u.reshape(b, t, LRU_BLOCKS, LRU_BLOCK).astype(f32)
        z = jnp.einsum('btnc,gncd->gbtnd', ub, gate_w[d].astype(f32)).reshape(2, b, t, w) + gate_b[d].astype(f32)[:, None, None, :]
        r, i = jax.nn.sigmoid(z[0]), jax.nn.sigmoid(z[1])
        log_a = -LRU_C * r * jax.nn.softplus(-lam[d].astype(f32))
        return jnp.exp(log_a), jnp.sqrt(-jnp.expm1(2.0 * log_a)) * (i * u.astype(f32))

    def run(u, d, h0, reverse):
        a, bb = gates(u, d)
        if reverse:
            a, bb = flip(a), flip(bb)
        h, h_last = linear_scan(a, bb, h0)
        return (flip(h) if reverse else h), h_last

    xc, gc = ctx_in
    xx, gx = lat_in
    uc = centred_conv(xc, conv_w, conv_b)
    ux = centred_conv(xx, conv_w, conv_b)
    h0 = jnp.zeros((xc.shape[0], LRU_WIDTH), f32)
    hcf, scf = run(uc, 0, h0, False)
    hcb, scb = run(uc, 1, h0, True)
    hxf, _ = run(ux, 0, scf, False)
    hxb, _ = run(ux, 1, scb, True)
    yc = ((hcf + hcb) * jax.nn.gelu(gc.astype(f32))).astype(gc.dtype)
    yx = ((hxf + hxb) * jax.nn.gelu(gx.astype(f32))).astype(gx.dtype)
    return yc, yx


def ssd_scan(x, dt, a, bm, cm, s0):
    b, t, h, p = x.shape
    g, n = bm.shape[2], bm.shape[3]
    hg, nc, cs = h // g, t // SSD_CHUNK, SSD_CHUNK
    f32 = jnp.float32
    xs = x.reshape(b, nc, cs, g, hg, p).astype(f32)
    dts = dt.reshape(b, nc, cs, g, hg).astype(f32)
    bs = bm.reshape(b, nc, cs, g, n).astype(f32)
    cs_ = cm.reshape(b, nc, cs, g, n).astype(f32)
    a_cum = jnp.cumsum(dts * a.reshape(g, hg), axis=2)
    seg = a_cum[:, :, :, None] - a_cum[:, :, None, :]
    causal = jnp.tril(jnp.ones((cs, cs), bool))[:, :, None, None]
    lmat = jnp.exp(jnp.where(causal, seg, -jnp.inf))
    cb = jnp.einsum('bnigd,bnjgd->bnijg', cs_, bs)
    w_intra = cb[..., None] * lmat * dts[:, :, None]
    y = jnp.einsum('bnijgh,bnjghp->bnighp', w_intra, xs)
    decay_end = jnp.exp(a_cum[:, :, -1:] - a_cum)
    states = jnp.einsum('bncgd,bncghp->nbghpd', bs, xs * (decay_end * dts)[..., None])
    chunk_dec = jnp.exp(a_cum[:, :, -1]).swapaxes(0, 1)

    def step(s, inp):
        st, dec = inp
        return s * dec[..., None, None] + st, s

    s_last, s_prev = lax.scan(step, s0.astype(f32).reshape(b, g, hg, p, n), (states, chunk_dec))
    y = y + jnp.einsum('bncgd,nbghpd->bncghp', cs_, s_prev) * jnp.exp(a_cum)[..., None]
    return y.reshape(b, t, h, p), s_last.reshape(b, h, p, n)


def ssd_branch(ctx_in, lat_in, conv_w, conv_b, dt_bias, a_log, d_skip, norm_w):
    f32 = jnp.float32
    a = -jnp.exp(a_log.astype(f32))
    gn = SSD_GROUPS * SSD_STATE

    def prep(z, xbc, dt_raw):
        b, t, _ = z.shape
        xbc = jax.nn.silu(centred_conv(xbc, conv_w, conv_b))
        xs = xbc[..., :SSD_WIDTH].reshape(b, t, SSD_HEADS, SSD_HEAD_DIM)
        bm = xbc[..., SSD_WIDTH:SSD_WIDTH + gn].reshape(b, t, SSD_GROUPS, SSD_STATE)
        cm = xbc[..., SSD_WIDTH + gn:].reshape(b, t, SSD_GROUPS, SSD_STATE)
        dt = jax.nn.softplus(dt_raw.astype(f32).reshape(b, t, 2, SSD_HEADS) + dt_bias.astype(f32))
        return xs, bm, cm, dt

    def run(xs, bm, cm, dt, d, s_init, reverse):
        dtd = dt[:, :, d]
        if reverse:
            xs, bm, cm, dtd = flip(xs), flip(bm), flip(cm), flip(dtd)
        y, s = ssd_scan(xs, dtd, a[d], bm, cm, s_init)
        return (flip(y) if reverse else y), s

    def finish(z, xs, yf, yb):
        b, t, _ = z.shape
        y = yf + yb + d_skip.astype(f32)[:, None] * xs.astype(f32)
        y = y.reshape(b, t, SSD_WIDTH) * jax.nn.silu(z.astype(f32))
        return rms_norm(y, norm_w).astype(z.dtype)

    zc, zx = ctx_in[0], lat_in[0]
    xs_c, bm_c, cm_c, dt_c = prep(*ctx_in)
    xs_x, bm_x, cm_x, dt_x = prep(*lat_in)
    s0 = jnp.zeros((zc.shape[0], SSD_HEADS, SSD_HEAD_DIM, SSD_STATE), f32)
    ycf, scf = run(xs_c, bm_c, cm_c, dt_c, 0, s0, False)
    ycb, scb = run(xs_c, bm_c, cm_c, dt_c, 1, s0, True)
    yxf, _ = run(xs_x, bm_x, cm_x, dt_x, 0, scf, False)
    yxb, _ = run(xs_x, bm_x, cm_x, dt_x, 1, scb, True)
    return finish(zc, xs_c, ycf, ycb), finish(zx, xs_x, yxf, yxb)


def token_mixer(h_ctx, h_lat, rope_ret, rope_mla, w_in, ret_decay, ret_gn_w, ret_gn_b, mla_q_norm, mla_w_uq, mla_kv_norm, mla_w_ukv, lru_conv_w, lru_conv_b, lru_gate_w, lru_gate_b, lru_lambda, ssd_conv_w, ssd_conv_b, ssd_dt_bias, ssd_a_log, ssd_d, ssd_norm_w, w_branch, w_out, with_ctx):
    offsets = [int(o) for o in np.cumsum(IN_SPLITS)[:-1]]
    pc = jnp.split(h_ctx @ w_in, offsets, axis=-1)
    px = jnp.split(h_lat @ w_in, offsets, axis=-1)
    ret = retention_branch(pc[1:5], px[1:5], rope_ret, ret_decay, ret_gn_w, ret_gn_b)
    mla = mla_branch(pc[5:8], px[5:8], rope_mla, mla_q_norm, mla_w_uq, mla_kv_norm, mla_w_ukv)
    lru = rglru_branch(pc[8:10], px[8:10], lru_conv_w, lru_conv_b, lru_gate_w, lru_gate_b, lru_lambda)
    ssd = ssd_branch(pc[10:13], px[10:13], ssd_conv_w, ssd_conv_b, ssd_dt_bias, ssd_a_log, ssd_d, ssd_norm_w)

    def merge(gate_logits, outs):
        gates = jax.nn.sigmoid(gate_logits.astype(jnp.float32)).astype(gate_logits.dtype)
        acc = gates[..., :D_MODEL] * (outs[0] @ w_branch[0])
        for i in range(1, N_BRANCH):
            acc = acc + gates[..., i * D_MODEL:(i + 1) * D_MODEL] * (outs[i] @ w_branch[i])
        return acc @ w_out

    out_lat = merge(px[0], (ret[1], mla[1], lru[1], ssd[1]))
    out_ctx = merge(pc[0], (ret[0], mla[0], lru[0], ssd[0])) if with_ctx else None
    return out_ctx, out_lat


def conv_ffn(h, w_up, conv_w, conv_b, w_down):
    u = centred_conv(h @ w_up, conv_w, conv_b)
    g, v = u[..., :D_FF], u[..., D_FF:]
    return (jax.nn.silu(g) * v) @ w_down


def setup_inputs(seed: int = 0) -> dict:
    key = jax.random.key(seed)
    keys = iter(jax.random.split(key, 48))
    f32 = jnp.float32
    L, D = DEPTH, D_MODEL

    def normal(shape, scale):
        return jax.random.normal(next(keys), shape, f32) * scale

    def uniform(shape, lo, hi):
        return jax.random.uniform(next(keys), shape, f32, lo, hi)

    def gain(shape):
        return 1.0 + normal(shape, 0.02)

    ret_gamma_logit = jnp.log(2.0 ** (5.0 + jnp.arange(RET_HEADS, dtype=f32)) - 1.0)
    lru_a = uniform((L, 2, LRU_WIDTH), 0.9, 0.999) ** (1.0 / LRU_C)
    dt0 = jnp.exp(uniform((L, 2, SSD_HEADS), math.log(1e-3), math.log(1e-1)))
    return {
        'x': normal((BATCH, SEQ, D), 1.0),
        'c': normal((BATCH, D), 1.0),
        'ctx': normal((BATCH, CTX_LEN, D), 1.0),
        'c_ctx': normal((D,), 1.0),
        'ada_w': normal((L, D, 6 * D), 0.5 * D ** -0.5),
        'ada_b': normal((L, 6 * D), 0.02),
        'w_in': normal((L, D, IN_WIDTH), D ** -0.5),
        'ret_decay': ret_gamma_logit + normal((L, 2, RET_HEADS), 0.1),
        'ret_gn_w': gain((L, RET_WIDTH)),
        'ret_gn_b': normal((L, RET_WIDTH), 0.02),
        'mla_q_norm': gain((L, MLA_Q_RANK)),
        'mla_w_uq': normal((L, MLA_Q_RANK, MLA_HEADS * (MLA_NOPE + MLA_ROPE)), MLA_Q_RANK ** -0.5),
        'mla_kv_norm': gain((L, MLA_KV_RANK)),
        'mla_w_ukv': normal((L, MLA_KV_RANK, MLA_HEADS * (MLA_NOPE + MLA_V)), MLA_KV_RANK ** -0.5),
        'lru_conv_w': normal((L, LRU_CONV, LRU_WIDTH), LRU_CONV ** -0.5),
        'lru_conv_b': normal((L, LRU_WIDTH), 0.02),
        'lru_gate_w': normal((L, 2, 2, LRU_BLOCKS, LRU_BLOCK, LRU_BLOCK), LRU_BLOCK ** -0.5),
        'lru_gate_b': normal((L, 2, 2, LRU_WIDTH), 0.02),
        'lru_lambda': jnp.log(lru_a) - jnp.log1p(-lru_a),
        'ssd_conv_w': normal((L, SSD_CONV, SSD_XBC), SSD_CONV ** -0.5),
        'ssd_conv_b': normal((L, SSD_XBC), 0.02),
        'ssd_dt_bias': dt0 + jnp.log(-jnp.expm1(-dt0)),
        'ssd_a_log': jnp.log(uniform((L, 2, SSD_HEADS), 1.0, 16.0)),
        'ssd_d': gain((L, SSD_HEADS)),
        'ssd_norm_w': gain((L, SSD_WIDTH)),
        'w_branch': normal((L, N_BRANCH, BRANCH_WIDTH, D), DEEPNORM_BETA * BRANCH_WIDTH ** -0.5),
        'w_out': normal((L, D, D), DEEPNORM_BETA * D ** -0.5),
        'ln1_w': gain((L, D)),
        'ln1_b': normal((L, D), 0.02),
        'ffn_w_up': normal((L, D, 2 * D_FF), D ** -0.5),
        'ffn_conv_w': normal((L, FFN_CONV, 2 * D_FF), FFN_CONV ** -0.5),
        'ffn_conv_b': normal((L, 2 * D_FF), 0.02),
        'ffn_w_down': normal((L, D_FF, D), DEEPNORM_BETA * D_FF ** -0.5),
        'ln2_w': gain((L, D)),
        'ln2_b': normal((L, D), 0.02),
    }


def reference(x, c, ctx, c_ctx, ada_w, ada_b, w_in, ret_decay, ret_gn_w, ret_gn_b, mla_q_norm, mla_w_uq, mla_kv_norm, mla_w_ukv, lru_conv_w, lru_conv_b, lru_gate_w, lru_gate_b, lru_lambda, ssd_conv_w, ssd_conv_b, ssd_dt_bias, ssd_a_log, ssd_d, ssd_norm_w, w_branch, w_out, ln1_w, ln1_b, ffn_w_up, ffn_conv_w, ffn_conv_b, ffn_w_down, ln2_w, ln2_b):
    rows = x.shape[1] // GRID_W
    rope_ret = axial_rope(rows, RET_HEAD_DIM)
    rope_mla = axial_rope(rows, MLA_ROPE)
    silu_c = jax.nn.silu(c)
    silu_cc = jax.nn.silu(c_ctx)
    h_lat, h_ctx = x, ctx
    for l in range(DEPTH):
        with_ctx = l < DEPTH - 1
        mod_x = jnp.split((silu_c @ ada_w[l] + ada_b[l])[:, None, :], 6, axis=-1)
        mod_c = jnp.split((silu_cc @ ada_w[l] + ada_b[l])[None, None, :], 6, axis=-1)
        o_ctx, o_lat = token_mixer(modulate(h_ctx, mod_c[0], mod_c[1]), modulate(h_lat, mod_x[0], mod_x[1]), rope_ret, rope_mla, w_in[l], ret_decay[l], ret_gn_w[l], ret_gn_b[l], mla_q_norm[l], mla_w_uq[l], mla_kv_norm[l], mla_w_ukv[l], lru_conv_w[l], lru_conv_b[l], lru_gate_w[l], lru_gate_b[l], lru_lambda[l], ssd_conv_w[l], ssd_conv_b[l], ssd_dt_bias[l], ssd_a_log[l], ssd_d[l], ssd_norm_w[l], w_branch[l], w_out[l], with_ctx)
        new_lat = post_norm(h_lat, mod_x[2] * o_lat, ln1_w[l], ln1_b[l])
        f_lat = conv_ffn(modulate(new_lat, mod_x[3], mod_x[4]), ffn_w_up[l], ffn_conv_w[l], ffn_conv_b[l], ffn_w_down[l])
        new_lat = post_norm(new_lat, mod_x[5] * f_lat, ln2_w[l], ln2_b[l])
        if with_ctx:
            new_ctx = post_norm(h_ctx, mod_c[2] * o_ctx, ln1_w[l], ln1_b[l])
            f_ctx = conv_ffn(modulate(new_ctx, mod_c[3], mod_c[4]), ffn_w_up[l], ffn_conv_w[l], ffn_conv_b[l], ffn_w_down[l])
            h_ctx = post_norm(new_ctx, mod_c[5] * f_ctx, ln2_w[l], ln2_b[l])
        h_lat = new_lat
    return h_lat
```

```python
import os
from contextlib import ExitStack, contextmanager
import numpy as np
import ml_dtypes
import concourse.bass as bass
import concourse.mybir as mybir
from concourse.bass_utils import run_bass_kernel_spmd

F32 = mybir.dt.float32
BF16 = mybir.dt.bfloat16
AF = mybir.ActivationFunctionType
ALU = mybir.AluOpType
AX = mybir.AxisListType

D = 1024
SEQ = 2048
CTX = 256
T = SEQ + CTX
NT = T // 128
L = 2
NB = 4
INW = 9424
DFF = 2816
ALPHA = (2 * L) ** 0.25
O_GATE, O_RQ, O_RK, O_RV, O_RG = 0, 4096, 4608, 5120, 5632
O_CQ, O_CKV, O_KR = 6144, 6528, 6784
O_LX, O_LG = 6848, 7360
O_SZ, O_SX, O_SDT = 7872, 8384, 9408
NEG = -30000.0


class Sem:
    def __init__(self, h):
        self.h = h
        self.total = 0


class Buf:
    __slots__ = ("name", "w", "r", "dsem", "psum")

    def __init__(self, name):
        self.name = name
        self.w = None
        self.r = {}
        self.dsem = None
        self.psum = False


class TT:
    def __init__(self, t, nb=1, name=""):
        self.t = t
        self.b = [Buf(f"{name}{i}") for i in range(nb)]
        self.b2 = None

    def rb(self, n):
        return [self.b[n]] if self.b2 is None else [self.b[n], self.b2[n]]

    def __getitem__(self, k):
        return self.t[k]


class Eng:
    def __init__(self, name, e, sem):
        self.name = name
        self.e = e
        self.sem = sem
        self.cnt = 0
        self.seen = {}

    def wait(self, sem, val):
        if self.seen.get(sem, 0) >= val:
            return
        self.e.wait_ge(sem.h, val)
        self.seen[sem] = val


def _flat(x):
    out = []
    if isinstance(x, (Buf, TT)):
        x = [x]
    for a in x:
        if a is None:
            continue
        if isinstance(a, Buf):
            out.append(a)
        elif isinstance(a, TT):
            out.extend(a.b)
            if a.b2 is not None:
                out.extend(a.b2)
        elif isinstance(a, (list, tuple)):
            out.extend(_flat(a))
        else:
            raise TypeError(f"bad dep object {type(a)}")
    return out


class K:
    def __init__(self, nc, es):
        self.nc = nc
        self.es = es
        self.E = {}
        for name, e in (("pe", nc.tensor), ("act", nc.scalar), ("dve", nc.vector), ("pool", nc.gpsimd), ("sp", nc.sync)):
            self.E[name] = Eng(name, e, Sem(es.enter_context(nc.semaphore("s_" + name))))
        self.dfree = [Sem(es.enter_context(nc.semaphore(f"dq{i}"))) for i in range(80)]
        self.dall = list(self.dfree)
        self.scopes = []
        self.uid = 0
        self.nins = 0

    @contextmanager
    def scope(self):
        st = ExitStack()
        rec = {"st": st, "bufs": []}
        self.scopes.append(rec)
        try:
            yield
        finally:
            self.barrier()
            for b in rec["bufs"]:
                if b.dsem is not None:
                    self.dfree.append(b.dsem)
                    b.dsem = None
            self.scopes.pop()
            st.close()

    def sb(self, name, shape, dt, nb=1):
        self.uid += 1
        st = self.scopes[-1]["st"] if self.scopes else self.es
        t = st.enter_context(self.nc.sbuf_tensor(f"{name}_{self.uid}", list(shape), dt))
        tt = TT(t, nb, name)
        if self.scopes:
            self.scopes[-1]["bufs"].extend(tt.b)
        return tt

    def rot(self, name, shape, dt, n):
        return Rot([self.sb(f"{name}{i}", shape, dt) for i in range(n)])

    def _deps(self, E, r, w):
        deps = []
        for b in r:
            if b.w is not None:
                deps.append(b.w)
            if b.psum:
                deps.extend((s_, v_) for s_, v_ in b.r.items() if s_ is not E.sem)
        for b in w:
            if b.w is not None:
                deps.append(b.w)
            deps.extend(b.r.items())
        for s, v in deps:
            if E.name == "pe" and s is E.sem:
                continue
            E.wait(s, v)

    def _record(self, ev, r, w):
        s, v = ev
        for b in r:
            if b.r.get(s, 0) < v:
                b.r[s] = v
        for b in w:
            b.w = ev
            b.r = {}

    def op(self, eng, fn, r=(), w=(), ww=()):
        E = self.E[eng]
        r = _flat(r)
        w = _flat(w)
        self._deps(E, r, w)
        ins = fn(E.e)
        E.cnt += 1
        ins.then_inc(E.sem.h, 1)
        self._record((E.sem, E.cnt), r, w)
        for b in _flat(ww):
            b.w = (E.sem, E.cnt)
        self.nins += 1
        return ins

    def dma(self, out, in_, key, r=(), w=(), **kw):
        E = self.E["sp"]
        r = _flat(r)
        w = _flat(w)
        key = _flat([key])[0]
        self._deps(E, r, w)
        if key.dsem is None:
            key.dsem = self.dfree.pop()
        s = key.dsem
        ins = E.e.dma_start(out=out, in_=in_, **kw)
        s.total += 16
        ins.then_inc(s.h, 16)
        self._record((s, s.total), r, w)
        self.nins += 1

    def dma_group(self, pairs, key, r=(), w=()):
        E = self.E["sp"]
        r = _flat(r)
        w = _flat(w)
        key = _flat([key])[0]
        self._deps(E, r, w)
        if key.dsem is None:
            key.dsem = self.dfree.pop()
        s = key.dsem
        for out, in_ in pairs:
            ins = E.e.dma_start(out=out, in_=in_)
            s.total += 16
            ins.then_inc(s.h, 16)
            self.nins += 1
        self._record((s, s.total), r, w)

    def barrier(self):
        sp = self.E["sp"]
        for n, E in self.E.items():
            if n != "sp" and E.cnt > 0:
                sp.wait(E.sem, E.cnt)
        for s in self.dall:
            if s.total > 0:
                sp.wait(s, s.total)
        sp.cnt += 1
        sp.e.sem_inc(sp.sem.h, 1)
        for n, E in self.E.items():
            if n != "sp":
                E.wait(sp.sem, sp.cnt)

    def mm(self, out, lhsT, rhs, start, stop, r, w):
        return self.op("pe", lambda e: e.matmul(out, lhsT=lhsT, rhs=rhs, start=start, stop=stop), r=r, w=w)

    def tr(self, out, in_, ident, r, w):
        return self.op("pe", lambda e: e.transpose(out, in_, ident), r=r, w=w)

    def act(self, out, in_, func, r, w, scale=1.0, bias=0.0, eng="act", ww=()):
        return self.op(eng, lambda e: e.activation(out=out, in_=in_, func=func, scale=scale, bias=bias), r=r, w=w, ww=ww)

    def tt(self, eng, out, in0, in1, op, r, w):
        return self.op(eng, lambda e: e.tensor_tensor(out=out, in0=in0, in1=in1, op=op), r=r, w=w)

    def ts(self, eng, out, in0, s1, s2, op0, op1, r, w, ww=()):
        if s2 is None:
            return self.op(eng, lambda e: e.tensor_scalar(out=out, in0=in0, scalar1=s1, scalar2=None, op0=op0), r=r, w=w, ww=ww)
        return self.op(eng, lambda e: e.tensor_scalar(out=out, in0=in0, scalar1=s1, scalar2=s2, op0=op0, op1=op1), r=r, w=w, ww=ww)

    def stt(self, out, in0, scalar, in1, op0, op1, r, w, eng="dve"):
        return self.op(eng, lambda e: e.scalar_tensor_tensor(out=out, in0=in0, scalar=scalar, in1=in1, op0=op0, op1=op1), r=r, w=w)

    def preload_lnexp(self):
        return

    def cp(self, eng, out, in_, r, w):
        if eng == "act":
            return self.op("act", lambda e: e.copy(out=out, in_=in_), r=r, w=w)
        return self.op(eng, lambda e: e.tensor_copy(out=out, in_=in_), r=r, w=w)


class Rot:
    def __init__(self, items):
        self.items = items
        self.i = 0

    def next(self):
        x = self.items[self.i % len(self.items)]
        self.i += 1
        return x


def run_pipe(steps, depth=2):
    st = {}
    n = len(steps)
    for i in range(min(depth, n)):
        st[i] = steps[i][0]()
    for i in range(n):
        if i + depth < n:
            st[i + depth] = steps[i + depth][0]()
        steps[i][1](st.pop(i))


def tiles_of(t0, n):
    return list(range(t0 // 128, (t0 + n + 127) // 128))


BLOCKS = [(0, 256)] + [(256 + 512 * m, 512) for m in range(4)]


class Prog:
    def __init__(self, nb=NB, nl=L, dbg=None):
        self.nb, self.nl, self.dbg = nb, nl, dbg or {}
        self.nc = nc = bass.Bass("TRN2", target_bir_lowering=False)
        self.es = ExitStack()
        self.k = K(nc, self.es)
        self.din = {}
        self.dbg_out = {}

        def inp(name, shape, dt=F32):
            self.din[name] = nc.dram_tensor(name, list(shape), dt, kind="ExternalInput").ap()
            return self.din[name]

        inp("x", [nb, SEQ, D]); inp("ctx", [nb, CTX, D]); inp("c5T", [128, 8 * 5])
        inp("ada_w", [L, D, 6 * D]); inp("ada_b", [L, 6 * D]); inp("w_in", [L, D, INW])
        inp("ret_decay", [L, 8]); inp("ret_gn_w", [L, 512]); inp("ret_gn_b", [L, 512])
        inp("mla_q_norm", [L, 384]); inp("mla_w_uq", [L, 384, 768]); inp("mla_kv_norm", [L, 256]); inp("mla_w_ukv", [L, 256, 1024])
        inp("lru_conv_w", [L, 4, 512]); inp("lru_conv_b", [L, 512]); inp("lru_gate_w", [L, 2, 2, 8, 64, 64])
        inp("lru_gate_b", [L, 2, 2, 512]); inp("lru_lambda", [L, 2, 512])
        inp("ssd_conv_w", [L, 4, 1024]); inp("ssd_conv_b", [L, 1024]); inp("ssd_dt_bias", [L, 16]); inp("ssd_a_log", [L, 16])
        inp("ssd_d", [L, 8]); inp("ssd_norm_w", [L, 512])
        inp("w_branch", [L, 2048, D]); inp("w_out", [L, D, D]); inp("ln1_w", [L, D]); inp("ln1_b", [L, D])
        inp("ffn_w_up", [L, D, 2 * DFF]); inp("ffn_conv_w", [L, 3, 2 * DFF]); inp("ffn_conv_b", [L, 2 * DFF]); inp("ffn_w_down", [L, DFF, D])
        inp("ln2_w", [L, D]); inp("ln2_b", [L, D])
        inp("k_ident", [128, 128]); inp("k_ones", [128, 128]); inp("k_identb", [128, 128], BF16); inp("k_onesb", [128, 128], BF16)
        inp("k_retC", [128, SEQ]); inp("k_retS", [128, SEQ]); inp("k_mlaC", [128, SEQ]); inp("k_mlaS", [128, SEQ])
        inp("k_mf", [128, 128]); inp("k_mb", [128, 128]); inp("k_up", [128, 128]); inp("k_lo", [128, 128])
        inp("k_posf", [128, T]); inp("k_posb", [128, T]); inp("k_posf_tm", [128, NT]); inp("k_posb_tm", [128, NT])
        self.out = nc.dram_tensor("out", [nb, SEQ, D], F32, kind="ExternalOutput").ap()

        def scr(name, shape, dt):
            return nc.dram_tensor(name, list(shape), dt, kind="Internal").ap()

        self.WB = {
            "in": scr("wb_in", [L, D, INW], BF16), "uq": scr("wb_uq", [L, 384, 768], BF16), "ukv": scr("wb_ukv", [L, 256, 1024], BF16),
            "br": scr("wb_br", [L, 2048, D], BF16), "out": scr("wb_out", [L, D, D], BF16), "up": scr("wb_up", [L, D, 2 * DFF], BF16),
            "dn": scr("wb_dn", [L, DFF, D], BF16),
        }
        self.MOD = scr("modscr", [L, 5, 6 * D], F32)
        self.HS = [scr("h_a", [T, D], F32), scr("h_b", [T, D], F32)]

    def dump(self, name, tt_or_ap, shape, dt, r):
        k = self.k
        o = self.nc.dram_tensor("dbg_" + name, list(shape), dt, kind="ExternalOutput").ap()
        self.dbg_out[name] = o
        key = _flat([r])[0]
        k.dma(o, tt_or_ap, key=key, r=r)

    def build(self):
        k = self.k
        nc = self.nc
        with self.es:
            self.setup_consts()
            self.stage0()
            if self.dbg.get("stop") in ("w0", "ada"):
                k.barrier()
                return nc
            for b in range(self.nb):
                for l in range(self.nl):
                    self.layer(b, l)
            k.barrier()
        return nc

    def setup_consts(self):
        k = self.k
        self.ident = k.sb("ident", [128, 128], F32)
        self.ones = k.sb("ones", [128, 128], F32)
        self.identb = k.sb("identb", [128, 128], BF16)
        self.onesb = k.sb("onesb", [128, 128], BF16)
        for tt, nm in ((self.ident, "k_ident"), (self.ones, "k_ones"), (self.identb, "k_identb"), (self.onesb, "k_onesb")):
            k.dma(tt[:], self.din[nm], key=tt, w=tt)
        self.ps = [TT(self.es.enter_context(self.nc.psum_tensor(f"ps{i}", [128, 512], F32)), 1, f"ps{i}") for i in range(8)]
        self.psi = 0
        for p in self.ps:
            p.b[0].psum = True

    def psum(self):
        p = self.ps[self.psi % 8]
        self.psi += 1
        return p

    def stage0(self):
        k = self.k
        with k.scope():
            stg = k.rot("cv_in", [128, 2048], F32, 3)
            stb = k.rot("cv_out", [128, 2048], BF16, 3)
            engs = ["dve", "pool", "act"]
            i = 0
            srcs = [("in", "w_in", D, INW), ("uq", "mla_w_uq", 384, 768), ("ukv", "mla_w_ukv", 256, 1024), ("br", "w_branch", 2048, D),
                    ("out", "w_out", D, D), ("up", "ffn_w_up", D, 2 * DFF), ("dn", "ffn_w_down", DFF, D)]
            for l in range(self.nl):
                for key, nm, R, C in srcs:
                    for r0 in range(0, R, 128):
                        for c0 in range(0, C, 2048):
                            cw = min(2048, C - c0)
                            a = stg.next(); bb = stb.next()
                            k.dma(a[:, :cw], self.din[nm][l, r0:r0 + 128, c0:c0 + cw], key=a, w=a)
                            k.cp(engs[i % 3], bb[:, :cw], a[:, :cw], r=a, w=bb)
                            k.dma(self.WB[key][l, r0:r0 + 128, c0:c0 + cw], bb[:, :cw], key=bb, r=bb)
                            i += 1
        if self.dbg.get("stop") == "w0":
            return
        with k.scope():
            c5 = k.sb("c5", [128, 40], F32)
            sc = k.sb("sc", [128, 40], F32)
            k.dma(c5[:], self.din["c5T"], key=c5, w=c5)
            k.act(sc[:], c5[:], AF.Silu, r=c5, w=sc)
            aw = k.rot("aw", [128, 8, 512], F32, 2)
            ab = k.rot("ab", [1, 512], F32, 2)
            mo = k.rot("mo", [5, 512], F32, 2)
            for l in range(self.nl):
                for cb in range(12):
                    w_ = aw.next(); b_ = ab.next(); m_ = mo.next()
                    k.dma(w_[:], self.din["ada_w"][l, :, cb * 512:(cb + 1) * 512].rearrange("(kc p) n -> p kc n", p=128), key=w_, w=w_)
                    k.dma(b_[:], self.din["ada_b"][l:l + 1, cb * 512:(cb + 1) * 512], key=b_, w=b_)
                    p = self.psum()
                    for kc in range(8):
                        k.mm(p[0:5, :], sc[:, kc * 5:(kc + 1) * 5], w_[:, kc, :], kc == 0, False, r=[sc, w_], w=p)
                    k.mm(p[0:5, :], self.ones[0:1, 0:5], b_[:], False, True, r=[self.ones, b_], w=p)
                    add1 = 1.0 if cb // 2 in (1, 4) else 0.0
                    k.act(m_[:], p[0:5, :], AF.Identity, r=p, w=m_, bias=add1)
                    k.dma(self.MOD[l, :, cb * 512:(cb + 1) * 512], m_[:], key=m_, r=m_)

    def hsrc(self, b, l, n, which):
        if which == 0 and l == 0:
            if n < 2:
                return self.din["ctx"][b, n * 128:(n + 1) * 128, :]
            return self.din["x"][b, (n - 2) * 128:(n - 1) * 128, :]
        return self.HS[which][n * 128:(n + 1) * 128, :]

    def load_modF(self, b, l, i0, dst):
        k = self.k
        m8 = k.sb("m8", [32, 128], F32)
        for j, row in ((0, b), (1, 4)):
            for q in range(2):
                src = self.MOD[l, row, (i0 + q) * D:(i0 + q + 1) * D].rearrange("(c p) -> c p", p=128)
                o = (j * 2 + q) * 8
                k.dma(m8[o:o + 8, :], src, key=m8, w=m8)
        p = self.psum()
        k.tr(p[:, 0:32], m8[0:32, :], self.ident[0:32, 0:32], r=[m8, self.ident], w=p)
        k.cp("dve", dst[:, 0:32], p[:, 0:32], r=p, w=dst)

    def ln_mod_phase(self, b, l, which, HM, tiles):
        k = self.k
        with k.scope():
            mf = k.sb("modF", [128, 32], F32)
            i0 = 0 if which == 0 else 3
            self.load_modF(b, l, i0, mf)
            LA = 4
            hts = k.rot("ht", [128, D], F32, LA + 3)
            xns = k.rot("xn", [128, D], F32, 2)
            sts = k.rot("st", [128, 16], F32, 4)
            tl = list(tiles)
            loads = {}

            def Ld(i):
                if i < len(tl):
                    ht = hts.next()
                    k.dma(ht[:], self.hsrc(b, l, tl[i], which), key=ht, w=ht)
                    loads[tl[i]] = ht

            for i in range(LA):
                Ld(i)

            def mkA(n):
                def A():
                    Ld(tl.index(n) + LA)
                    ht = loads.pop(n); st = sts.next()
                    self.ln_stats(ht, st)
                    return ht, st
                return A

            def mkB(n):
                def B(state):
                    ht, st = state
                    xn = xns.next()
                    self.ln_apply(ht, xn, st)
                    j = 1 if n < 2 else 0
                    for half in range(2):
                        p = self.psum()
                        for q in range(4):
                            c = half * 4 + q
                            k.tr(p[:, q * 128:(q + 1) * 128], xn[:, c * 128:(c + 1) * 128], self.ident[:], r=[xn, self.ident], w=p)
                        for q in range(4):
                            c = half * 4 + q
                            if half == 0:
                                k.ts("dve", HM[:, c, n * 128:(n + 1) * 128], p[:, q * 128:(q + 1) * 128], mf[:, j * 16 + 8 + c:j * 16 + 9 + c],
                                     mf[:, j * 16 + c:j * 16 + c + 1], ALU.mult, ALU.add, r=[p, mf], w=[], ww=HM.b[n])
                            else:
                                k.act(HM[:, c, n * 128:(n + 1) * 128], p[:, q * 128:(q + 1) * 128], AF.Identity, r=[p, mf], w=[], ww=HM.b2[n],
                                      scale=mf[:, j * 16 + 8 + c:j * 16 + 9 + c], bias=mf[:, j * 16 + c:j * 16 + c + 1])
                return B

            run_pipe([(mkA(n), mkB(n)) for n in tiles], 2)

    def ln_stats(self, ht, st, eps=1e-6):
        k = self.k
        k.op("dve", lambda e: e.bn_stats(out=st[:, 0:6], in_=ht[:, 0:512]), r=ht, w=st)
        k.op("dve", lambda e: e.bn_stats(out=st[:, 6:12], in_=ht[:, 512:1024]), r=ht, w=st)
        k.op("dve", lambda e: e.bn_aggr(out=st[:, 12:14], in_=st[:, 0:12]), r=st, w=st)
        k.act(st[:, 14:15], st[:, 13:14], AF.Sqrt, r=st, w=st, bias=eps)

    def ln_apply(self, ht, xn, st):
        k = self.k
        k.op("dve", lambda e: e.reciprocal(out=st[:, 14:15], in_=st[:, 14:15]), r=st, w=st)
        k.stt(st[:, 15:16], st[:, 12:13], -1.0, st[:, 14:15], ALU.mult, ALU.mult, r=st, w=st)
        k.act(xn[:], ht[:], AF.Identity, r=[ht, st], w=xn, scale=st[:, 14:15], bias=st[:, 15:16])

    def wload(self, pool, key, l, kc_n, segs, rows0=0):
        k = self.k
        w = pool.next()
        src = self.WB[key][l]
        pairs = []
        o = 0
        for c0, n in segs:
            pairs.append((w[:, 0:kc_n, o:o + n], src[rows0:rows0 + kc_n * 128, c0:c0 + n].rearrange("(kc p) n -> p kc n", p=128)))
            o += n
        k.dma_group(pairs, key=w, w=w)
        return w

    def proj(self, p, N, w, wc0, src, t0, kc_n, extra_r=()):
        k = self.k
        rb = [src.rb(n) for n in tiles_of(t0, N)]
        for kc in range(kc_n):
            k.mm(p[:, :N], w[:, kc, wc0:wc0 + 128], src[:, kc, t0:t0 + N], kc == 0, kc == kc_n - 1, r=[w, rb, extra_r], w=p)

    def fm_params(self, dst, rows):
        k = self.k
        tot = sum(r.shape[0] for r in rows)
        st = k.sb("fmst", [tot, 128], F32)
        pairs = []
        o = 0
        for r in rows:
            pairs.append((st[o:o + r.shape[0], :], r))
            o += r.shape[0]
        k.dma_group(pairs, key=st, w=st)
        p = self.psum()
        k.tr(p[:, 0:tot], st[0:tot, :], self.ident[0:tot, 0:tot], r=[st, self.ident], w=p)
        k.cp("dve", dst[:, 0:tot], p[:, 0:tot], r=p, w=dst)

    def bcast_load(self, dst, row_ap, n):
        self.k.dma(dst[:, 0:n], row_ap.to_broadcast([128, n]), key=dst, w=dst)

    def log1p_small(self, tt, out, y, t):
        k = self.k
        k.ts("dve", t, y, -0.2, 0.25, ALU.mult, ALU.add, r=tt, w=tt)
        for cst in (1.0 / 3.0, 0.5, 1.0):
            k.tt("dve", t, t, y, ALU.mult, r=tt, w=tt)
            k.ts("dve", t, t, -1.0, cst, ALU.mult, ALU.add, r=tt, w=tt)
        k.tt("dve", out, t, y, ALU.mult, r=tt, w=tt)

    def retention(self, b, l, HM, BR, do_ctx):
        k = self.k
        lnscale = float(np.log(128.0 ** -0.5))
        with k.scope():
            QK = k.sb("QK", [128, 8, T], BF16, nb=NT)
            Vt = k.sb("Vt", [128, NT, 512], BF16, nb=NT)
            with k.scope():
                rc = k.sb("rc", [128, SEQ], F32); rs = k.sb("rs", [128, SEQ], F32)
                k.dma(rc[:], self.din["k_retC"], key=rc, w=rc)
                k.dma(rs[:], self.din["k_retS"], key=rs, w=rs)
                wp = k.rot("wqk", [128, 8, 256], BF16, 2)
                t1s = k.rot("rt1", [128, 512], F32, 2); t2s = k.rot("rt2", [128, 512], F32, 2)
                for qi in range(8):
                    base = (O_RQ if qi < 4 else O_RK) + (qi % 4) * 128
                    w = self.wload(wp, "in", l, 8, [(base, 128), (base + 64, 64), (base, 64)])
                    for t0, N in BLOCKS:
                        tl = [QK.b[n] for n in tiles_of(t0, N)]
                        pa = self.psum()
                        self.proj(pa, N, w, 0, HM, t0, 8)
                        if t0 < CTX:
                            k.cp("act", QK[:, qi, t0:t0 + N], pa[:, :N], r=pa, w=tl)
                        else:
                            pb = self.psum()
                            self.proj(pb, N, w, 128, HM, t0, 8)
                            s0 = t0 - CTX
                            t1 = t1s.next(); t2 = t2s.next()
                            k.tt("dve", t1[:, :N], pa[:, :N], rc[:, s0:s0 + N], ALU.mult, r=[pa, rc], w=t1)
                            k.tt("dve", t2[:, :N], pb[:, :N], rs[:, s0:s0 + N], ALU.mult, r=[pb, rs], w=t2)
                            k.tt("pool", QK[:, qi, t0:t0 + N], t1[:, :N], t2[:, :N], ALU.add, r=[t1, t2], w=tl)
                wv = self.wload(k.rot("wv", [128, 8, 512], BF16, 1), "in", l, 8, [(O_RV, 512)])
                for n in range(NT):
                    p = self.psum()
                    for kc in range(8):
                        k.mm(p[:, :], HM[:, kc, n * 128:(n + 1) * 128], wv[:, kc, :], kc == 0, kc == 7, r=[wv, HM.rb(n)], w=p)
                    k.cp("act", Vt[:, n, :], p[:, :], r=p, w=Vt.b[n])
            sm = k.sb("rsm", [128, 64], F32)
            self.bcast_load(sm, self.din["ret_decay"][l:l + 1, :], 8)
            k.act(sm[:, 8:16], sm[:, 0:8], AF.Exp, r=sm, w=sm, scale=-1.0)
            self.log1p_small(sm, sm[:, 16:24], sm[:, 8:16], sm[:, 24:32])
            k.ts("dve", sm[:, 32:40], sm[:, 16:24], -1.0, None, ALU.mult, None, r=sm, w=sm)
            gp = k.sb("rgp", [128, 8], F32)
            self.fm_params(gp, [self.din["ret_gn_w"][l].rearrange("(c p) -> c p", p=128), self.din["ret_gn_b"][l].rearrange("(c p) -> c p", p=128)])
            posf = k.sb("posf", [128, T], F32); posb = k.sb("posb", [128, T], F32)
            ptm = k.sb("ptm", [128, 2, NT], F32)
            k.dma(posf[:], self.din["k_posf"], key=posf, w=posf)
            k.dma(posb[:], self.din["k_posb"], key=posb, w=posb)
            k.dma(ptm[:, 0, :], self.din["k_posf_tm"], key=ptm, w=ptm)
            k.dma(ptm[:, 1, :], self.din["k_posb_tm"], key=ptm, w=ptm)
            mf = k.sb("mfm", [128, 128], F32); mb = k.sb("mbm", [128, 128], F32)
            k.dma(mf[:], self.din["k_mf"], key=mf, w=mf)
            k.dma(mb[:], self.din["k_mb"], key=mb, w=mb)
            Qf = k.sb("Qf", [128, T], F32); Qb = k.sb("Qb", [128, T], F32)
            bias = k.sb("rbias", [128, 2, NT], F32)
            Ed = k.sb("Ed", [128, 128], F32)
            Es = k.rot("E", [128, 512], F32, 3); E2s = k.rot("E2", [128, 512], F32, 2)
            Ws = k.rot("W", [128, 512], BF16, 3)
            wg = k.rot("wg", [128, 8, 128], BF16, 2)
            fin = [k.rot(f"rf{i}", [128, 512], F32, 2) for i in range(3)]
            ysbs = k.rot("ysb", [128, 512], BF16, 2); ysqbs = k.rot("ysqb", [128, 512], BF16, 2)
            SG = k.sb("SG", [128, T], F32)
            accb = Rot(self.ps[0:2]); rotb = Rot(self.ps[2:8])
            for h in range(4):
                lgf = sm[:, 32 + h:33 + h]; lgb = sm[:, 36 + h:37 + h]
                k.act(Qf[:], posf[:], AF.Copy, r=[posf, sm], w=Qf, scale=lgf)
                k.act(Qb[:], posb[:], AF.Copy, r=[posb, sm], w=Qb, scale=lgb)
                k.ts("dve", bias[:, 0, :], ptm[:, 0, :], lgf, -1.0, ALU.mult, ALU.mult, r=[ptm, sm], w=bias)
                k.ts("dve", bias[:, 1, :], ptm[:, 1, :], lgb, -1.0, ALU.mult, ALU.mult, r=[ptm, sm], w=bias)
                k.ts("dve", bias[:, :, :], bias[:, :, :], lnscale, None, ALU.add, None, r=bias, w=bias)
                e1 = Es.next(); e2 = E2s.next()
                k.tt("dve", e1[:, 0:128], Qf[:, 256:384], mf[:], ALU.add, r=[Qf, mf], w=e1)
                k.act(e1[:, 0:128], e1[:, 0:128], AF.Exp, r=[e1, bias], w=e1, bias=bias[:, 0, 2:3])
                k.tt("dve", e2[:, 0:128], Qb[:, 256:384], mb[:], ALU.add, r=[Qb, mb], w=e2)
                k.act(e2[:, 0:128], e2[:, 0:128], AF.Exp, r=[e2, bias], w=e2, bias=bias[:, 1, 2:3])
                k.tt("pool", Ed[:], e1[:, 0:128], e2[:, 0:128], ALU.add, r=[e1, e2], w=Ed)
                wgt = self.wload(wg, "in", l, 8, [(O_RG + h * 128, 128)])
                for t0, N in BLOCKS:
                    if t0 < CTX and not do_ctx:
                        continue
                    G = rotb.next()
                    self.proj(G, N, wgt, 0, HM, t0, 8)
                    k.act(SG[:, t0:t0 + N], G[:, :N], AF.Silu, r=G, w=SG)
                k.preload_lnexp()
                pending = None
                for t0, N in BLOCKS:
                    if t0 < CTX and not do_ctx:
                        continue
                    I0 = t0 // 128
                    nI = N // 128
                    keys = [0, 1] if t0 < CTX else list(range(NT))
                    Y = accb.next()
                    def mkA(J):
                        def A():
                            S = rotb.next()
                            k.mm(S[:, :N], QK[:, 4 + h, J * 128:(J + 1) * 128], QK[:, h, t0:t0 + N], True, True,
                                 r=[QK.b[J]] + [QK.b[n] for n in tiles_of(t0, N)], w=S)
                            return S
                        return A

                    def mkB(ji, J):
                        def B(S):
                            E = Es.next()
                            if J < 2 and t0 >= CTX:
                                E2 = E2s.next()
                                k.act(E[:, :N], Qf[:, t0:t0 + N], AF.Exp, r=[Qf, bias], w=E, bias=bias[:, 0, J:J + 1])
                                k.act(E2[:, :N], Qb[:, t0:t0 + N], AF.Exp, r=[Qb, bias], w=E2, bias=bias[:, 1, J:J + 1])
                                k.tt("pool", E[:, :N], E[:, :N], E2[:, :N], ALU.add, r=[E, E2], w=E)
                            else:
                                d = J - I0
                                lo = min(max(d * 128, 0), N); hi = min(max((d + 1) * 128, 0), N)
                                if lo > 0:
                                    k.act(E[:, 0:lo], Qb[:, t0:t0 + lo], AF.Exp, r=[Qb, bias], w=E, bias=bias[:, 1, J:J + 1])
                                if hi < N:
                                    k.act(E[:, hi:N], Qf[:, t0 + hi:t0 + N], AF.Exp, r=[Qf, bias], w=E, bias=bias[:, 0, J:J + 1])
                                if lo < hi:
                                    k.cp("pool", E[:, lo:hi], Ed[:], r=Ed, w=E)
                            W = Ws.next()
                            k.tt("dve", W[:, :N], S[:, :N], E[:, :N], ALU.mult, r=[S, E], w=W)
                            k.mm(Y[:, :N], Vt[:, J, h * 128:(h + 1) * 128], W[:, :N], ji == 0, ji == len(keys) - 1, r=[Vt.b[J], W], w=Y)
                        return B

                    steps = [(mkA(J), mkB(ji, J)) for ji, J in enumerate(keys)]
                    if pending is not None:
                        if len(steps) > 3:
                            a3, b3 = steps[2]
                            steps[2] = (a3, (lambda S_, b3=b3, fp=pending: (b3(S_), fp())))
                        else:
                            pending()
                        pending = None
                    run_pipe(steps, 2)

                    def mkfin(Y=Y, N=N, t0=t0, h=h):
                        def finish():
                            mean, t3, t4 = [f.next() for f in fin]
                            ysb = ysbs.next(); ysqb = ysqbs.next()
                            k.cp("act", ysb[:, :N], Y[:, :N], r=Y, w=ysb)
                            k.act(ysqb[:, :N], Y[:, :N], AF.Square, r=Y, w=ysqb)
                            P1 = rotb.next(); P2 = rotb.next()
                            k.mm(P1[:, :N], self.onesb[:], ysb[:, :N], True, True, r=[self.onesb, ysb], w=P1)
                            k.mm(P2[:, :N], self.onesb[:], ysqb[:, :N], True, True, r=[self.onesb, ysqb], w=P2)
                            k.act(mean[:, :N], P1[:, :N], AF.Copy, r=P1, w=mean, scale=1.0 / 128)
                            k.act(t3[:, :N], P1[:, :N], AF.Square, r=P1, w=t3, scale=1.0 / 128)
                            k.stt(t3[:, :N], P2[:, :N], 1.0 / 128, t3[:, :N], ALU.mult, ALU.subtract, r=[P2, t3], w=t3)
                            k.act(t3[:, :N], t3[:, :N], AF.Ln, r=t3, w=t3, bias=1e-6)
                            k.act(t3[:, :N], t3[:, :N], AF.Exp, r=t3, w=t3, scale=-0.5)
                            k.tt("dve", t4[:, :N], Y[:, :N], mean[:, :N], ALU.subtract, r=[Y, mean], w=t4)
                            k.tt("dve", t4[:, :N], t4[:, :N], t3[:, :N], ALU.mult, r=[t4, t3], w=t4)
                            k.ts("dve", t4[:, :N], t4[:, :N], gp[:, h:h + 1], gp[:, 4 + h:5 + h], ALU.mult, ALU.add, r=[t4, gp], w=t4)
                            k.tt("pool", BR[:, h, t0:t0 + N], SG[:, t0:t0 + N], t4[:, :N], ALU.mult, r=[SG, t4], w=[BR.b[n] for n in tiles_of(t0, N)])
                        return finish

                    pending = mkfin()
                if pending is not None:
                    pending()
                    pending = None

    def rms_block(self, p_list, N, nch, dim, normw, out_bf, cbuf, sqbuf, rbuf, rotb, nw_tt):
        k = self.k
        for c in range(nch):
            k.cp("act", cbuf[:, c, :N], p_list[c][:, :N], r=p_list[c], w=cbuf)
            k.tt("pool", sqbuf[:, c, :N], cbuf[:, c, :N], cbuf[:, c, :N], ALU.mult, r=cbuf, w=sqbuf)
        ss = rotb.next()
        for c in range(nch):
            k.mm(ss[:, :N], self.ones[:], sqbuf[:, c, :N], c == 0, c == nch - 1, r=[self.ones, sqbuf], w=ss)
        k.act(rbuf[:, :N], ss[:, :N], AF.Ln, r=ss, w=rbuf, scale=1.0 / dim, bias=1e-6)
        k.act(rbuf[:, :N], rbuf[:, :N], AF.Exp, r=rbuf, w=rbuf, scale=-0.5)
        for c in range(nch):
            k.tt("pool", cbuf[:, c, :N], cbuf[:, c, :N], rbuf[:, :N], ALU.mult, r=[cbuf, rbuf], w=cbuf)
            k.ts("dve", out_bf[:, c, :N], cbuf[:, c, :N], normw[:, c:c + 1], None, ALU.mult, None, r=[cbuf, nw_tt], w=out_bf)

    def rope_combine(self, dst_ap, dst_bufs, pa, pb, N, cT, sT, s0, t1, t2):
        k = self.k
        k.tt("dve", t1[:, :N], pa[:, :N], cT[:, s0:s0 + N], ALU.mult, r=[pa, cT], w=t1)
        k.tt("dve", t2[:, :N], pb[:, :N], sT[:, s0:s0 + N], ALU.mult, r=[pb, sT], w=t2)
        k.tt("pool", dst_ap, t1[:, :N], t2[:, :N], ALU.add, r=[t1, t2], w=dst_bufs)

    def mla(self, b, l, HM, BR, do_ctx):
        k = self.k
        scale = 192.0 ** -0.5
        with k.scope():
            KN = k.sb("KN", [128, 4, T], BF16, nb=NT)
            KR = k.sb("KR", [128, T], BF16, nb=NT)
            Vt = k.sb("Vt", [128, NT, 512], BF16, nb=NT)
            mc = k.sb("mc", [128, SEQ], F32); ms = k.sb("ms", [128, SEQ], F32)
            k.dma(mc[:], self.din["k_mlaC"], key=mc, w=mc)
            k.dma(ms[:], self.din["k_mlaS"], key=ms, w=ms)
            nw = k.sb("mnw", [128, 8], F32)
            self.fm_params(nw, [self.din["mla_q_norm"][l].rearrange("(c p) -> c p", p=128), self.din["mla_kv_norm"][l].rearrange("(c p) -> c p", p=128)])
            cb = k.sb("mcb", [128, 3, 512], F32); sq = k.sb("msq", [128, 3, 512], F32); rb = k.sb("mrb", [128, 512], F32)
            t1s = k.rot("mt1", [128, 512], F32, 2); t2s = k.rot("mt2", [128, 512], F32, 2)
            rotb = Rot(self.ps[4:8])
            k.preload_lnexp()
            with k.scope():
                one = lambda nm, shp: k.rot(nm, shp, BF16, 1)
                wckv = self.wload(one("wckv", [128, 8, 256]), "in", l, 8, [(O_CKV, 256)])
                wkr = self.wload(one("wkr", [128, 8, 256]), "in", l, 8, [(O_KR, 64), (O_KR, 64), (O_KR + 32, 32), (O_KR, 32), (O_KR + 32, 32), (O_KR, 32)])
                wk = self.wload(one("wukvk", [128, 2, 512]), "ukv", l, 2, [(256 * h, 128) for h in range(4)])
                wv = self.wload(one("wukvv", [128, 2, 512]), "ukv", l, 2, [(256 * h + 128, 128) for h in range(4)])
                ckvn = k.rot("ckvn", [128, 2, 512], BF16, 2)
                for t0, N in BLOCKS:
                    tl = tiles_of(t0, N)
                    ps_ = [rotb.next() for _ in range(2)]
                    for c in range(2):
                        self.proj(ps_[c], N, wckv, c * 128, HM, t0, 8)
                    cn = ckvn.next()
                    self.rms_block(ps_, N, 2, 256.0, nw[:, 3:5], cn, cb, sq, rb, rotb, nw)
                    for h in range(4):
                        p = rotb.next()
                        for kc in range(2):
                            k.mm(p[:, :N], wk[:, kc, h * 128:(h + 1) * 128], cn[:, kc, :N], kc == 0, kc == 1, r=[wk, cn], w=p)
                        k.cp("act", KN[:, h, t0:t0 + N], p[:, :N], r=p, w=[KN.b[n] for n in tl])
                    for n in tl:
                        p = rotb.next()
                        o = n * 128 - t0
                        for kc in range(2):
                            k.mm(p[:, :], cn[:, kc, o:o + 128], wv[:, kc, :], kc == 0, kc == 1, r=[wv, cn], w=p)
                        k.cp("act", Vt[:, n, :], p[:, :], r=p, w=Vt.b[n])
                    pa = rotb.next()
                    self.proj(pa, N, wkr, 0, HM, t0, 8)
                    if t0 < CTX:
                        k.cp("act", KR[:, t0:t0 + N], pa[:, :N], r=pa, w=[KR.b[n] for n in tl])
                    else:
                        pb = rotb.next()
                        self.proj(pb, N, wkr, 128, HM, t0, 8)
                        self.rope_combine(KR[:, t0:t0 + N], [KR.b[n] for n in tl], pa, pb, N, mc, ms, t0 - CTX, t1s.next(), t2s.next())
            one = lambda nm, shp: k.rot(nm, shp, BF16, 1)
            wcq = self.wload(one("wcq", [128, 8, 384]), "in", l, 8, [(O_CQ, 384)])
            wqn = self.wload(one("wuqn", [128, 3, 512]), "uq", l, 3, [(192 * h, 128) for h in range(4)])
            segs = [(192 * h + 128, 64) for h in range(4)]
            for h in range(4):
                segs += [(192 * h + 128 + 32, 32), (192 * h + 128, 32)]
            wqr = self.wload(one("wuqr", [128, 3, 512]), "uq", l, 3, segs)
            cqn = k.sb("cqn", [128, 3, 512], BF16)
            QN = k.sb("QN", [128, 4, 512], BF16); QR = k.sb("QR", [128, 2, 512], BF16)
            Ps = k.rot("P", [128, 512], BF16, 5)
            Pb = k.sb("Pb", [128, 512], BF16)
            rinv = k.rot("rinv", [128, 512], F32, 2)
            paccs = k.rot("pacc", [128, 512], F32, 2)
            accb = Rot(self.ps[0:4])
            for t0, N in BLOCKS:
                if t0 < CTX and not do_ctx:
                    continue
                ps_ = [rotb.next() for _ in range(3)]
                for c in range(3):
                    self.proj(ps_[c], N, wcq, c * 128, HM, t0, 8)
                self.rms_block(ps_, N, 3, 384.0, nw[:, 0:3], cqn, cb, sq, rb, rotb, nw)
                for h in range(4):
                    p = rotb.next()
                    for kc in range(3):
                        k.mm(p[:, :N], wqn[:, kc, h * 128:(h + 1) * 128], cqn[:, kc, :N], kc == 0, kc == 2, r=[wqn, cqn], w=p)
                    k.cp("act", QN[:, h, :N], p[:, :N], r=p, w=QN)
                for a in range(2):
                    pa = rotb.next()
                    for kc in range(3):
                        k.mm(pa[:, :N], wqr[:, kc, a * 128:(a + 1) * 128], cqn[:, kc, :N], kc == 0, kc == 2, r=[wqr, cqn], w=pa)
                    if t0 < CTX:
                        k.cp("act", QR[:, a, :N], pa[:, :N], r=pa, w=QR)
                    else:
                        pb = rotb.next()
                        for kc in range(3):
                            k.mm(pb[:, :N], wqr[:, kc, 256 + a * 128:256 + (a + 1) * 128], cqn[:, kc, :N], kc == 0, kc == 2, r=[wqr, cqn], w=pb)
                        self.rope_combine(QR[:, a, :N], QR, pa, pb, N, mc, ms, t0 - CTX, t1s.next(), t2s.next())
                keys = [0, 1] if t0 < CTX else list(range(NT))
                pending = None
                for h in range(4):
                    O = accb.next(); R = accb.next()
                    hp = 64 * (h % 2)
                    def mkA(J, h=h, hp=hp):
                        def A():
                            S = rotb.next()
                            k.mm(S[:, :N], KN[:, h, J * 128:(J + 1) * 128], QN[:, h, :N], True, False, r=[KN.b[J], QN], w=S)
                            k.mm(S[:, :N], KR[hp:hp + 64, J * 128:(J + 1) * 128], QR[hp:hp + 64, h // 2, :N], False, True, r=[KR.b[J], QR], w=S)
                            return S
                        return A

                    Pacc = paccs.next()

                    def mkB(ji, J, h=h, O=O, R=R, Pacc=Pacc):
                        def B(S):
                            P = Ps.next()
                            k.act(P[:, :N], S[:, :N], AF.Exp, r=S, w=P, scale=scale)
                            k.mm(O[:, :N], Vt[:, J, h * 128:(h + 1) * 128], P[:, :N], ji == 0, ji == len(keys) - 1, r=[Vt.b[J], P], w=O)
                            if ji == 0:
                                k.cp("dve", Pacc[:, :N], P[:, :N], r=P, w=Pacc)
                            else:
                                k.tt("dve", Pacc[:, :N], Pacc[:, :N], P[:, :N], ALU.add, r=[Pacc, P], w=Pacc)
                        return B

                    steps = [(mkA(J), mkB(ji, J)) for ji, J in enumerate(keys)]
                    if pending is not None:
                        if len(steps) > 3:
                            a3, b3 = steps[2]
                            steps[2] = (a3, (lambda S_, b3=b3, fp=pending: (b3(S_), fp())))
                        else:
                            pending()
                        pending = None
                    run_pipe(steps, 2)

                    def mkfin(O=O, R=R, N=N, t0=t0, h=h, Pacc=Pacc):
                        def finish():
                            ri = rinv.next()
                            k.cp("act", Pb[:, :N], Pacc[:, :N], r=Pacc, w=Pb)
                            k.mm(R[:, :N], self.onesb[:], Pb[:, :N], True, True, r=[self.onesb, Pb], w=R)
                            k.act(ri[:, :N], R[:, :N], AF.Ln, r=R, w=ri)
                            k.act(ri[:, :N], ri[:, :N], AF.Exp, r=ri, w=ri, scale=-1.0)
                            k.tt("dve", BR[:, h, t0:t0 + N], O[:, :N], ri[:, :N], ALU.mult, r=[O, ri], w=[BR.b[n] for n in tiles_of(t0, N)])
                        return finish

                    pending = mkfin()
                if pending is not None:
                    pending()
                    pending = None

    def conv4(self, u, U0, pr, wcol, bcol, eng="dve"):
        k = self.k
        k.ts(eng, u[:, :], U0[:, :], wcol(1), bcol, ALU.mult, ALU.add, r=[U0, pr], w=u)
        for s_, e_ in ((0, CTX), (CTX, T)):
            k.stt(u[:, s_ + 1:e_], U0[:, s_:e_ - 1], wcol(0), u[:, s_ + 1:e_], ALU.mult, ALU.add, r=[U0, u, pr], w=u)
            k.stt(u[:, s_:e_ - 1], U0[:, s_ + 1:e_], wcol(2), u[:, s_:e_ - 1], ALU.mult, ALU.add, r=[U0, u, pr], w=u)
            k.stt(u[:, s_:e_ - 2], U0[:, s_ + 2:e_], wcol(3), u[:, s_:e_ - 2], ALU.mult, ALU.add, r=[U0, u, pr], w=u)

    def lru(self, b, l, HM, BR, do_ctx):
        k = self.k
        with k.scope():
            pr = k.sb("lpr", [128, 48], F32)
            self.fm_params(pr, [self.din["lru_conv_w"][l].rearrange("k (c p) -> (k c) p", p=128),
                                self.din["lru_conv_b"][l].rearrange("(c p) -> c p", p=128),
                                self.din["lru_gate_b"][l].rearrange("d g (c p) -> (d g c) p", p=128),
                                self.din["lru_lambda"][l].rearrange("d (c p) -> (d c) p", p=128)])
            sm = k.sb("lsm", [128, 32], F32)
            k.act(sm[:, 0:8], pr[:, 36:44], AF.Exp, r=pr, w=sm, scale=-1.0)
            self.log1p_small(sm, sm[:, 8:16], sm[:, 0:8], sm[:, 16:24])
            k.ts("dve", sm[:, 24:32], sm[:, 8:16], -8.0, None, ALU.mult, None, r=sm, w=sm)
            GW = k.sb("GW", [128, 16, 128], BF16)
            with k.scope():
                stg = k.sb("gwst", [128, 16, 128], F32)
                k.op("pool", lambda e: e.memset(stg[:], 0.0), w=stg)
                pairs = []
                for d in range(2):
                    for g in range(2):
                        for cc in range(4):
                            for j in range(2):
                                pairs.append((stg[64 * j:64 * j + 64, (d * 2 + g) * 4 + cc, 64 * j:64 * j + 64], self.din["lru_gate_w"][l, d, g, 2 * cc + j]))
                k.dma_group(pairs, key=stg, w=stg)
                k.cp("dve", GW[:], stg[:], r=stg, w=GW)
            wp = k.rot("lw", [128, 8, 256], BF16, 2)
            U0 = k.sb("lU0", [128, T], F32); u = k.sb("lu", [128, T], F32)
            aa = [k.sb(f"la{d}", [128, T], F32) for d in range(2)]
            iis = [k.sb(f"li{d}", [128, T], F32) for d in range(2)]
            hh = [k.sb(f"lh{d}", [128, T], F32) for d in range(2)]
            ub = k.sb("lub", [128, T], BF16)
            for cc in range(4):
                w = self.wload(wp, "in", l, 8, [(O_LX + cc * 128, 128), (O_LG + cc * 128, 128)])
                for t0, N in BLOCKS:
                    p = self.psum()
                    self.proj(p, N, w, 0, HM, t0, 8)
                    k.cp("act", U0[:, t0:t0 + N], p[:, :N], r=p, w=U0)
                self.conv4(u, U0, pr, lambda kk: pr[:, kk * 4 + cc:kk * 4 + cc + 1], pr[:, 16 + cc:17 + cc])
                k.cp("act", ub[:], u[:], r=u, w=ub)
                for d in range(2):
                    a = aa[d]; ii = iis[d]; h_ = hh[d]
                    for g, dst in ((0, a), (1, ii)):
                        gi = (d * 2 + g) * 4 + cc
                        for t0, N in BLOCKS:
                            p = self.psum()
                            k.mm(p[:, :N], GW[:, gi, :], ub[:, t0:t0 + N], True, True, r=[GW, ub], w=p)
                            k.act(dst[:, t0:t0 + N], p[:, :N], AF.Sigmoid, r=[p, pr], w=dst, bias=pr[:, 20 + gi:21 + gi])
                    k.act(a[:], a[:], AF.Exp, r=[a, sm], w=a, scale=sm[:, 24 + d * 4 + cc:25 + d * 4 + cc])
                    k.act(h_[:], a[:], AF.Square, r=a, w=h_)
                    k.act(h_[:], h_[:], AF.Sqrt, r=h_, w=h_, scale=-1.0, bias=1.0)
                    k.tt("dve", ii[:], ii[:], h_[:], ALU.mult, r=[h_, ii], w=ii)
                    k.tt("dve", ii[:], ii[:], u[:], ALU.mult, r=[ii, u], w=ii)
                    if d == 0:
                        k.op("dve", lambda e: e.tensor_tensor_scan(out=h_[:], data0=a[:], data1=ii[:], initial=0.0, op0=ALU.mult, op1=ALU.add), r=[a, ii], w=h_)
                    else:
                        k.op("dve", lambda e: e.tensor_tensor_scan(out=h_[:, 0:CTX][:, ::-1], data0=a[:, 0:CTX][:, ::-1], data1=ii[:, 0:CTX][:, ::-1],
                                                                   initial=0.0, op0=ALU.mult, op1=ALU.add), r=[a, ii], w=h_)
                        k.op("dve", lambda e: e.tensor_tensor_scan(out=h_[:, CTX:T][:, ::-1], data0=a[:, CTX:T][:, ::-1], data1=ii[:, CTX:T][:, ::-1],
                                                                   initial=h_[:, 0:1], op0=ALU.mult, op1=ALU.add), r=[a, ii, h_], w=h_)
                t = iis[0]; hs = aa[0]
                for t0, N in BLOCKS:
                    p = self.psum()
                    self.proj(p, N, w, 128, HM, t0, 8)
                    k.cp("act", U0[:, t0:t0 + N], p[:, :N], r=p, w=U0)
                k.act(t[:], U0[:], AF.Square, r=U0, w=t)
                k.act(t[:], t[:], AF.Identity, r=t, w=t, scale=0.044715, bias=1.0)
                k.tt("dve", t[:], t[:], U0[:], ALU.mult, r=[t, U0], w=t)
                k.act(t[:], t[:], AF.Sigmoid, r=t, w=t, scale=1.5957691216)
                k.tt("dve", hs[:], hh[0][:], hh[1][:], ALU.add, r=[hh[0], hh[1]], w=hs)
                k.tt("dve", hs[:], hs[:], U0[:], ALU.mult, r=[hs, U0], w=hs)
                k.tt("dve", BR[:, cc, :], hs[:], t[:], ALU.mult, r=[hs, t], w=BR)

    def ssd(self, b, l, HM, BR, do_ctx):
        k = self.k
        tq0 = 0 if do_ctx else CTX
        with k.scope():
            BT = k.sb("BT", [128, 2, T], BF16, nb=NT); CT = k.sb("CT", [128, 2, T], BF16, nb=NT)
            Xt = k.sb("Xt", [128, NT, 512], BF16, nb=NT)
            DT = k.sb("DT", [128, NT, 16], F32); Vc = k.sb("Vc", [128, NT, 16], F32); bia = k.sb("bia", [128, NT, 16], F32)
            pr = k.sb("spr", [128, 48], F32)
            self.fm_params(pr, [self.din["ssd_conv_w"][l].rearrange("k (c p) -> (k c) p", p=128),
                                self.din["ssd_conv_b"][l].rearrange("(c p) -> c p", p=128),
                                self.din["ssd_norm_w"][l].rearrange("(c p) -> c p", p=128)])
            sm = k.sb("ssm", [128, 48], F32)
            self.bcast_load(sm, self.din["ssd_a_log"][l:l + 1, :], 16)
            k.act(sm[:, 0:16], sm[:, 0:16], AF.Exp, r=sm, w=sm)
            k.ts("dve", sm[:, 0:16], sm[:, 0:16], -1.0, None, ALU.mult, None, r=sm, w=sm)
            k.dma(sm[:, 16:32], self.din["ssd_dt_bias"][l:l + 1, :].to_broadcast([128, 16]), key=sm, w=sm)
            k.dma(sm[:, 32:40], self.din["ssd_d"][l:l + 1, :].to_broadcast([128, 8]), key=sm, w=sm)
            dI = k.sb("dI", [128, 8, 128], BF16)
            for h in range(8):
                k.ts("dve", dI[:, h, :], self.ident[:], sm[:, 32 + h:33 + h], None, ALU.mult, None, r=[self.ident, sm], w=dI)
            mf = k.sb("smf", [128, 128], F32); mb = k.sb("smb", [128, 128], F32)
            k.dma(mf[:], self.din["k_mf"], key=mf, w=mf)
            k.dma(mb[:], self.din["k_mb"], key=mb, w=mb)
            with k.scope():
                U0 = k.sb("sU0", [128, T], F32); u = k.sb("su", [128, T], F32); xs = k.sb("sxs", [128, T], BF16)
                wp = k.rot("sw", [128, 8, 128], BF16, 2)
                up = k.sb("sup", [128, 128], F32); lo = k.sb("slo", [128, 128], F32)
                k.dma(up[:], self.din["k_up"], key=up, w=up)
                k.dma(lo[:], self.din["k_lo"], key=lo, w=lo)
                for c in range(8):
                    w = self.wload(wp, "in", l, 8, [(O_SX + c * 128, 128)])
                    for t0, N in BLOCKS:
                        p = self.psum()
                        self.proj(p, N, w, 0, HM, t0, 8)
                        k.cp("act", U0[:, t0:t0 + N], p[:, :N], r=p, w=U0)
                    self.conv4(u, U0, pr, lambda kk: pr[:, kk * 8 + c:kk * 8 + c + 1], pr[:, 32 + c:33 + c])
                    if c < 4:
                        k.act(xs[:], u[:], AF.Silu, r=u, w=xs)
                        for n in range(NT):
                            pt = self.psum()
                            ptb = pt[:].bitcast(BF16)
                            k.tr(ptb[:, 0:128], xs[:, n * 128:(n + 1) * 128], self.identb[:], r=[xs, self.identb], w=pt)
                            k.cp("pool" if False else "dve", Xt[:, n, c * 128:(c + 1) * 128], ptb[:, 0:128], r=pt, w=Xt.b[n])
                    elif c < 6:
                        k.act(BT[:, c - 4, :], u[:], AF.Silu, r=u, w=BT)
                    else:
                        k.act(CT[:, c - 6, :], u[:], AF.Silu, r=u, w=CT)
                wdt = self.wload(k.rot("swdt", [128, 8, 16], BF16, 1), "in", l, 8, [(O_SDT, 16)])
                for n in range(NT):
                    p = self.psum()
                    for kc in range(8):
                        k.mm(p[:, 0:16], HM[:, kc, n * 128:(n + 1) * 128], wdt[:, kc, :], kc == 0, kc == 7, r=[wdt, HM.rb(n)], w=p)
                    k.tt("dve", DT[:, n, :], p[:, 0:16], sm[:, 16:32], ALU.add, r=[p, sm], w=DT)
                tA = k.sb("stA", [128, NT, 16], F32); tB = k.sb("stB", [128, NT, 16], F32)
                k.ts("dve", tA[:], DT[:], -1.0, None, ALU.mult, None, r=DT, w=tA)
                k.tt("dve", tA[:], tA[:], DT[:], ALU.min, r=[tA, DT], w=tA)
                k.act(tA[:], tA[:], AF.Exp, r=tA, w=tA)
                k.act(tA[:], tA[:], AF.Ln, r=tA, w=tA, bias=1.0)
                k.ts("dve", tB[:], DT[:], 0.0, None, ALU.max, None, r=DT, w=tB)
                k.tt("dve", DT[:], tA[:], tB[:], ALU.add, r=[tA, tB], w=DT)
                k.act(bia[:], DT[:], AF.Ln, r=DT, w=bia)
                for n in range(NT):
                    k.tt("dve", tA[:, n, :], DT[:, n, :], sm[:, 0:16], ALU.mult, r=[DT, sm], w=tA)
                for n in range(NT):
                    p = self.psum()
                    k.mm(p[:, 0:16], self.ones[:], tA[:, n, :], True, True, r=[self.ones, tA], w=p)
                    k.cp("act", tB[:, n, :], p[:, 0:16], r=p, w=tB)
                car = k.sb("scar", [128, NT, 16], F32)
                k.op("pool", lambda e: e.memset(car[:], 0.0), w=car)
                for n in range(1, NT):
                    k.tt("dve", car[:, n, 0:8], car[:, n - 1, 0:8], tB[:, n - 1, 0:8], ALU.add, r=[car, tB], w=car)
                k.cp("dve", car[:, 0, 8:16], tB[:, 1, 8:16], r=tB, w=car)
                k.tt("dve", car[:, NT - 1, 8:16], tB[:, 0, 8:16], tB[:, 1, 8:16], ALU.add, r=tB, w=car)
                for n in range(NT - 2, 1, -1):
                    k.tt("dve", car[:, n, 8:16], car[:, n + 1, 8:16], tB[:, n + 1, 8:16], ALU.add, r=[car, tB], w=car)
                for n in range(NT):
                    p = self.psum()
                    k.mm(p[:, 0:8], up[:], tA[:, n, 0:8], True, True, r=[up, tA], w=p)
                    k.mm(p[:, 8:16], lo[:], tA[:, n, 8:16], True, True, r=[lo, tA], w=p)
                    k.tt("dve", Vc[:, n, :], p[:, 0:16], car[:, n, :], ALU.add, r=[p, car], w=Vc)
                k.tt("dve", bia[:], bia[:], Vc[:], ALU.subtract, r=[bia, Vc], w=bia)
            Q = [k.sb(f"sQ{d}", [128, T], F32) for d in range(2)]
            YZ = k.sb("sYZ", [128, T], F32); SS = k.sb("sSS", [128, T], F32)
            Es = k.rot("sE", [128, 512], F32, 3); Ws = k.rot("sW", [128, 512], BF16, 3)
            tds = k.rot("std", [128, 128], F32, 2)
            szs = k.rot("ssz", [128, 512], F32, 2); sqs = k.rot("ssq", [128, 512], F32, 2)
            wzp = k.rot("swz", [128, 8, 128], BF16, 2)
            accb = Rot(self.ps[0:2]); rotb = Rot(self.ps[2:8])
            for h in range(8):
                g = h // 4
                hp = 64 * (h % 2)
                for d in range(2):
                    for n0 in range(0, NT, 4):
                        p = rotb.next()
                        nn = min(4, NT - n0)
                        for q in range(nn):
                            k.mm(p[:, q * 128:(q + 1) * 128], Vc[:, n0 + q, d * 8 + h:d * 8 + h + 1].to_broadcast([128, 128]), self.ident[:],
                                 True, True, r=[Vc, self.ident], w=p)
                        k.cp("act", Q[d][:, n0 * 128:(n0 + nn) * 128], p[:, 0:nn * 128], r=p, w=Q[d])
                k.preload_lnexp()
                for t0, N in BLOCKS:
                    if t0 < CTX and not do_ctx:
                        continue
                    I0 = t0 // 128
                    nI = N // 128
                    keys = [0, 1] if t0 < CTX else list(range(NT))
                    Y = accb.next()
                    first = True
                    for q in range(nI):
                        k.op("pe", lambda e: e.matmul(Y[hp:hp + 64, q * 128:(q + 1) * 128], lhsT=Xt[:, I0 + q, h * 64:(h + 1) * 64], rhs=dI[:, h, :],
                                                      start=first, stop=False, skip_group_check=True), r=[Xt.b[I0 + q], dI], w=Y)
                        first = False
                    def mkA(J):
                        def A():
                            CBp = rotb.next()
                            k.mm(CBp[:, :N], BT[:, g, J * 128:(J + 1) * 128], CT[:, g, t0:t0 + N], True, True, r=[BT, CT], w=CBp)
                            return CBp
                        return A

                    def mkB(J):
                        def B(CBp):
                            for d in range(2):
                                mk = mf if d == 0 else mb
                                if J < 2 and t0 >= CTX:
                                    lo_, hi_, dlo = 0, N, None
                                else:
                                    dd = J - I0
                                    if d == 0:
                                        lo_, hi_ = max(dd * 128, 0), N
                                    else:
                                        lo_, hi_ = 0, min((dd + 1) * 128, N)
                                    dlo = dd * 128 if 0 <= dd < nI else None
                                if lo_ >= hi_:
                                    continue
                                bcol = bia[:, J, d * 8 + h:d * 8 + h + 1]
                                E = Es.next(); W = Ws.next()
                                segs = [(lo_, hi_)]
                                if dlo is not None:
                                    segs = [(a_, b_) for a_, b_ in ((lo_, dlo), (dlo + 128, hi_)) if a_ < b_]
                                    td = tds.next()
                                    k.tt("pool", td[:], Q[d][:, t0 + dlo:t0 + dlo + 128], mk[:], ALU.add, r=[Q[d], mk], w=td)
                                    k.act(E[:, dlo:dlo + 128], td[:], AF.Exp, r=[td, bia], w=E, bias=bcol)
                                for a_, b_ in segs:
                                    k.act(E[:, a_:b_], Q[d][:, t0 + a_:t0 + b_], AF.Exp, r=[Q[d], bia], w=E, bias=bcol)
                                k.tt("dve", W[:, lo_:hi_], E[:, lo_:hi_], CBp[:, lo_:hi_], ALU.mult, r=[E, CBp], w=W)
                                k.op("pe", lambda e: e.matmul(Y[hp:hp + 64, lo_:hi_], lhsT=Xt[:, J, h * 64:(h + 1) * 64], rhs=W[:, lo_:hi_],
                                                              start=False, stop=False, skip_group_check=True), r=[Xt.b[J], W], w=Y)
                        return B

                    run_pipe([(mkA(J), mkB(J)) for J in keys], 2)
                    k.cp("act", YZ[hp:hp + 64, t0:t0 + N], Y[hp:hp + 64, :N], r=Y, w=YZ)
                if h % 2 == 1:
                    c = h // 2
                    wz = self.wload(wzp, "in", l, 8, [(O_SZ + c * 128, 128)])
                    for t0, N in BLOCKS:
                        if t0 < CTX and not do_ctx:
                            continue
                        p = rotb.next()
                        self.proj(p, N, wz, 0, HM, t0, 8)
                        sz = szs.next(); sq = sqs.next()
                        k.act(sz[:, :N], p[:, :N], AF.Silu, r=p, w=sz)
                        k.tt("pool", YZ[:, t0:t0 + N], YZ[:, t0:t0 + N], sz[:, :N], ALU.mult, r=[YZ, sz], w=YZ)
                        k.tt("pool", sq[:, :N], YZ[:, t0:t0 + N], YZ[:, t0:t0 + N], ALU.mult, r=YZ, w=sq)
                        p2 = rotb.next()
                        k.mm(p2[:, :N], self.ones[:], sq[:, :N], True, True, r=[self.ones, sq], w=p2)
                        if c == 0:
                            k.cp("act", SS[:, t0:t0 + N], p2[:, :N], r=p2, w=SS)
                        else:
                            k.tt("dve", SS[:, t0:t0 + N], SS[:, t0:t0 + N], p2[:, :N], ALU.add, r=[SS, p2], w=SS)
                        k.cp("pool", BR[:, c, t0:t0 + N], YZ[:, t0:t0 + N], r=YZ, w=[BR.b[n] for n in tiles_of(t0, N)])
            k.preload_lnexp()
            k.act(SS[:, tq0:T], SS[:, tq0:T], AF.Ln, r=SS, w=SS, scale=1.0 / 512, bias=1e-6)
            k.act(SS[:, tq0:T], SS[:, tq0:T], AF.Exp, r=SS, w=SS, scale=-0.5)
            for c in range(4):
                k.stt(BR[:, c, tq0:T], BR[:, c, tq0:T], pr[:, 40 + c:41 + c], SS[:, tq0:T], ALU.mult, ALU.mult, r=[BR, pr, SS], w=BR)

    def postnorm_tiles(self, b, l, tiles, mm_fn, src_which, mod_i, lnw, lnb, dst_fn, depth=2, la=None):
        k = self.k
        with k.scope():
            bc = k.sb("bc", [128, 4, D], F32)
            k.dma(bc[:, 0, :], self.MOD[l, b:b + 1, mod_i * D:(mod_i + 1) * D].to_broadcast([128, D]), key=bc, w=bc)
            k.dma(bc[:, 1, :], self.MOD[l, 4:5, mod_i * D:(mod_i + 1) * D].to_broadcast([128, D]), key=bc, w=bc)
            k.dma(bc[:, 2, :], self.din[lnw][l:l + 1, :].to_broadcast([128, D]), key=bc, w=bc)
            k.dma(bc[:, 3, :], self.din[lnb][l:l + 1, :].to_broadcast([128, D]), key=bc, w=bc)
            LA = la or (depth + 1)
            hts = k.rot("pht", [128, D], F32, LA + 1); t1s = k.rot("pt1", [128, D], F32, depth + 2)
            sts = k.rot("pst", [128, 16], F32, 4)
            tl = list(tiles)
            loads = {}

            def L(i):
                if i < len(tl):
                    ht = hts.next()
                    k.dma(ht[:], self.hsrc(b, l, tl[i], src_which), key=ht, w=ht)
                    loads[i] = ht

            for i in range(LA):
                L(i)

            def mkA(i, n):
                def A():
                    L(i + LA)
                    j = 1 if n < 2 else 0
                    o = [self.psum(), self.psum()]
                    mm_fn(n, o)
                    ht = loads.pop(i); t1 = t1s.next(); st = sts.next()
                    for half in range(2):
                        k.tt("dve", t1[:, half * 512:(half + 1) * 512], o[half][:, :], bc[:, j, half * 512:(half + 1) * 512], ALU.mult, r=[o[half], bc], w=t1)
                    k.stt(t1[:], ht[:], ALPHA, t1[:], ALU.mult, ALU.add, r=[ht, t1], w=t1)
                    self.ln_stats(t1, st)
                    return t1, st
                return A

            def mkB(n):
                def B(state):
                    t1, st = state
                    self.ln_apply(t1, t1, st)
                    k.tt("pool", t1[:], t1[:], bc[:, 2, :], ALU.mult, r=[t1, bc], w=t1)
                    k.tt("pool", t1[:], t1[:], bc[:, 3, :], ALU.add, r=[t1, bc], w=t1)
                    k.dma(dst_fn(n), t1[:], key=t1, r=t1)
                return B

            run_pipe([(mkA(i, n), mkB(n)) for i, n in enumerate(tl)], depth)

    def merge(self, b, l, HM, BR, do_ctx):
        k = self.k
        blocks = [bl for bl in BLOCKS if do_ctx or bl[0] >= CTX]
        tiles = list(range(0 if do_ctx else 2, NT))
        with k.scope():
            ACC = k.sb("ACC", [128, 8, T], BF16, nb=NT)
            with k.scope():
                wgp = k.rot("wgate", [128, 8, 512], BF16, 2)
                wbp = k.rot("wbr", [128, 4, 512], BF16, 2)
                sgs = k.rot("msg", [128, 512], F32, 3); accs = k.rot("macc", [128, 512], F32, 2); tms = k.rot("mtm", [128, 512], F32, 2)
                rotb = Rot(self.ps)
                for oc in range(8):
                    wg = self.wload(wgp, "in", l, 8, [(1024 * i + 128 * oc, 128) for i in range(4)])
                    wb = wbp.next()
                    k.dma_group([(wb[:, 0:4, i * 128:(i + 1) * 128],
                                  self.WB["br"][l][512 * i:512 * (i + 1), oc * 128:(oc + 1) * 128].rearrange("(kc p) n -> p kc n", p=128)) for i in range(4)],
                                key=wb, w=wb)
                    for t0, N in blocks:
                        acc = accs.next()
                        tl = [ACC.b[n] for n in tiles_of(t0, N)]
                        for i in range(4):
                            G = rotb.next()
                            self.proj(G, N, wg, i * 128, HM, t0, 8)
                            Pj = rotb.next()
                            self.proj(Pj, N, wb, i * 128, BR[i], t0, 4)
                            sg = sgs.next()
                            k.act(sg[:, :N], G[:, :N], AF.Sigmoid, r=G, w=sg)
                            if i == 0:
                                k.tt("dve", acc[:, :N], Pj[:, :N], sg[:, :N], ALU.mult, r=[Pj, sg], w=acc)
                            else:
                                tm = tms.next()
                                k.tt("dve", tm[:, :N], Pj[:, :N], sg[:, :N], ALU.mult, r=[Pj, sg], w=tm)
                                if i < 3:
                                    k.tt("pool", acc[:, :N], acc[:, :N], tm[:, :N], ALU.add, r=[acc, tm], w=acc)
                                else:
                                    k.tt("pool", ACC[:, oc, t0:t0 + N], acc[:, :N], tm[:, :N], ALU.add, r=[acc, tm], w=tl)
            if self.dbg.get("stop") == "acc":
                self.dump("acc", ACC[:], [128, 8, T], BF16, r=ACC)
                return
            with k.scope():
                wo = self.wload(k.rot("wo", [128, 8, 1024], BF16, 1), "out", l, 8, [(0, 1024)])

                def mmf(n, o):
                    for half in range(2):
                        for kc in range(8):
                            k.mm(o[half][:, :], ACC[:, kc, n * 128:(n + 1) * 128], wo[:, kc, half * 512:(half + 1) * 512], kc == 0, kc == 7,
                                 r=[ACC.b[n], wo], w=o[half])

                self.postnorm_tiles(b, l, tiles, mmf, 0, 2, "ln1_w", "ln1_b", lambda n: self.HS[1][n * 128:(n + 1) * 128, :], depth=1)

    def ffn(self, b, l, do_ctx, is_last):
        k = self.k
        tq0 = 0 if do_ctx else CTX
        blocks = [bl for bl in BLOCKS if do_ctx or bl[0] >= CTX]
        tiles = list(range(0 if do_ctx else 2, NT))
        segs = [(CTX, T)] if not do_ctx else [(0, CTX), (CTX, T)]
        with k.scope():
            ACTT = k.sb("ACTT", [128, 22, T], BF16, nb=NT)
            with k.scope():
                HM2 = k.sb("HM2", [128, 8, T], BF16, nb=NT)
                HM2.b2 = [Buf(f"HM2x{i}") for i in range(NT)]
                self.ln_mod_phase(b, l, 1, HM2, tiles)
                prA = k.sb("fprA", [128, 88], F32); prB = k.sb("fprB", [128, 88], F32)
                self.fm_params(prA, [self.din["ffn_conv_w"][l, 0:2, :].rearrange("k (c p) -> (k c) p", p=128)])
                self.fm_params(prB, [self.din["ffn_conv_w"][l, 2, :].rearrange("(c p) -> c p", p=128), self.din["ffn_conv_b"][l].rearrange("(c p) -> c p", p=128)])
                wup = k.rot("wup", [128, 8, 256], BF16, 2)
                U = [k.sb(f"fU{i}", [128, T], F32) for i in range(2)]
                Yv = [k.sb(f"fY{i}", [128, T], F32) for i in range(2)]
                for c in range(22):
                    w = self.wload(wup, "in" if False else "up", l, 8, [(c * 128, 128), (DFF + c * 128, 128)])
                    for part in range(2):
                        cc = part * 22 + c
                        Up = U[part]; Y = Yv[part]
                        for t0, N in blocks:
                            p = self.psum()
                            self.proj(p, N, w, part * 128, HM2, t0, 8)
                            k.cp("act", Up[:, t0:t0 + N], p[:, :N], r=p, w=Up)
                        eng = "dve" if part == 0 else "pool"
                        k.ts(eng, Y[:, tq0:T], Up[:, tq0:T], prA[:, 44 + cc:45 + cc], prB[:, 44 + cc:45 + cc], ALU.mult, ALU.add, r=[Up, prA, prB], w=Y)
                        for s_, e_ in segs:
                            k.stt(Y[:, s_ + 1:e_], Up[:, s_:e_ - 1], prA[:, cc:cc + 1], Y[:, s_ + 1:e_], ALU.mult, ALU.add, r=[Up, Y, prA], w=Y)
                            k.stt(Y[:, s_:e_ - 1], Up[:, s_ + 1:e_], prB[:, cc:cc + 1], Y[:, s_:e_ - 1], ALU.mult, ALU.add, r=[Up, Y, prB], w=Y)
                    k.act(Yv[0][:, tq0:T], Yv[0][:, tq0:T], AF.Silu, r=Yv[0], w=Yv[0])
                    k.tt("pool", ACTT[:, c, tq0:T], Yv[0][:, tq0:T], Yv[1][:, tq0:T], ALU.mult, r=[Yv[0], Yv[1]], w=ACTT)
            with k.scope():
                wd = k.sb("wd", [128, 22, D], BF16)
                srcw = self.WB["dn"][l].rearrange("(kc p) n -> p kc n", p=128)
                k.dma_group([(wd[:, 0:8, :], srcw[:, 0:8, :]), (wd[:, 8:16, :], srcw[:, 8:16, :]), (wd[:, 16:22, :], srcw[:, 16:22, :])], key=wd, w=wd)

                def mmf(n, o):
                    for half in range(2):
                        for c in range(22):
                            k.mm(o[half][:, :], ACTT[:, c, n * 128:(n + 1) * 128], wd[:, c, half * 512:(half + 1) * 512], c == 0, c == 21,
                                 r=[ACTT, wd], w=o[half])

                if is_last:
                    dst = lambda n: self.out[b, (n - 2) * 128:(n - 1) * 128, :]
                else:
                    dst = lambda n: self.HS[0][n * 128:(n + 1) * 128, :]
                self.postnorm_tiles(b, l, tiles, mmf, 1, 5, "ln2_w", "ln2_b", dst, depth=1, la=4)

    def layer(self, b, l):
        k = self.k
        last = (l == L - 1)
        with k.scope():
            HM = k.sb("HM", [128, 8, T], BF16, nb=NT)
            HM.b2 = [Buf(f"HMx{i}") for i in range(NT)]
            self.ln_mod_phase(b, l, 0, HM, range(NT))
            if self.dbg.get("stop") == "hm":
                self.dump("hm", HM[:], [128, 8, T], BF16, r=HM)
                return
            BR = [None] * 4
            BR[0] = k.sb("BR0", [128, 4, T], BF16, nb=NT)
            if not self.dbg.get("skip_ret"):
                self.retention(b, l, HM, BR[0], not last)
            if self.dbg.get("stop") == "ret":
                self.dump("ret", BR[0][:], [128, 4, T], BF16, r=BR[0])
                return
            BR[1] = k.sb("BR1", [128, 4, T], BF16, nb=NT)
            if not self.dbg.get("skip_mla"):
                self.mla(b, l, HM, BR[1], not last)
            if self.dbg.get("stop") == "mla":
                self.dump("mla", BR[1][:], [128, 4, T], BF16, r=BR[1])
                return
            BR[3] = k.sb("BR3", [128, 4, T], BF16, nb=NT)
            if not self.dbg.get("skip_ssd"):
                self.ssd(b, l, HM, BR[3], not last)
            if self.dbg.get("stop") == "ssd":
                self.dump("ssd", BR[3][:], [128, 4, T], BF16, r=BR[3])
                return
            BR[2] = k.sb("BR2", [128, 4, T], BF16, nb=NT)
            if not self.dbg.get("skip_lru"):
                self.lru(b, l, HM, BR[2], not last)
            if self.dbg.get("stop") == "lru":
                for i, nm in enumerate(("ret", "mla", "lru", "ssd")):
                    self.dump(nm, BR[i][:], [128, 4, T], BF16, r=BR[i])
                return
            self.merge(b, l, HM, BR, not last)
            if self.dbg.get("stop") == "acc":
                return
        if self.dbg.get("stop") == "h1":
            self.dump("h1", self.HS[1], [T, D], F32, r=Buf("dummy"))
            return
        self.ffn(b, l, not last, last)
        if self.dbg.get("stop") == "h2" and l == self.nl - 1:
            self.dump("h2", self.HS[0], [T, D], F32, r=Buf("dummy2"))


def host_consts():
    c = {}
    c["k_ident"] = np.eye(128, dtype=np.float32)
    c["k_ones"] = np.ones((128, 128), np.float32)
    c["k_identb"] = np.eye(128).astype(ml_dtypes.bfloat16)
    c["k_onesb"] = np.ones((128, 128)).astype(ml_dtypes.bfloat16)

    def rope(dim):
        rows = SEQ // 64
        r, col = np.meshgrid(np.arange(rows, dtype=np.float32), np.arange(64, dtype=np.float32), indexing="ij")
        quarter = dim // 4
        inv = (np.float32(10000.0) ** (-np.arange(quarter, dtype=np.float32) / np.float32(quarter))).astype(np.float32)
        ang = np.concatenate([r.reshape(-1, 1) * inv, col.reshape(-1, 1) * inv], axis=-1).astype(np.float32)
        return np.cos(ang).astype(np.float32), np.sin(ang).astype(np.float32)

    cs, sn = rope(128)
    p = np.arange(128)
    c["k_retC"] = np.ascontiguousarray(cs[:, p % 64].T)
    c["k_retS"] = np.ascontiguousarray((sn[:, p % 64] * np.where(p < 64, -1.0, 1.0)[None, :]).T.astype(np.float32))
    cs, sn = rope(64)
    c["k_mlaC"] = np.ascontiguousarray(cs[:, p % 32].T)
    c["k_mlaS"] = np.ascontiguousarray((sn[:, p % 32] * np.where((p % 64) < 32, -1.0, 1.0)[None, :]).T.astype(np.float32))
    j = np.arange(128)[:, None]; i = np.arange(128)[None, :]
    c["k_mf"] = np.where(i >= j, 0.0, NEG).astype(np.float32)
    c["k_mb"] = np.where(i <= j, 0.0, NEG).astype(np.float32)
    c["k_up"] = (j <= i).astype(np.float32)
    c["k_lo"] = (j >= i).astype(np.float32)
    t = np.arange(T)
    posf = (t + 1).astype(np.float32)
    posb = np.where(t < CTX, CTX - t, T - (t - CTX)).astype(np.float32)
    c["k_posf"] = np.ascontiguousarray(np.broadcast_to(posf[None, :], (128, T)))
    c["k_posb"] = np.ascontiguousarray(np.broadcast_to(posb[None, :], (128, T)))
    c["k_posf_tm"] = np.ascontiguousarray(posf.reshape(NT, 128).T)
    c["k_posb_tm"] = np.ascontiguousarray(posb.reshape(NT, 128).T)
    return c


def make_in_maps(inputs, n_cores, nb):
    consts = host_consts()
    f = lambda a: np.ascontiguousarray(np.asarray(a, dtype=np.float32))
    shared = {
        "ada_w": f(inputs["ada_w"]), "ada_b": f(inputs["ada_b"]), "w_in": f(inputs["w_in"]),
        "ret_decay": f(inputs["ret_decay"]).reshape(L, 8), "ret_gn_w": f(inputs["ret_gn_w"]), "ret_gn_b": f(inputs["ret_gn_b"]),
        "mla_q_norm": f(inputs["mla_q_norm"]), "mla_w_uq": f(inputs["mla_w_uq"]), "mla_kv_norm": f(inputs["mla_kv_norm"]), "mla_w_ukv": f(inputs["mla_w_ukv"]),
        "lru_conv_w": f(inputs["lru_conv_w"]), "lru_conv_b": f(inputs["lru_conv_b"]), "lru_gate_w": f(inputs["lru_gate_w"]),
        "lru_gate_b": f(inputs["lru_gate_b"]), "lru_lambda": f(inputs["lru_lambda"]),
        "ssd_conv_w": f(inputs["ssd_conv_w"]), "ssd_conv_b": f(inputs["ssd_conv_b"]), "ssd_dt_bias": f(inputs["ssd_dt_bias"]).reshape(L, 16),
        "ssd_a_log": f(inputs["ssd_a_log"]).reshape(L, 16), "ssd_d": f(inputs["ssd_d"]), "ssd_norm_w": f(inputs["ssd_norm_w"]),
        "w_branch": f(inputs["w_branch"]).reshape(L, 2048, D), "w_out": f(inputs["w_out"]), "ln1_w": f(inputs["ln1_w"]), "ln1_b": f(inputs["ln1_b"]),
        "ffn_w_up": f(inputs["ffn_w_up"]), "ffn_conv_w": f(inputs["ffn_conv_w"]), "ffn_conv_b": f(inputs["ffn_conv_b"]), "ffn_w_down": f(inputs["ffn_w_down"]),
        "ln2_w": f(inputs["ln2_w"]), "ln2_b": f(inputs["ln2_b"]),
    }
    shared.update(consts)
    x = f(inputs["x"]); ctx = f(inputs["ctx"]); c = f(inputs["c"]); cc = f(inputs["c_ctx"])
    maps = []
    for i in range(n_cores):
        sl = slice(i * nb, (i + 1) * nb)
        c5 = np.zeros((5, D), np.float32)
        c5[:nb] = c[sl]
        c5[4] = cc
        c5T = np.ascontiguousarray(c5.reshape(5, 8, 128).transpose(2, 1, 0).reshape(128, 40))
        m = dict(shared)
        m.update({"x": np.ascontiguousarray(x[sl]), "ctx": np.ascontiguousarray(ctx[sl]), "c5T": c5T})
        maps.append(m)
    return maps


def kernel(**inputs):
    n_cores = 8
    prog = Prog(NB, L)
    nc = prog.build()
    maps = make_in_maps(inputs, n_cores, NB)
    res = run_bass_kernel_spmd(nc, maps, core_ids=list(range(n_cores)))
    out = np.concatenate([np.asarray(r["out"]) for r in res.results], axis=0)
    return out.astype(np.float32)
```

```python
import os
from contextlib import ExitStack, contextmanager
import numpy as np
import ml_dtypes
import concourse.bass as bass
import concourse.mybir as mybir
from concourse.bass_utils import run_bass_kernel_spmd

F32 = mybir.dt.float32
BF16 = mybir.dt.bfloat16
AF = mybir.ActivationFunctionType
ALU = mybir.AluOpType
AX = mybir.AxisListType

D = 1024
SEQ = 2048
CTX = 256
T = SEQ + CTX
NT = T // 128
L = 2
NB = 4
INW = 9424
DFF = 2816
ALPHA = (2 * L) ** 0.25
O_GATE, O_RQ, O_RK, O_RV, O_RG = 0, 4096, 4608, 5120, 5632
O_CQ, O_CKV, O_KR = 6144, 6528, 6784
O_LX, O_LG = 6848, 7360
O_SZ, O_SX, O_SDT = 7872, 8384, 9408
NEG = -30000.0


class Sem:
    def __init__(self, h):
        self.h = h
        self.total = 0


class Buf:
    __slots__ = ("name", "w", "r", "dsem", "psum")

    def __init__(self, name):
        self.name = name
        self.w = None
        self.r = {}
        self.dsem = None
        self.psum = False


class TT:
    def __init__(self, t, nb=1, name=""):
        self.t = t
        self.b = [Buf(f"{name}{i}") for i in range(nb)]
        self.b2 = None

    def rb(self, n):
        return [self.b[n]] if self.b2 is None else [self.b[n], self.b2[n]]

    def __getitem__(self, k):
        return self.t[k]


class Eng:
    def __init__(self, name, e, sem):
        self.name = name
        self.e = e
        self.sem = sem
        self.cnt = 0
        self.seen = {}

    def wait(self, sem, val):
        if self.seen.get(sem, 0) >= val:
            return
        self.e.wait_ge(sem.h, val)
        self.seen[sem] = val


def _flat(x):
    out = []
    if isinstance(x, (Buf, TT)):
        x = [x]
    for a in x:
        if a is None:
            continue
        if isinstance(a, Buf):
            out.append(a)
        elif isinstance(a, TT):
            out.extend(a.b)
            if a.b2 is not None:
                out.extend(a.b2)
        elif isinstance(a, (list, tuple)):
            out.extend(_flat(a))
        else:
            raise TypeError(f"bad dep object {type(a)}")
    return out


class K:
    def __init__(self, nc, es):
        self.nc = nc
        self.es = es
        self.E = {}
        for name, e in (("pe", nc.tensor), ("act", nc.scalar), ("dve", nc.vector), ("pool", nc.gpsimd), ("sp", nc.sync)):
            self.E[name] = Eng(name, e, Sem(es.enter_context(nc.semaphore("s_" + name))))
        self.dfree = [Sem(es.enter_context(nc.semaphore(f"dq{i}"))) for i in range(80)]
        self.dall = list(self.dfree)
        self.scopes = []
        self.uid = 0
        self.nins = 0

    @contextmanager
    def scope(self):
        st = ExitStack()
        rec = {"st": st, "bufs": []}
        self.scopes.append(rec)
        try:
            yield
        finally:
            self.barrier()
            for b in rec["bufs"]:
                if b.dsem is not None:
                    self.dfree.append(b.dsem)
                    b.dsem = None
            self.scopes.pop()
            st.close()

    def sb(self, name, shape, dt, nb=1):
        self.uid += 1
        st = self.scopes[-1]["st"] if self.scopes else self.es
        t = st.enter_context(self.nc.sbuf_tensor(f"{name}_{self.uid}", list(shape), dt))
        tt = TT(t, nb, name)
        if self.scopes:
            self.scopes[-1]["bufs"].extend(tt.b)
        return tt

    def rot(self, name, shape, dt, n):
        return Rot([self.sb(f"{name}{i}", shape, dt) for i in range(n)])

    def _deps(self, E, r, w):
        deps = []
        for b in r:
            if b.w is not None:
                deps.append(b.w)
            if b.psum:
                deps.extend((s_, v_) for s_, v_ in b.r.items() if s_ is not E.sem)
        for b in w:
            if b.w is not None:
                deps.append(b.w)
            deps.extend(b.r.items())
        for s, v in deps:
            if E.name == "pe" and s is E.sem:
                continue
            E.wait(s, v)

    def _record(self, ev, r, w):
        s, v = ev
        for b in r:
            if b.r.get(s, 0) < v:
                b.r[s] = v
        for b in w:
            b.w = ev
            b.r = {}

    def op(self, eng, fn, r=(), w=(), ww=()):
        E = self.E[eng]
        r = _flat(r)
        w = _flat(w)
        self._deps(E, r, w)
        ins = fn(E.e)
        E.cnt += 1
        ins.then_inc(E.sem.h, 1)
        self._record((E.sem, E.cnt), r, w)
        for b in _flat(ww):
            b.w = (E.sem, E.cnt)
        self.nins += 1
        return ins

    def dma(self, out, in_, key, r=(), w=(), **kw):
        E = self.E["sp"]
        r = _flat(r)
        w = _flat(w)
        key = _flat([key])[0]
        self._deps(E, r, w)
        if key.dsem is None:
            key.dsem = self.dfree.pop()
        s = key.dsem
        ins = E.e.dma_start(out=out, in_=in_, **kw)
        s.total += 16
        ins.then_inc(s.h, 16)
        self._record((s, s.total), r, w)
        self.nins += 1

    def dma_group(self, pairs, key, r=(), w=()):
        E = self.E["sp"]
        r = _flat(r)
        w = _flat(w)
        key = _flat([key])[0]
        self._deps(E, r, w)
        if key.dsem is None:
            key.dsem = self.dfree.pop()
        s = key.dsem
        for out, in_ in pairs:
            ins = E.e.dma_start(out=out, in_=in_)
            s.total += 16
            ins.then_inc(s.h, 16)
            self.nins += 1
        self._record((s, s.total), r, w)

    def barrier(self):
        sp = self.E["sp"]
        for n, E in self.E.items():
            if n != "sp" and E.cnt > 0:
                sp.wait(E.sem, E.cnt)
        for s in self.dall:
            if s.total > 0:
                sp.wait(s, s.total)
        sp.cnt += 1
        sp.e.sem_inc(sp.sem.h, 1)
        for n, E in self.E.items():
            if n != "sp":
                E.wait(sp.sem, sp.cnt)

    def mm(self, out, lhsT, rhs, start, stop, r, w):
        return self.op("pe", lambda e: e.matmul(out, lhsT=lhsT, rhs=rhs, start=start, stop=stop), r=r, w=w)

    def tr(self, out, in_, ident, r, w):
        return self.op("pe", lambda e: e.transpose(out, in_, ident), r=r, w=w)

    def act(self, out, in_, func, r, w, scale=1.0, bias=0.0, eng="act", ww=()):
        return self.op(eng, lambda e: e.activation(out=out, in_=in_, func=func, scale=scale, bias=bias), r=r, w=w, ww=ww)

    def tt(self, eng, out, in0, in1, op, r, w):
        return self.op(eng, lambda e: e.tensor_tensor(out=out, in0=in0, in1=in1, op=op), r=r, w=w)

    def ts(self, eng, out, in0, s1, s2, op0, op1, r, w, ww=()):
        if s2 is None:
            return self.op(eng, lambda e: e.tensor_scalar(out=out, in0=in0, scalar1=s1, scalar2=None, op0=op0), r=r, w=w, ww=ww)
        return self.op(eng, lambda e: e.tensor_scalar(out=out, in0=in0, scalar1=s1, scalar2=s2, op0=op0, op1=op1), r=r, w=w, ww=ww)

    def stt(self, out, in0, scalar, in1, op0, op1, r, w, eng="dve"):
        return self.op(eng, lambda e: e.scalar_tensor_tensor(out=out, in0=in0, scalar=scalar, in1=in1, op0=op0, op1=op1), r=r, w=w)

    def preload_lnexp(self):
        return

    def cp(self, eng, out, in_, r, w):
        if eng == "act":
            return self.op("act", lambda e: e.copy(out=out, in_=in_), r=r, w=w)
        return self.op(eng, lambda e: e.tensor_copy(out=out, in_=in_), r=r, w=w)


class Rot:
    def __init__(self, items):
        self.items = items
        self.i = 0

    def next(self):
        x = self.items[self.i % len(self.items)]
        self.i += 1
        return x


def run_pipe(steps, depth=2):
    st = {}
    n = len(steps)
    for i in range(min(depth, n)):
        st[i] = steps[i][0]()
    for i in range(n):
        if i + depth < n:
            st[i + depth] = steps[i + depth][0]()
        steps[i][1](st.pop(i))


def tiles_of(t0, n):
    return list(range(t0 // 128, (t0 + n + 127) // 128))


BLOCKS = [(0, 256)] + [(256 + 512 * m, 512) for m in range(4)]


class Prog:
    def __init__(self, nb=NB, nl=L, dbg=None):
        self.nb, self.nl, self.dbg = nb, nl, dbg or {}
        self.nc = nc = bass.Bass("TRN2", target_bir_lowering=False)
        self.es = ExitStack()
        self.k = K(nc, self.es)
        self.din = {}
        self.dbg_out = {}

        def inp(name, shape, dt=F32):
            self.din[name] = nc.dram_tensor(name, list(shape), dt, kind="ExternalInput").ap()
            return self.din[name]

        inp("x", [nb, SEQ, D]); inp("ctx", [nb, CTX, D]); inp("c5T", [128, 8 * 5])
        inp("ada_w", [L, D, 6 * D]); inp("ada_b", [L, 6 * D]); inp("w_in", [L, D, INW])
        inp("ret_decay", [L, 8]); inp("ret_gn_w", [L, 512]); inp("ret_gn_b", [L, 512])
        inp("mla_q_norm", [L, 384]); inp("mla_w_uq", [L, 384, 768]); inp("mla_kv_norm", [L, 256]); inp("mla_w_ukv", [L, 256, 1024])
        inp("lru_conv_w", [L, 4, 512]); inp("lru_conv_b", [L, 512]); inp("lru_gate_w", [L, 2, 2, 8, 64, 64])
        inp("lru_gate_b", [L, 2, 2, 512]); inp("lru_lambda", [L, 2, 512])
        inp("ssd_conv_w", [L, 4, 1024]); inp("ssd_conv_b", [L, 1024]); inp("ssd_dt_bias", [L, 16]); inp("ssd_a_log", [L, 16])
        inp("ssd_d", [L, 8]); inp("ssd_norm_w", [L, 512])
        inp("w_branch", [L, 2048, D]); inp("w_out", [L, D, D]); inp("ln1_w", [L, D]); inp("ln1_b", [L, D])
        inp("ffn_w_up", [L, D, 2 * DFF]); inp("ffn_conv_w", [L, 3, 2 * DFF]); inp("ffn_conv_b", [L, 2 * DFF]); inp("ffn_w_down", [L, DFF, D])
        inp("ln2_w", [L, D]); inp("ln2_b", [L, D])
        inp("k_ident", [128, 128]); inp("k_ones", [128, 128]); inp("k_identb", [128, 128], BF16); inp("k_onesb", [128, 128], BF16)
        inp("k_retC", [128, SEQ]); inp("k_retS", [128, SEQ]); inp("k_mlaC", [128, SEQ]); inp("k_mlaS", [128, SEQ])
        inp("k_mf", [128, 128]); inp("k_mb", [128, 128]); inp("k_up", [128, 128]); inp("k_lo", [128, 128])
        inp("k_perm", [128, 128], BF16)
        inp("k_posf", [128, T]); inp("k_posb", [128, T]); inp("k_posf_tm", [128, NT]); inp("k_posb_tm", [128, NT])
        self.out = nc.dram_tensor("out", [nb, SEQ, D], F32, kind="ExternalOutput").ap()

        def scr(name, shape, dt):
            return nc.dram_tensor(name, list(shape), dt, kind="Internal").ap()

        self.WB = {
            "in": scr("wb_in", [L, D, INW], BF16), "uq": scr("wb_uq", [L, 384, 768], BF16), "ukv": scr("wb_ukv", [L, 256, 1024], BF16),
            "br": scr("wb_br", [L, 2048, D], BF16), "out": scr("wb_out", [L, D, D], BF16), "up": scr("wb_up", [L, D, 2 * DFF], BF16),
            "dn": scr("wb_dn", [L, DFF, D], BF16),
        }
        self.MOD = scr("modscr", [L, 5, 6 * D], F32)
        self.HS = [scr("h_a", [T, D], F32), scr("h_b", [T, D], F32)]

    def dump(self, name, tt_or_ap, shape, dt, r):
        k = self.k
        o = self.nc.dram_tensor("dbg_" + name, list(shape), dt, kind="ExternalOutput").ap()
        self.dbg_out[name] = o
        key = _flat([r])[0]
        k.dma(o, tt_or_ap, key=key, r=r)

    def build(self):
        k = self.k
        nc = self.nc
        with self.es:
            self.setup_consts()
            self.stage0()
            if self.dbg.get("stop") in ("ada",):
                k.barrier()
                return nc
            for b in range(self.nb):
                for l in range(self.nl):
                    self.layer(b, l)
            k.barrier()
        return nc

    def setup_consts(self):
        k = self.k
        self.ident = k.sb("ident", [128, 128], F32)
        self.ones = k.sb("ones", [128, 128], F32)
        self.identb = k.sb("identb", [128, 128], BF16)
        self.onesb = k.sb("onesb", [128, 128], BF16)
        for tt, nm in ((self.ident, "k_ident"), (self.ones, "k_ones"), (self.identb, "k_identb"), (self.onesb, "k_onesb")):
            k.dma(tt[:], self.din[nm], key=tt, w=tt)
        self.ps = [TT(self.es.enter_context(self.nc.psum_tensor(f"ps{i}", [128, 512], F32)), 1, f"ps{i}") for i in range(8)]
        self.psi = 0
        for p in self.ps:
            p.b[0].psum = True

    def psum(self):
        p = self.ps[self.psi % 8]
        self.psi += 1
        return p

    def stage0(self):
        k = self.k
        with k.scope():
            stg = k.rot("cv_in", [128, 2048], F32, 3)
            stb = k.rot("cv_out", [128, 2048], BF16, 3)
            engs = ["dve", "pool", "act"]
            srcs = [("in", "w_in", D, INW), ("uq", "mla_w_uq", 384, 768), ("ukv", "mla_w_ukv", 256, 1024), ("br", "w_branch", 2048, D),
                    ("out", "w_out", D, D), ("up", "ffn_w_up", D, 2 * DFF), ("dn", "ffn_w_down", DFF, D)]
            conv_jobs = []
            for l in range(self.nl):
                for key, nm, R, C in srcs:
                    for r0 in range(0, R, 128):
                        for c0 in range(0, C, 2048):
                            conv_jobs.append((l, key, nm, r0, c0, min(2048, C - c0)))

            def conv(i, job):
                l, key, nm, r0, c0, cw = job
                a = stg.next(); bb = stb.next()
                k.dma(a[:, :cw], self.din[nm][l, r0:r0 + 128, c0:c0 + cw], key=a, w=a)
                k.cp(engs[i % 3], bb[:, :cw], a[:, :cw], r=a, w=bb)
                k.dma(self.WB[key][l, r0:r0 + 128, c0:c0 + cw], bb[:, :cw], key=bb, r=bb)

            c5 = k.sb("c5", [128, 40], F32)
            sc = k.sb("sc", [128, 40], F32)
            k.dma(c5[:], self.din["c5T"], key=c5, w=c5)
            k.act(sc[:], c5[:], AF.Silu, r=c5, w=sc)
            aw = k.rot("aw", [128, 8, 512], F32, 2)
            ab = k.rot("ab", [1, 512], F32, 2)
            mo = k.rot("mo", [5, 512], F32, 2)

            def ada(l, cb):
                w_ = aw.next(); b_ = ab.next(); m_ = mo.next()
                k.dma(w_[:], self.din["ada_w"][l, :, cb * 512:(cb + 1) * 512].rearrange("(kc p) n -> p kc n", p=128), key=w_, w=w_)
                k.dma(b_[:], self.din["ada_b"][l:l + 1, cb * 512:(cb + 1) * 512], key=b_, w=b_)
                p = self.psum()
                for kc in range(8):
                    k.mm(p[0:5, :], sc[:, kc * 5:(kc + 1) * 5], w_[:, kc, :], kc == 0, False, r=[sc, w_], w=p)
                k.mm(p[0:5, :], self.ones[0:1, 0:5], b_[:], False, True, r=[self.ones, b_], w=p)
                add1 = 1.0 if cb // 2 in (1, 4) else 0.0
                k.act(m_[:], p[0:5, :], AF.Identity, r=p, w=m_, bias=add1)
                k.dma(self.MOD[l, :, cb * 512:(cb + 1) * 512], m_[:], key=m_, r=m_)

            ada_jobs = [(l, cb) for l in range(self.nl) for cb in range(12)]
            every = max(1, len(conv_jobs) // max(1, len(ada_jobs)))
            ai = 0
            for i, job in enumerate(conv_jobs):
                conv(i, job)
                if i % every == every - 1 and ai < len(ada_jobs):
                    ada(*ada_jobs[ai])
                    ai += 1
            while ai < len(ada_jobs):
                ada(*ada_jobs[ai])
                ai += 1

    def hsrc(self, b, l, n, which):
        if which == 0 and l == 0:
            if n < 2:
                return self.din["ctx"][b, n * 128:(n + 1) * 128, :]
            return self.din["x"][b, (n - 2) * 128:(n - 1) * 128, :]
        return self.HS[which][n * 128:(n + 1) * 128, :]

    def load_modF(self, b, l, i0, dst):
        k = self.k
        m8 = k.sb("m8", [32, 128], F32)
        for j, row in ((0, b), (1, 4)):
            for q in range(2):
                src = self.MOD[l, row, (i0 + q) * D:(i0 + q + 1) * D].rearrange("(c p) -> c p", p=128)
                o = (j * 2 + q) * 8
                k.dma(m8[o:o + 8, :], src, key=m8, w=m8)
        p = self.psum()
        k.tr(p[:, 0:32], m8[0:32, :], self.ident[0:32, 0:32], r=[m8, self.ident], w=p)
        k.cp("dve", dst[:, 0:32], p[:, 0:32], r=p, w=dst)

    def ln_mod_phase(self, b, l, which, HM, tiles):
        k = self.k
        with k.scope():
            mf = k.sb("modF", [128, 32], F32)
            i0 = 0 if which == 0 else 3
            self.load_modF(b, l, i0, mf)
            LA = 4
            hts = k.rot("ht", [128, D], F32, LA + 3)
            xns = k.rot("xn", [128, D], F32, 2)
            sts = k.rot("st", [128, 16], F32, 4)
            tl = list(tiles)
            loads = {}

            def Ld(i):
                if i < len(tl):
                    ht = hts.next()
                    k.dma(ht[:], self.hsrc(b, l, tl[i], which), key=ht, w=ht)
                    loads[tl[i]] = ht

            for i in range(LA):
                Ld(i)

            def mkA(n):
                def A():
                    Ld(tl.index(n) + LA)
                    ht = loads.pop(n); st = sts.next()
                    self.ln_stats(ht, st)
                    return ht, st
                return A

            def mkB(n):
                def B(state):
                    ht, st = state
                    xn = xns.next()
                    self.ln_apply(ht, xn, st)
                    j = 1 if n < 2 else 0
                    for half in range(2):
                        p = self.psum()
                        for q in range(4):
                            c = half * 4 + q
                            k.tr(p[:, q * 128:(q + 1) * 128], xn[:, c * 128:(c + 1) * 128], self.ident[:], r=[xn, self.ident], w=p)
                        for q in range(4):
                            c = half * 4 + q
                            if half == 0:
                                k.ts("dve", HM[:, c, n * 128:(n + 1) * 128], p[:, q * 128:(q + 1) * 128], mf[:, j * 16 + 8 + c:j * 16 + 9 + c],
                                     mf[:, j * 16 + c:j * 16 + c + 1], ALU.mult, ALU.add, r=[p, mf], w=[], ww=HM.b[n])
                            else:
                                k.act(HM[:, c, n * 128:(n + 1) * 128], p[:, q * 128:(q + 1) * 128], AF.Identity, r=[p, mf], w=[], ww=HM.b2[n],
                                      scale=mf[:, j * 16 + 8 + c:j * 16 + 9 + c], bias=mf[:, j * 16 + c:j * 16 + c + 1])
                return B

            run_pipe([(mkA(n), mkB(n)) for n in tiles], 2)

    def ln_stats(self, ht, st, eps=1e-6):
        k = self.k
        k.op("dve", lambda e: e.bn_stats(out=st[:, 0:6], in_=ht[:, 0:512]), r=ht, w=st)
        k.op("dve", lambda e: e.bn_stats(out=st[:, 6:12], in_=ht[:, 512:1024]), r=ht, w=st)
        k.op("dve", lambda e: e.bn_aggr(out=st[:, 12:14], in_=st[:, 0:12]), r=st, w=st)
        k.act(st[:, 14:15], st[:, 13:14], AF.Sqrt, r=st, w=st, bias=eps)

    def ln_apply(self, ht, xn, st):
        k = self.k
        k.op("dve", lambda e: e.reciprocal(out=st[:, 14:15], in_=st[:, 14:15]), r=st, w=st)
        k.stt(st[:, 15:16], st[:, 12:13], -1.0, st[:, 14:15], ALU.mult, ALU.mult, r=st, w=st)
        k.act(xn[:], ht[:], AF.Identity, r=[ht, st], w=xn, scale=st[:, 14:15], bias=st[:, 15:16])

    def wload(self, pool, key, l, kc_n, segs, rows0=0):
        k = self.k
        w = pool.next()
        src = self.WB[key][l]
        pairs = []
        o = 0
        for c0, n in segs:
            pairs.append((w[:, 0:kc_n, o:o + n], src[rows0:rows0 + kc_n * 128, c0:c0 + n].rearrange("(kc p) n -> p kc n", p=128)))
            o += n
        k.dma_group(pairs, key=w, w=w)
        return w

    def proj(self, p, N, w, wc0, src, t0, kc_n, extra_r=()):
        k = self.k
        rb = [src.rb(n) for n in tiles_of(t0, N)]
        for kc in range(kc_n):
            k.mm(p[:, :N], w[:, kc, wc0:wc0 + 128], src[:, kc, t0:t0 + N], kc == 0, kc == kc_n - 1, r=[w, rb, extra_r], w=p)

    def fm_params(self, dst, rows):
        k = self.k
        tot = sum(r.shape[0] for r in rows)
        st = k.sb("fmst", [tot, 128], F32)
        pairs = []
        o = 0
        for r in rows:
            pairs.append((st[o:o + r.shape[0], :], r))
            o += r.shape[0]
        k.dma_group(pairs, key=st, w=st)
        p = self.psum()
        k.tr(p[:, 0:tot], st[0:tot, :], self.ident[0:tot, 0:tot], r=[st, self.ident], w=p)
        k.cp("dve", dst[:, 0:tot], p[:, 0:tot], r=p, w=dst)

    def bcast_load(self, dst, row_ap, n):
        self.k.dma(dst[:, 0:n], row_ap.to_broadcast([128, n]), key=dst, w=dst)

    def log1p_small(self, tt, out, y, t):
        k = self.k
        k.ts("dve", t, y, -0.2, 0.25, ALU.mult, ALU.add, r=tt, w=tt)
        for cst in (1.0 / 3.0, 0.5, 1.0):
            k.tt("dve", t, t, y, ALU.mult, r=tt, w=tt)
            k.ts("dve", t, t, -1.0, cst, ALU.mult, ALU.add, r=tt, w=tt)
        k.tt("dve", out, t, y, ALU.mult, r=tt, w=tt)

    def retention(self, b, l, HM, BR, do_ctx):
        k = self.k
        lnscale = float(np.log(128.0 ** -0.5))
        with k.scope():
            QK = k.sb("QK", [128, 8, T], BF16, nb=NT)
            Vt = k.sb("Vt", [128, NT, 512], BF16, nb=NT)
            with k.scope():
                rc = k.sb("rc", [128, SEQ], F32); rs = k.sb("rs", [128, SEQ], F32)
                k.dma(rc[:], self.din["k_retC"], key=rc, w=rc)
                k.dma(rs[:], self.din["k_retS"], key=rs, w=rs)
                wp = k.rot("wqk", [128, 8, 128], BF16, 2)
                t1s = k.rot("rt1", [128, 512], F32, 2); t2s = k.rot("rt2", [128, 512], F32, 2)
                qas = k.rot("rqa", [128, 512], BF16, 2)
                perm = k.sb("perm", [128, 128], BF16)
                k.dma(perm[:], self.din["k_perm"], key=perm, w=perm)
                for qi in range(8):
                    base = (O_RQ if qi < 4 else O_RK) + (qi % 4) * 128
                    w = self.wload(wp, "in", l, 8, [(base, 128)])
                    for t0, N in BLOCKS:
                        tl = [QK.b[n] for n in tiles_of(t0, N)]
                        pa = self.psum()
                        self.proj(pa, N, w, 0, HM, t0, 8)
                        if t0 < CTX:
                            k.cp("act", QK[:, qi, t0:t0 + N], pa[:, :N], r=pa, w=tl)
                        else:
                            qa = qas.next()
                            k.cp("act", qa[:, :N], pa[:, :N], r=pa, w=qa)
                            pb = self.psum()
                            k.mm(pb[:, :N], perm[:], qa[:, :N], True, True, r=[perm, qa], w=pb)
                            s0 = t0 - CTX
                            t1 = t1s.next(); t2 = t2s.next()
                            k.tt("dve", t1[:, :N], pa[:, :N], rc[:, s0:s0 + N], ALU.mult, r=[pa, rc], w=t1)
                            k.tt("dve", t2[:, :N], pb[:, :N], rs[:, s0:s0 + N], ALU.mult, r=[pb, rs], w=t2)
                            k.tt("pool", QK[:, qi, t0:t0 + N], t1[:, :N], t2[:, :N], ALU.add, r=[t1, t2], w=tl)
                wv = self.wload(k.rot("wv", [128, 8, 512], BF16, 1), "in", l, 8, [(O_RV, 512)])
                for n in range(NT):
                    p = self.psum()
                    for kc in range(8):
                        k.mm(p[:, :], HM[:, kc, n * 128:(n + 1) * 128], wv[:, kc, :], kc == 0, kc == 7, r=[wv, HM.rb(n)], w=p)
                    k.cp("act", Vt[:, n, :], p[:, :], r=p, w=Vt.b[n])
            sm = k.sb("rsm", [128, 64], F32)
            self.bcast_load(sm, self.din["ret_decay"][l:l + 1, :], 8)
            k.act(sm[:, 8:16], sm[:, 0:8], AF.Exp, r=sm, w=sm, scale=-1.0)
            self.log1p_small(sm, sm[:, 16:24], sm[:, 8:16], sm[:, 24:32])
            k.ts("dve", sm[:, 32:40], sm[:, 16:24], -1.0, None, ALU.mult, None, r=sm, w=sm)
            gp = k.sb("rgp", [128, 8], F32)
            self.fm_params(gp, [self.din["ret_gn_w"][l].rearrange("(c p) -> c p", p=128), self.din["ret_gn_b"][l].rearrange("(c p) -> c p", p=128)])
            posf = k.sb("posf", [128, T], F32); posb = k.sb("posb", [128, T], F32)
            ptm = k.sb("ptm", [128, 2, NT], F32)
            k.dma(posf[:], self.din["k_posf"], key=posf, w=posf)
            k.dma(posb[:], self.din["k_posb"], key=posb, w=posb)
            k.dma(ptm[:, 0, :], self.din["k_posf_tm"], key=ptm, w=ptm)
            k.dma(ptm[:, 1, :], self.din["k_posb_tm"], key=ptm, w=ptm)
            mf = k.sb("mfm", [128, 128], F32); mb = k.sb("mbm", [128, 128], F32)
            k.dma(mf[:], self.din["k_mf"], key=mf, w=mf)
            k.dma(mb[:], self.din["k_mb"], key=mb, w=mb)
            Qf = k.sb("Qf", [128, T], F32); Qb = k.sb("Qb", [128, T], F32)
            bias = k.sb("rbias", [128, 2, NT], F32)
            Ed = k.sb("Ed", [128, 128], F32)
            Es = k.rot("E", [128, 512], F32, 3); E2s = k.rot("E2", [128, 512], F32, 2)
            Ws = k.rot("W", [128, 512], BF16, 3)
            wg = k.rot("wg", [128, 8, 128], BF16, 2)
            fin = [k.rot(f"rf{i}", [128, 512], F32, 2) for i in range(3)]
            ysbs = k.rot("ysb", [128, 512], BF16, 2); ysqbs = k.rot("ysqb", [128, 512], BF16, 2)
            SG = k.sb("SG", [128, T], F32)
            accb = Rot(self.ps[0:2]); rotb = Rot(self.ps[2:8])
            for h in range(4):
                lgf = sm[:, 32 + h:33 + h]; lgb = sm[:, 36 + h:37 + h]
                k.act(Qf[:], posf[:], AF.Copy, r=[posf, sm], w=Qf, scale=lgf)
                k.act(Qb[:], posb[:], AF.Copy, r=[posb, sm], w=Qb, scale=lgb)
                k.ts("dve", bias[:, 0, :], ptm[:, 0, :], lgf, -1.0, ALU.mult, ALU.mult, r=[ptm, sm], w=bias)
                k.ts("dve", bias[:, 1, :], ptm[:, 1, :], lgb, -1.0, ALU.mult, ALU.mult, r=[ptm, sm], w=bias)
                k.ts("dve", bias[:, :, :], bias[:, :, :], lnscale, None, ALU.add, None, r=bias, w=bias)
                e1 = Es.next(); e2 = E2s.next()
                k.tt("dve", e1[:, 0:128], Qf[:, 256:384], mf[:], ALU.add, r=[Qf, mf], w=e1)
                k.act(e1[:, 0:128], e1[:, 0:128], AF.Exp, r=[e1, bias], w=e1, bias=bias[:, 0, 2:3])
                k.tt("dve", e2[:, 0:128], Qb[:, 256:384], mb[:], ALU.add, r=[Qb, mb], w=e2)
                k.act(e2[:, 0:128], e2[:, 0:128], AF.Exp, r=[e2, bias], w=e2, bias=bias[:, 1, 2:3])
                k.tt("pool", Ed[:], e1[:, 0:128], e2[:, 0:128], ALU.add, r=[e1, e2], w=Ed)
                wgt = self.wload(wg, "in", l, 8, [(O_RG + h * 128, 128)])
                for t0, N in BLOCKS:
                    if t0 < CTX and not do_ctx:
                        continue
                    G = rotb.next()
                    self.proj(G, N, wgt, 0, HM, t0, 8)
                    k.act(SG[:, t0:t0 + N], G[:, :N], AF.Silu, r=G, w=SG)
                k.preload_lnexp()
                pending = None
                for t0, N in BLOCKS:
                    if t0 < CTX and not do_ctx:
                        continue
                    I0 = t0 // 128
                    nI = N // 128
                    keys = [0, 1] if t0 < CTX else list(range(NT))
                    Y = accb.next()
                    def mkA(J):
                        def A():
                            S = rotb.next()
                            k.mm(S[:, :N], QK[:, 4 + h, J * 128:(J + 1) * 128], QK[:, h, t0:t0 + N], True, True,
                                 r=[QK.b[J]] + [QK.b[n] for n in tiles_of(t0, N)], w=S)
                            return S
                        return A

                    def mkB(ji, J):
                        def B(S):
                            E = Es.next()
                            if J < 2 and t0 >= CTX:
                                E2 = E2s.next()
                                k.act(E[:, :N], Qf[:, t0:t0 + N], AF.Exp, r=[Qf, bias], w=E, bias=bias[:, 0, J:J + 1])
                                k.act(E2[:, :N], Qb[:, t0:t0 + N], AF.Exp, r=[Qb, bias], w=E2, bias=bias[:, 1, J:J + 1])
                                k.tt("pool", E[:, :N], E[:, :N], E2[:, :N], ALU.add, r=[E, E2], w=E)
                            else:
                                d = J - I0
                                lo = min(max(d * 128, 0), N); hi = min(max((d + 1) * 128, 0), N)
                                if lo > 0:
                                    k.act(E[:, 0:lo], Qb[:, t0:t0 + lo], AF.Exp, r=[Qb, bias], w=E, bias=bias[:, 1, J:J + 1])
                                if hi < N:
                                    k.act(E[:, hi:N], Qf[:, t0 + hi:t0 + N], AF.Exp, r=[Qf, bias], w=E, bias=bias[:, 0, J:J + 1])
                                if lo < hi:
                                    k.cp("pool", E[:, lo:hi], Ed[:], r=Ed, w=E)
                            W = Ws.next()
                            k.tt("dve", W[:, :N], S[:, :N], E[:, :N], ALU.mult, r=[S, E], w=W)
                            k.mm(Y[:, :N], Vt[:, J, h * 128:(h + 1) * 128], W[:, :N], ji == 0, ji == len(keys) - 1, r=[Vt.b[J], W], w=Y)
                        return B

                    steps = [(mkA(J), mkB(ji, J)) for ji, J in enumerate(keys)]
                    if pending is not None:
                        if len(steps) > 3:
                            a3, b3 = steps[2]
                            steps[2] = (a3, (lambda S_, b3=b3, fp=pending: (b3(S_), fp())))
                        else:
                            pending()
                        pending = None
                    run_pipe(steps, 2)

                    def mkfin(Y=Y, N=N, t0=t0, h=h):
                        def finish():
                            mean, t3, t4 = [f.next() for f in fin]
                            ysb = ysbs.next(); ysqb = ysqbs.next()
                            k.cp("act", ysb[:, :N], Y[:, :N], r=Y, w=ysb)
                            k.act(ysqb[:, :N], Y[:, :N], AF.Square, r=Y, w=ysqb)
                            P1 = rotb.next(); P2 = rotb.next()
                            k.mm(P1[:, :N], self.onesb[:], ysb[:, :N], True, True, r=[self.onesb, ysb], w=P1)
                            k.mm(P2[:, :N], self.onesb[:], ysqb[:, :N], True, True, r=[self.onesb, ysqb], w=P2)
                            k.act(mean[:, :N], P1[:, :N], AF.Copy, r=P1, w=mean, scale=1.0 / 128)
                            k.act(t3[:, :N], P1[:, :N], AF.Square, r=P1, w=t3, scale=1.0 / 128)
                            k.stt(t3[:, :N], P2[:, :N], 1.0 / 128, t3[:, :N], ALU.mult, ALU.subtract, r=[P2, t3], w=t3)
                            k.act(t3[:, :N], t3[:, :N], AF.Ln, r=t3, w=t3, bias=1e-6)
                            k.act(t3[:, :N], t3[:, :N], AF.Exp, r=t3, w=t3, scale=-0.5)
                            k.tt("dve", t4[:, :N], Y[:, :N], mean[:, :N], ALU.subtract, r=[Y, mean], w=t4)
                            k.tt("dve", t4[:, :N], t4[:, :N], t3[:, :N], ALU.mult, r=[t4, t3], w=t4)
                            k.ts("dve", t4[:, :N], t4[:, :N], gp[:, h:h + 1], gp[:, 4 + h:5 + h], ALU.mult, ALU.add, r=[t4, gp], w=t4)
                            k.tt("pool", BR[:, h, t0:t0 + N], SG[:, t0:t0 + N], t4[:, :N], ALU.mult, r=[SG, t4], w=[BR.b[n] for n in tiles_of(t0, N)])
                        return finish

                    pending = mkfin()
                if pending is not None:
                    pending()
                    pending = None

    def rms_block(self, p_list, N, nch, dim, normw, out_bf, cbuf, sqbuf, rbuf, rotb, nw_tt):
        k = self.k
        for c in range(nch):
            k.cp("act", cbuf[:, c, :N], p_list[c][:, :N], r=p_list[c], w=cbuf)
            k.tt("pool", sqbuf[:, c, :N], cbuf[:, c, :N], cbuf[:, c, :N], ALU.mult, r=cbuf, w=sqbuf)
        ss = rotb.next()
        for c in range(nch):
            k.mm(ss[:, :N], self.ones[:], sqbuf[:, c, :N], c == 0, c == nch - 1, r=[self.ones, sqbuf], w=ss)
        k.act(rbuf[:, :N], ss[:, :N], AF.Ln, r=ss, w=rbuf, scale=1.0 / dim, bias=1e-6)
        k.act(rbuf[:, :N], rbuf[:, :N], AF.Exp, r=rbuf, w=rbuf, scale=-0.5)
        for c in range(nch):
            k.tt("pool", cbuf[:, c, :N], cbuf[:, c, :N], rbuf[:, :N], ALU.mult, r=[cbuf, rbuf], w=cbuf)
            k.ts("dve", out_bf[:, c, :N], cbuf[:, c, :N], normw[:, c:c + 1], None, ALU.mult, None, r=[cbuf, nw_tt], w=out_bf)

    def rope_combine(self, dst_ap, dst_bufs, pa, pb, N, cT, sT, s0, t1, t2):
        k = self.k
        k.tt("dve", t1[:, :N], pa[:, :N], cT[:, s0:s0 + N], ALU.mult, r=[pa, cT], w=t1)
        k.tt("dve", t2[:, :N], pb[:, :N], sT[:, s0:s0 + N], ALU.mult, r=[pb, sT], w=t2)
        k.tt("pool", dst_ap, t1[:, :N], t2[:, :N], ALU.add, r=[t1, t2], w=dst_bufs)

    def mla(self, b, l, HM, BR, do_ctx):
        k = self.k
        scale = 192.0 ** -0.5
        with k.scope():
            KN = k.sb("KN", [128, 4, T], BF16, nb=NT)
            KR = k.sb("KR", [128, T], BF16, nb=NT)
            Vt = k.sb("Vt", [128, NT, 512], BF16, nb=NT)
            mc = k.sb("mc", [128, SEQ], F32); ms = k.sb("ms", [128, SEQ], F32)
            k.dma(mc[:], self.din["k_mlaC"], key=mc, w=mc)
            k.dma(ms[:], self.din["k_mlaS"], key=ms, w=ms)
            nw = k.sb("mnw", [128, 8], F32)
            self.fm_params(nw, [self.din["mla_q_norm"][l].rearrange("(c p) -> c p", p=128), self.din["mla_kv_norm"][l].rearrange("(c p) -> c p", p=128)])
            cb = k.sb("mcb", [128, 3, 512], F32); sq = k.sb("msq", [128, 3, 512], F32); rb = k.sb("mrb", [128, 512], F32)
            t1s = k.rot("mt1", [128, 512], F32, 2); t2s = k.rot("mt2", [128, 512], F32, 2)
            rotb = Rot(self.ps[4:8])
            k.preload_lnexp()
            with k.scope():
                one = lambda nm, shp: k.rot(nm, shp, BF16, 1)
                wckv = self.wload(one("wckv", [128, 8, 256]), "in", l, 8, [(O_CKV, 256)])
                wkr = self.wload(one("wkr", [128, 8, 256]), "in", l, 8, [(O_KR, 64), (O_KR, 64), (O_KR + 32, 32), (O_KR, 32), (O_KR + 32, 32), (O_KR, 32)])
                wk = self.wload(one("wukvk", [128, 2, 512]), "ukv", l, 2, [(256 * h, 128) for h in range(4)])
                wv = self.wload(one("wukvv", [128, 2, 512]), "ukv", l, 2, [(256 * h + 128, 128) for h in range(4)])
                ckvn = k.rot("ckvn", [128, 2, 512], BF16, 2)
                for t0, N in BLOCKS:
                    tl = tiles_of(t0, N)
                    ps_ = [rotb.next() for _ in range(2)]
                    for c in range(2):
                        self.proj(ps_[c], N, wckv, c * 128, HM, t0, 8)
                    cn = ckvn.next()
                    self.rms_block(ps_, N, 2, 256.0, nw[:, 3:5], cn, cb, sq, rb, rotb, nw)
                    for h in range(4):
                        p = rotb.next()
                        for kc in range(2):
                            k.mm(p[:, :N], wk[:, kc, h * 128:(h + 1) * 128], cn[:, kc, :N], kc == 0, kc == 1, r=[wk, cn], w=p)
                        k.cp("act", KN[:, h, t0:t0 + N], p[:, :N], r=p, w=[KN.b[n] for n in tl])
                    for n in tl:
                        p = rotb.next()
                        o = n * 128 - t0
                        for kc in range(2):
                            k.mm(p[:, :], cn[:, kc, o:o + 128], wv[:, kc, :], kc == 0, kc == 1, r=[wv, cn], w=p)
                        k.cp("act", Vt[:, n, :], p[:, :], r=p, w=Vt.b[n])
                    pa = rotb.next()
                    self.proj(pa, N, wkr, 0, HM, t0, 8)
                    if t0 < CTX:
                        k.cp("act", KR[:, t0:t0 + N], pa[:, :N], r=pa, w=[KR.b[n] for n in tl])
                    else:
                        pb = rotb.next()
                        self.proj(pb, N, wkr, 128, HM, t0, 8)
                        self.rope_combine(KR[:, t0:t0 + N], [KR.b[n] for n in tl], pa, pb, N, mc, ms, t0 - CTX, t1s.next(), t2s.next())
            one = lambda nm, shp: k.rot(nm, shp, BF16, 1)
            wcq = self.wload(one("wcq", [128, 8, 384]), "in", l, 8, [(O_CQ, 384)])
            wqn = self.wload(one("wuqn", [128, 3, 512]), "uq", l, 3, [(192 * h, 128) for h in range(4)])
            segs = [(192 * h + 128, 64) for h in range(4)]
            for h in range(4):
                segs += [(192 * h + 128 + 32, 32), (192 * h + 128, 32)]
            wqr = self.wload(one("wuqr", [128, 3, 512]), "uq", l, 3, segs)
            cqn = k.sb("cqn", [128, 3, 512], BF16)
            QN = k.sb("QN", [128, 4, 512], BF16); QR = k.sb("QR", [128, 2, 512], BF16)
            Ps = k.rot("P", [128, 512], BF16, 3)
            rinv = k.rot("rinv", [128, 512], F32, 2)
            paccs = k.rot("pacc", [128, 512], F32, 2)
            accb = Rot(self.ps[0:4])
            for t0, N in BLOCKS:
                if t0 < CTX and not do_ctx:
                    continue
                ps_ = [rotb.next() for _ in range(3)]
                for c in range(3):
                    self.proj(ps_[c], N, wcq, c * 128, HM, t0, 8)
                self.rms_block(ps_, N, 3, 384.0, nw[:, 0:3], cqn, cb, sq, rb, rotb, nw)
                for h in range(4):
                    p = rotb.next()
                    for kc in range(3):
                        k.mm(p[:, :N], wqn[:, kc, h * 128:(h + 1) * 128], cqn[:, kc, :N], kc == 0, kc == 2, r=[wqn, cqn], w=p)
                    k.cp("act", QN[:, h, :N], p[:, :N], r=p, w=QN)
                for a in range(2):
                    pa = rotb.next()
                    for kc in range(3):
                        k.mm(pa[:, :N], wqr[:, kc, a * 128:(a + 1) * 128], cqn[:, kc, :N], kc == 0, kc == 2, r=[wqr, cqn], w=pa)
                    if t0 < CTX:
                        k.cp("act", QR[:, a, :N], pa[:, :N], r=pa, w=QR)
                    else:
                        pb = rotb.next()
                        for kc in range(3):
                            k.mm(pb[:, :N], wqr[:, kc, 256 + a * 128:256 + (a + 1) * 128], cqn[:, kc, :N], kc == 0, kc == 2, r=[wqr, cqn], w=pb)
                        self.rope_combine(QR[:, a, :N], QR, pa, pb, N, mc, ms, t0 - CTX, t1s.next(), t2s.next())
                keys = [0, 1] if t0 < CTX else list(range(NT))
                pending = None
                for h in range(4):
                    O = accb.next(); R = accb.next()
                    hp = 64 * (h % 2)
                    def mkA(J, h=h, hp=hp):
                        def A():
                            S = rotb.next()
                            k.mm(S[:, :N], KN[:, h, J * 128:(J + 1) * 128], QN[:, h, :N], True, False, r=[KN.b[J], QN], w=S)
                            k.mm(S[:, :N], KR[hp:hp + 64, J * 128:(J + 1) * 128], QR[hp:hp + 64, h // 2, :N], False, True, r=[KR.b[J], QR], w=S)
                            return S
                        return A

                    Pacc = paccs.next()

                    def mkB(ji, J, h=h, O=O, R=R, Pacc=Pacc):
                        def B(S):
                            P = Ps.next()
                            k.act(P[:, :N], S[:, :N], AF.Exp, r=S, w=P, scale=scale)
                            k.mm(O[:, :N], Vt[:, J, h * 128:(h + 1) * 128], P[:, :N], ji == 0, ji == len(keys) - 1, r=[Vt.b[J], P], w=O)
                            k.mm(R[:, :N], self.onesb[:], P[:, :N], ji == 0, ji == len(keys) - 1, r=[self.onesb, P], w=R)
                        return B

                    steps = [(mkA(J), mkB(ji, J)) for ji, J in enumerate(keys)]
                    if pending is not None:
                        if len(steps) > 3:
                            a3, b3 = steps[2]
                            steps[2] = (a3, (lambda S_, b3=b3, fp=pending: (b3(S_), fp())))
                        else:
                            pending()
                        pending = None
                    run_pipe(steps, 2)

                    def mkfin(O=O, R=R, N=N, t0=t0, h=h):
                        def finish():
                            ri = rinv.next()
                            k.act(ri[:, :N], R[:, :N], AF.Ln, r=R, w=ri)
                            k.act(ri[:, :N], ri[:, :N], AF.Exp, r=ri, w=ri, scale=-1.0)
                            k.tt("dve", BR[:, h, t0:t0 + N], O[:, :N], ri[:, :N], ALU.mult, r=[O, ri], w=[BR.b[n] for n in tiles_of(t0, N)])
                        return finish

                    pending = mkfin()
                if pending is not None:
                    pending()
                    pending = None

    def conv4(self, u, U0, pr, wcol, bcol, eng="dve"):
        k = self.k
        k.ts(eng, u[:, :], U0[:, :], wcol(1), bcol, ALU.mult, ALU.add, r=[U0, pr], w=u)
        for s_, e_ in ((0, CTX), (CTX, T)):
            k.stt(u[:, s_ + 1:e_], U0[:, s_:e_ - 1], wcol(0), u[:, s_ + 1:e_], ALU.mult, ALU.add, r=[U0, u, pr], w=u)
            k.stt(u[:, s_:e_ - 1], U0[:, s_ + 1:e_], wcol(2), u[:, s_:e_ - 1], ALU.mult, ALU.add, r=[U0, u, pr], w=u)
            k.stt(u[:, s_:e_ - 2], U0[:, s_ + 2:e_], wcol(3), u[:, s_:e_ - 2], ALU.mult, ALU.add, r=[U0, u, pr], w=u)

    def lru(self, b, l, HM, BR, do_ctx):
        k = self.k
        with k.scope():
            pr = k.sb("lpr", [128, 48], F32)
            self.fm_params(pr, [self.din["lru_conv_w"][l].rearrange("k (c p) -> (k c) p", p=128),
                                self.din["lru_conv_b"][l].rearrange("(c p) -> c p", p=128),
                                self.din["lru_gate_b"][l].rearrange("d g (c p) -> (d g c) p", p=128),
                                self.din["lru_lambda"][l].rearrange("d (c p) -> (d c) p", p=128)])
            sm = k.sb("lsm", [128, 32], F32)
            k.act(sm[:, 0:8], pr[:, 36:44], AF.Exp, r=pr, w=sm, scale=-1.0)
            self.log1p_small(sm, sm[:, 8:16], sm[:, 0:8], sm[:, 16:24])
            k.ts("dve", sm[:, 24:32], sm[:, 8:16], -8.0, None, ALU.mult, None, r=sm, w=sm)
            GW = k.sb("GW", [128, 16, 128], BF16)
            with k.scope():
                stg = k.sb("gwst", [128, 16, 128], F32)
                k.op("pool", lambda e: e.memset(stg[:], 0.0), w=stg)
                pairs = []
                for d in range(2):
                    for g in range(2):
                        for cc in range(4):
                            for j in range(2):
                                pairs.append((stg[64 * j:64 * j + 64, (d * 2 + g) * 4 + cc, 64 * j:64 * j + 64], self.din["lru_gate_w"][l, d, g, 2 * cc + j]))
                k.dma_group(pairs, key=stg, w=stg)
                k.cp("dve", GW[:], stg[:], r=stg, w=GW)
            wp = k.rot("lw", [128, 8, 256], BF16, 2)
            U0 = k.sb("lU0", [128, T], F32); u = k.sb("lu", [128, T], F32)
            aa = [k.sb(f"la{d}", [128, T], F32) for d in range(2)]
            iis = [k.sb(f"li{d}", [128, T], F32) for d in range(2)]
            hh = [k.sb(f"lh{d}", [128, T], F32) for d in range(2)]
            ub = k.sb("lub", [128, T], BF16)
            for cc in range(4):
                w = self.wload(wp, "in", l, 8, [(O_LX + cc * 128, 128), (O_LG + cc * 128, 128)])
                for t0, N in BLOCKS:
                    p = self.psum()
                    self.proj(p, N, w, 0, HM, t0, 8)
                    k.cp("act", U0[:, t0:t0 + N], p[:, :N], r=p, w=U0)
                self.conv4(u, U0, pr, lambda kk: pr[:, kk * 4 + cc:kk * 4 + cc + 1], pr[:, 16 + cc:17 + cc])
                k.cp("act", ub[:], u[:], r=u, w=ub)
                for d in range(2):
                    a = aa[d]; ii = iis[d]; h_ = hh[d]
                    for g, dst in ((0, a), (1, ii)):
                        gi = (d * 2 + g) * 4 + cc
                        for t0, N in BLOCKS:
                            p = self.psum()
                            k.mm(p[:, :N], GW[:, gi, :], ub[:, t0:t0 + N], True, True, r=[GW, ub], w=p)
                            k.act(dst[:, t0:t0 + N], p[:, :N], AF.Sigmoid, r=[p, pr], w=dst, bias=pr[:, 20 + gi:21 + gi])
                    k.act(a[:], a[:], AF.Exp, r=[a, sm], w=a, scale=sm[:, 24 + d * 4 + cc:25 + d * 4 + cc])
                    k.act(h_[:], a[:], AF.Square, r=a, w=h_)
                    k.act(h_[:], h_[:], AF.Sqrt, r=h_, w=h_, scale=-1.0, bias=1.0)
                    k.tt("dve", ii[:], ii[:], h_[:], ALU.mult, r=[h_, ii], w=ii)
                    k.tt("dve", ii[:], ii[:], u[:], ALU.mult, r=[ii, u], w=ii)
                    if d == 0:
                        k.op("dve", lambda e: e.tensor_tensor_scan(out=h_[:], data0=a[:], data1=ii[:], initial=0.0, op0=ALU.mult, op1=ALU.add), r=[a, ii], w=h_)
                    else:
                        k.op("dve", lambda e: e.tensor_tensor_scan(out=h_[:, 0:CTX][:, ::-1], data0=a[:, 0:CTX][:, ::-1], data1=ii[:, 0:CTX][:, ::-1],
                                                                   initial=0.0, op0=ALU.mult, op1=ALU.add), r=[a, ii], w=h_)
                        k.op("dve", lambda e: e.tensor_tensor_scan(out=h_[:, CTX:T][:, ::-1], data0=a[:, CTX:T][:, ::-1], data1=ii[:, CTX:T][:, ::-1],
                                                                   initial=h_[:, 0:1], op0=ALU.mult, op1=ALU.add), r=[a, ii, h_], w=h_)
                t = iis[0]; hs = aa[0]
                for t0, N in BLOCKS:
                    p = self.psum()
                    self.proj(p, N, w, 128, HM, t0, 8)
                    k.cp("act", U0[:, t0:t0 + N], p[:, :N], r=p, w=U0)
                k.act(t[:], U0[:], AF.Square, r=U0, w=t)
                k.act(t[:], t[:], AF.Identity, r=t, w=t, scale=0.044715, bias=1.0)
                k.tt("dve", t[:], t[:], U0[:], ALU.mult, r=[t, U0], w=t)
                k.act(t[:], t[:], AF.Sigmoid, r=t, w=t, scale=1.5957691216)
                k.tt("dve", hs[:], hh[0][:], hh[1][:], ALU.add, r=[hh[0], hh[1]], w=hs)
                k.tt("dve", hs[:], hs[:], U0[:], ALU.mult, r=[hs, U0], w=hs)
                k.tt("dve", BR[:, cc, :], hs[:], t[:], ALU.mult, r=[hs, t], w=BR)

    def ssd(self, b, l, HM, BR, do_ctx):
        k = self.k
        tq0 = 0 if do_ctx else CTX
        with k.scope():
            BT = k.sb("BT", [128, 2, T], BF16, nb=NT); CT = k.sb("CT", [128, 2, T], BF16, nb=NT)
            Xt = k.sb("Xt", [128, NT, 512], BF16, nb=NT)
            DT = k.sb("DT", [128, NT, 16], F32); Vc = k.sb("Vc", [128, NT, 16], F32); bia = k.sb("bia", [128, NT, 16], F32)
            pr = k.sb("spr", [128, 48], F32)
            self.fm_params(pr, [self.din["ssd_conv_w"][l].rearrange("k (c p) -> (k c) p", p=128),
                                self.din["ssd_conv_b"][l].rearrange("(c p) -> c p", p=128),
                                self.din["ssd_norm_w"][l].rearrange("(c p) -> c p", p=128)])
            sm = k.sb("ssm", [128, 48], F32)
            self.bcast_load(sm, self.din["ssd_a_log"][l:l + 1, :], 16)
            k.act(sm[:, 0:16], sm[:, 0:16], AF.Exp, r=sm, w=sm)
            k.ts("dve", sm[:, 0:16], sm[:, 0:16], -1.0, None, ALU.mult, None, r=sm, w=sm)
            k.dma(sm[:, 16:32], self.din["ssd_dt_bias"][l:l + 1, :].to_broadcast([128, 16]), key=sm, w=sm)
            k.dma(sm[:, 32:40], self.din["ssd_d"][l:l + 1, :].to_broadcast([128, 8]), key=sm, w=sm)
            dI = k.sb("dI", [128, 8, 128], BF16)
            for h in range(8):
                k.ts("dve", dI[:, h, :], self.ident[:], sm[:, 32 + h:33 + h], None, ALU.mult, None, r=[self.ident, sm], w=dI)
            mf = k.sb("smf", [128, 128], F32); mb = k.sb("smb", [128, 128], F32)
            k.dma(mf[:], self.din["k_mf"], key=mf, w=mf)
            k.dma(mb[:], self.din["k_mb"], key=mb, w=mb)
            with k.scope():
                U0 = k.sb("sU0", [128, T], F32); u = k.sb("su", [128, T], F32); xs = k.sb("sxs", [128, T], BF16)
                wp = k.rot("sw", [128, 8, 128], BF16, 2)
                up = k.sb("sup", [128, 128], F32); lo = k.sb("slo", [128, 128], F32)
                k.dma(up[:], self.din["k_up"], key=up, w=up)
                k.dma(lo[:], self.din["k_lo"], key=lo, w=lo)
                for c in range(8):
                    w = self.wload(wp, "in", l, 8, [(O_SX + c * 128, 128)])
                    for t0, N in BLOCKS:
                        p = self.psum()
                        self.proj(p, N, w, 0, HM, t0, 8)
                        k.cp("act", U0[:, t0:t0 + N], p[:, :N], r=p, w=U0)
                    self.conv4(u, U0, pr, lambda kk: pr[:, kk * 8 + c:kk * 8 + c + 1], pr[:, 32 + c:33 + c])
                    if c < 4:
                        k.act(xs[:], u[:], AF.Silu, r=u, w=xs)
                        for n in range(NT):
                            pt = self.psum()
                            ptb = pt[:].bitcast(BF16)
                            k.tr(ptb[:, 0:128], xs[:, n * 128:(n + 1) * 128], self.identb[:], r=[xs, self.identb], w=pt)
                            k.cp("pool" if False else "dve", Xt[:, n, c * 128:(c + 1) * 128], ptb[:, 0:128], r=pt, w=Xt.b[n])
                    elif c < 6:
                        k.act(BT[:, c - 4, :], u[:], AF.Silu, r=u, w=BT)
                    else:
                        k.act(CT[:, c - 6, :], u[:], AF.Silu, r=u, w=CT)
                wdt = self.wload(k.rot("swdt", [128, 8, 16], BF16, 1), "in", l, 8, [(O_SDT, 16)])
                for n in range(NT):
                    p = self.psum()
                    for kc in range(8):
                        k.mm(p[:, 0:16], HM[:, kc, n * 128:(n + 1) * 128], wdt[:, kc, :], kc == 0, kc == 7, r=[wdt, HM.rb(n)], w=p)
                    k.tt("dve", DT[:, n, :], p[:, 0:16], sm[:, 16:32], ALU.add, r=[p, sm], w=DT)
                tA = k.sb("stA", [128, NT, 16], F32); tB = k.sb("stB", [128, NT, 16], F32)
                k.ts("dve", tA[:], DT[:], -1.0, None, ALU.mult, None, r=DT, w=tA)
                k.tt("dve", tA[:], tA[:], DT[:], ALU.min, r=[tA, DT], w=tA)
                k.act(tA[:], tA[:], AF.Exp, r=tA, w=tA)
                k.act(tA[:], tA[:], AF.Ln, r=tA, w=tA, bias=1.0)
                k.ts("dve", tB[:], DT[:], 0.0, None, ALU.max, None, r=DT, w=tB)
                k.tt("dve", DT[:], tA[:], tB[:], ALU.add, r=[tA, tB], w=DT)
                k.act(bia[:], DT[:], AF.Ln, r=DT, w=bia)
                for n in range(NT):
                    k.tt("dve", tA[:, n, :], DT[:, n, :], sm[:, 0:16], ALU.mult, r=[DT, sm], w=tA)
                for n in range(NT):
                    p = self.psum()
                    k.mm(p[:, 0:16], self.ones[:], tA[:, n, :], True, True, r=[self.ones, tA], w=p)
                    k.cp("act", tB[:, n, :], p[:, 0:16], r=p, w=tB)
                car = k.sb("scar", [128, NT, 16], F32)
                k.op("pool", lambda e: e.memset(car[:], 0.0), w=car)
                for n in range(1, NT):
                    k.tt("dve", car[:, n, 0:8], car[:, n - 1, 0:8], tB[:, n - 1, 0:8], ALU.add, r=[car, tB], w=car)
                k.cp("dve", car[:, 0, 8:16], tB[:, 1, 8:16], r=tB, w=car)
                k.tt("dve", car[:, NT - 1, 8:16], tB[:, 0, 8:16], tB[:, 1, 8:16], ALU.add, r=tB, w=car)
                for n in range(NT - 2, 1, -1):
                    k.tt("dve", car[:, n, 8:16], car[:, n + 1, 8:16], tB[:, n + 1, 8:16], ALU.add, r=[car, tB], w=car)
                for n in range(NT):
                    p = self.psum()
                    k.mm(p[:, 0:8], up[:], tA[:, n, 0:8], True, True, r=[up, tA], w=p)
                    k.mm(p[:, 8:16], lo[:], tA[:, n, 8:16], True, True, r=[lo, tA], w=p)
                    k.tt("dve", Vc[:, n, :], p[:, 0:16], car[:, n, :], ALU.add, r=[p, car], w=Vc)
                k.tt("dve", bia[:], bia[:], Vc[:], ALU.subtract, r=[bia, Vc], w=bia)
            Q = [k.sb(f"sQ{d}", [128, T], F32) for d in range(2)]
            YZ = k.sb("sYZ", [128, T], F32); SS = k.sb("sSS", [128, T], F32)
            Es = k.rot("sE", [128, 512], F32, 3); Ws = k.rot("sW", [128, 512], BF16, 3)
            tds = k.rot("std", [128, 128], F32, 2)
            szs = k.rot("ssz", [128, 512], F32, 2); sqs = k.rot("ssq", [128, 512], F32, 2)
            wzp = k.rot("swz", [128, 8, 128], BF16, 2)
            accb = Rot(self.ps[0:2]); rotb = Rot(self.ps[2:8])
            for h in range(8):
                g = h // 4
                hp = 64 * (h % 2)
                for d in range(2):
                    for n0 in range(0, NT, 4):
                        p = rotb.next()
                        nn = min(4, NT - n0)
                        for q in range(nn):
                            k.mm(p[:, q * 128:(q + 1) * 128], Vc[:, n0 + q, d * 8 + h:d * 8 + h + 1].to_broadcast([128, 128]), self.ident[:],
                                 True, True, r=[Vc, self.ident], w=p)
                        k.cp("act", Q[d][:, n0 * 128:(n0 + nn) * 128], p[:, 0:nn * 128], r=p, w=Q[d])
                k.preload_lnexp()
                for t0, N in BLOCKS:
                    if t0 < CTX and not do_ctx:
                        continue
                    I0 = t0 // 128
                    nI = N // 128
                    keys = [0, 1] if t0 < CTX else list(range(NT))
                    Y = accb.next()
                    first = True
                    for q in range(nI):
                        k.op("pe", lambda e: e.matmul(Y[hp:hp + 64, q * 128:(q + 1) * 128], lhsT=Xt[:, I0 + q, h * 64:(h + 1) * 64], rhs=dI[:, h, :],
                                                      start=first, stop=False, skip_group_check=True), r=[Xt.b[I0 + q], dI], w=Y)
                        first = False
                    def mkA(J):
                        def A():
                            CBp = rotb.next()
                            k.mm(CBp[:, :N], BT[:, g, J * 128:(J + 1) * 128], CT[:, g, t0:t0 + N], True, True, r=[BT, CT], w=CBp)
                            return CBp
                        return A

                    def mkB(J):
                        def B(CBp):
                            for d in range(2):
                                mk = mf if d == 0 else mb
                                if J < 2 and t0 >= CTX:
                                    lo_, hi_, dlo = 0, N, None
                                else:
                                    dd = J - I0
                                    if d == 0:
                                        lo_, hi_ = max(dd * 128, 0), N
                                    else:
                                        lo_, hi_ = 0, min((dd + 1) * 128, N)
                                    dlo = dd * 128 if 0 <= dd < nI else None
                                if lo_ >= hi_:
                                    continue
                                bcol = bia[:, J, d * 8 + h:d * 8 + h + 1]
                                E = Es.next(); W = Ws.next()
                                segs = [(lo_, hi_)]
                                if dlo is not None:
                                    segs = [(a_, b_) for a_, b_ in ((lo_, dlo), (dlo + 128, hi_)) if a_ < b_]
                                    td = tds.next()
                                    k.tt("pool", td[:], Q[d][:, t0 + dlo:t0 + dlo + 128], mk[:], ALU.add, r=[Q[d], mk], w=td)
                                    k.act(E[:, dlo:dlo + 128], td[:], AF.Exp, r=[td, bia], w=E, bias=bcol)
                                for a_, b_ in segs:
                                    k.act(E[:, a_:b_], Q[d][:, t0 + a_:t0 + b_], AF.Exp, r=[Q[d], bia], w=E, bias=bcol)
                                k.tt("dve", W[:, lo_:hi_], E[:, lo_:hi_], CBp[:, lo_:hi_], ALU.mult, r=[E, CBp], w=W)
                                k.op("pe", lambda e: e.matmul(Y[hp:hp + 64, lo_:hi_], lhsT=Xt[:, J, h * 64:(h + 1) * 64], rhs=W[:, lo_:hi_],
                                                              start=False, stop=False, skip_group_check=True), r=[Xt.b[J], W], w=Y)
                        return B

                    run_pipe([(mkA(J), mkB(J)) for J in keys], 2)
                    k.cp("act", YZ[hp:hp + 64, t0:t0 + N], Y[hp:hp + 64, :N], r=Y, w=YZ)
                if h % 2 == 1:
                    c = h // 2
                    wz = self.wload(wzp, "in", l, 8, [(O_SZ + c * 128, 128)])
                    for t0, N in BLOCKS:
                        if t0 < CTX and not do_ctx:
                            continue
                        p = rotb.next()
                        self.proj(p, N, wz, 0, HM, t0, 8)
                        sz = szs.next(); sq = sqs.next()
                        k.act(sz[:, :N], p[:, :N], AF.Silu, r=p, w=sz)
                        k.tt("pool", YZ[:, t0:t0 + N], YZ[:, t0:t0 + N], sz[:, :N], ALU.mult, r=[YZ, sz], w=YZ)
                        k.tt("pool", sq[:, :N], YZ[:, t0:t0 + N], YZ[:, t0:t0 + N], ALU.mult, r=YZ, w=sq)
                        p2 = rotb.next()
                        k.mm(p2[:, :N], self.ones[:], sq[:, :N], True, True, r=[self.ones, sq], w=p2)
                        if c == 0:
                            k.cp("act", SS[:, t0:t0 + N], p2[:, :N], r=p2, w=SS)
                        else:
                            k.tt("dve", SS[:, t0:t0 + N], SS[:, t0:t0 + N], p2[:, :N], ALU.add, r=[SS, p2], w=SS)
                        k.cp("pool", BR[:, c, t0:t0 + N], YZ[:, t0:t0 + N], r=YZ, w=[BR.b[n] for n in tiles_of(t0, N)])
            k.preload_lnexp()
            k.act(SS[:, tq0:T], SS[:, tq0:T], AF.Ln, r=SS, w=SS, scale=1.0 / 512, bias=1e-6)
            k.act(SS[:, tq0:T], SS[:, tq0:T], AF.Exp, r=SS, w=SS, scale=-0.5)
            for c in range(4):
                k.stt(BR[:, c, tq0:T], BR[:, c, tq0:T], pr[:, 40 + c:41 + c], SS[:, tq0:T], ALU.mult, ALU.mult, r=[BR, pr, SS], w=BR)

    def postnorm_tiles(self, b, l, tiles, mm_fn, src_which, mod_i, lnw, lnb, dst_fn, depth=2, la=None):
        k = self.k
        with k.scope():
            bc = k.sb("bc", [128, 4, D], F32)
            k.dma(bc[:, 0, :], self.MOD[l, b:b + 1, mod_i * D:(mod_i + 1) * D].to_broadcast([128, D]), key=bc, w=bc)
            k.dma(bc[:, 1, :], self.MOD[l, 4:5, mod_i * D:(mod_i + 1) * D].to_broadcast([128, D]), key=bc, w=bc)
            k.dma(bc[:, 2, :], self.din[lnw][l:l + 1, :].to_broadcast([128, D]), key=bc, w=bc)
            k.dma(bc[:, 3, :], self.din[lnb][l:l + 1, :].to_broadcast([128, D]), key=bc, w=bc)
            LA = la or (depth + 1)
            hts = k.rot("pht", [128, D], F32, LA + 1); t1s = k.rot("pt1", [128, D], F32, depth + 2)
            sts = k.rot("pst", [128, 16], F32, 4)
            tl = list(tiles)
            loads = {}

            def L(i):
                if i < len(tl):
                    ht = hts.next()
                    k.dma(ht[:], self.hsrc(b, l, tl[i], src_which), key=ht, w=ht)
                    loads[i] = ht

            for i in range(LA):
                L(i)

            def mkA(i, n):
                def A():
                    L(i + LA)
                    j = 1 if n < 2 else 0
                    o = [self.psum(), self.psum()]
                    mm_fn(n, o)
                    ht = loads.pop(i); t1 = t1s.next(); st = sts.next()
                    for half in range(2):
                        k.tt("dve", t1[:, half * 512:(half + 1) * 512], o[half][:, :], bc[:, j, half * 512:(half + 1) * 512], ALU.mult, r=[o[half], bc], w=t1)
                    k.stt(t1[:], ht[:], ALPHA, t1[:], ALU.mult, ALU.add, r=[ht, t1], w=t1)
                    self.ln_stats(t1, st)
                    return t1, st
                return A

            def mkB(n):
                def B(state):
                    t1, st = state
                    self.ln_apply(t1, t1, st)
                    k.tt("pool", t1[:], t1[:], bc[:, 2, :], ALU.mult, r=[t1, bc], w=t1)
                    k.tt("pool", t1[:], t1[:], bc[:, 3, :], ALU.add, r=[t1, bc], w=t1)
                    k.dma(dst_fn(n), t1[:], key=t1, r=t1)
                return B

            run_pipe([(mkA(i, n), mkB(n)) for i, n in enumerate(tl)], depth)

    def merge(self, b, l, HM, BR, do_ctx):
        k = self.k
        blocks = [bl for bl in BLOCKS if do_ctx or bl[0] >= CTX]
        tiles = list(range(0 if do_ctx else 2, NT))
        with k.scope():
            ACC = k.sb("ACC", [128, 8, T], BF16, nb=NT)
            with k.scope():
                wgp = k.rot("wgate", [128, 8, 512], BF16, 2)
                wbp = k.rot("wbr", [128, 4, 512], BF16, 2)
                sgs = k.rot("msg", [128, 512], F32, 3); accs = k.rot("macc", [128, 512], F32, 2); tms = k.rot("mtm", [128, 512], F32, 2)
                rotb = Rot(self.ps)
                for oc in range(8):
                    wg = self.wload(wgp, "in", l, 8, [(1024 * i + 128 * oc, 128) for i in range(4)])
                    wb = wbp.next()
                    k.dma_group([(wb[:, 0:4, i * 128:(i + 1) * 128],
                                  self.WB["br"][l][512 * i:512 * (i + 1), oc * 128:(oc + 1) * 128].rearrange("(kc p) n -> p kc n", p=128)) for i in range(4)],
                                key=wb, w=wb)
                    for t0, N in blocks:
                        acc = accs.next()
                        tl = [ACC.b[n] for n in tiles_of(t0, N)]
                        for i in range(4):
                            G = rotb.next()
                            self.proj(G, N, wg, i * 128, HM, t0, 8)
                            Pj = rotb.next()
                            self.proj(Pj, N, wb, i * 128, BR[i], t0, 4)
                            sg = sgs.next()
                            k.act(sg[:, :N], G[:, :N], AF.Sigmoid, r=G, w=sg)
                            if i == 0:
                                k.tt("dve", acc[:, :N], Pj[:, :N], sg[:, :N], ALU.mult, r=[Pj, sg], w=acc)
                            else:
                                tm = tms.next()
                                k.tt("dve", tm[:, :N], Pj[:, :N], sg[:, :N], ALU.mult, r=[Pj, sg], w=tm)
                                if i < 3:
                                    k.tt("pool", acc[:, :N], acc[:, :N], tm[:, :N], ALU.add, r=[acc, tm], w=acc)
                                else:
                                    k.tt("pool", ACC[:, oc, t0:t0 + N], acc[:, :N], tm[:, :N], ALU.add, r=[acc, tm], w=tl)
            if self.dbg.get("stop") == "acc":
                self.dump("acc", ACC[:], [128, 8, T], BF16, r=ACC)
                return
            with k.scope():
                wo = self.wload(k.rot("wo", [128, 8, 1024], BF16, 1), "out", l, 8, [(0, 1024)])

                def mmf(n, o):
                    for half in range(2):
                        for kc in range(8):
                            k.mm(o[half][:, :], ACC[:, kc, n * 128:(n + 1) * 128], wo[:, kc, half * 512:(half + 1) * 512], kc == 0, kc == 7,
                                 r=[ACC.b[n], wo], w=o[half])

                self.postnorm_tiles(b, l, tiles, mmf, 0, 2, "ln1_w", "ln1_b", lambda n: self.HS[1][n * 128:(n + 1) * 128, :], depth=1)

    def ffn(self, b, l, do_ctx, is_last):
        k = self.k
        tq0 = 0 if do_ctx else CTX
        blocks = [bl for bl in BLOCKS if do_ctx or bl[0] >= CTX]
        tiles = list(range(0 if do_ctx else 2, NT))
        segs = [(CTX, T)] if not do_ctx else [(0, CTX), (CTX, T)]
        with k.scope():
            ACTT = k.sb("ACTT", [128, 22, T], BF16, nb=NT)
            with k.scope():
                HM2 = k.sb("HM2", [128, 8, T], BF16, nb=NT)
                HM2.b2 = [Buf(f"HM2x{i}") for i in range(NT)]
                self.ln_mod_phase(b, l, 1, HM2, tiles)
                prA = k.sb("fprA", [128, 88], F32); prB = k.sb("fprB", [128, 88], F32)
                self.fm_params(prA, [self.din["ffn_conv_w"][l, 0:2, :].rearrange("k (c p) -> (k c) p", p=128)])
                self.fm_params(prB, [self.din["ffn_conv_w"][l, 2, :].rearrange("(c p) -> c p", p=128), self.din["ffn_conv_b"][l].rearrange("(c p) -> c p", p=128)])
                wup = k.rot("wup", [128, 8, 256], BF16, 2)
                U = [k.sb(f"fU{i}", [128, T], F32) for i in range(2)]
                Yv = [k.sb(f"fY{i}", [128, T], F32) for i in range(2)]
                for c in range(22):
                    w = self.wload(wup, "in" if False else "up", l, 8, [(c * 128, 128), (DFF + c * 128, 128)])
                    for part in range(2):
                        cc = part * 22 + c
                        Up = U[part]; Y = Yv[part]
                        for t0, N in blocks:
                            p = self.psum()
                            self.proj(p, N, w, part * 128, HM2, t0, 8)
                            k.cp("act", Up[:, t0:t0 + N], p[:, :N], r=p, w=Up)
                        eng = "dve" if part == 0 else "pool"
                        k.ts(eng, Y[:, tq0:T], Up[:, tq0:T], prA[:, 44 + cc:45 + cc], prB[:, 44 + cc:45 + cc], ALU.mult, ALU.add, r=[Up, prA, prB], w=Y)
                        for s_, e_ in segs:
                            k.stt(Y[:, s_ + 1:e_], Up[:, s_:e_ - 1], prA[:, cc:cc + 1], Y[:, s_ + 1:e_], ALU.mult, ALU.add, r=[Up, Y, prA], w=Y)
                            k.stt(Y[:, s_:e_ - 1], Up[:, s_ + 1:e_], prB[:, cc:cc + 1], Y[:, s_:e_ - 1], ALU.mult, ALU.add, r=[Up, Y, prB], w=Y)
                    k.act(Yv[0][:, tq0:T], Yv[0][:, tq0:T], AF.Silu, r=Yv[0], w=Yv[0])
                    k.tt("pool", ACTT[:, c, tq0:T], Yv[0][:, tq0:T], Yv[1][:, tq0:T], ALU.mult, r=[Yv[0], Yv[1]], w=ACTT)
            with k.scope():
                wd = k.sb("wd", [128, 22, D], BF16)
                srcw = self.WB["dn"][l].rearrange("(kc p) n -> p kc n", p=128)
                k.dma_group([(wd[:, 0:8, :], srcw[:, 0:8, :]), (wd[:, 8:16, :], srcw[:, 8:16, :]), (wd[:, 16:22, :], srcw[:, 16:22, :])], key=wd, w=wd)

                def mmf(n, o):
                    for half in range(2):
                        for c in range(22):
                            k.mm(o[half][:, :], ACTT[:, c, n * 128:(n + 1) * 128], wd[:, c, half * 512:(half + 1) * 512], c == 0, c == 21,
                                 r=[ACTT, wd], w=o[half])

                if is_last:
                    dst = lambda n: self.out[b, (n - 2) * 128:(n - 1) * 128, :]
                else:
                    dst = lambda n: self.HS[0][n * 128:(n + 1) * 128, :]
                self.postnorm_tiles(b, l, tiles, mmf, 1, 5, "ln2_w", "ln2_b", dst, depth=1, la=4)

    def layer(self, b, l):
        k = self.k
        last = (l == L - 1)
        with k.scope():
            HM = k.sb("HM", [128, 8, T], BF16, nb=NT)
            HM.b2 = [Buf(f"HMx{i}") for i in range(NT)]
            self.ln_mod_phase(b, l, 0, HM, range(NT))
            if self.dbg.get("stop") == "hm":
                self.dump("hm", HM[:], [128, 8, T], BF16, r=HM)
                return
            BR = [None] * 4
            BR[0] = k.sb("BR0", [128, 4, T], BF16, nb=NT)
            if not self.dbg.get("skip_ret"):
                self.retention(b, l, HM, BR[0], not last)
            if self.dbg.get("stop") == "ret":
                self.dump("ret", BR[0][:], [128, 4, T], BF16, r=BR[0])
                return
            BR[1] = k.sb("BR1", [128, 4, T], BF16, nb=NT)
            if not self.dbg.get("skip_mla"):
                self.mla(b, l, HM, BR[1], not last)
            if self.dbg.get("stop") == "mla":
                self.dump("mla", BR[1][:], [128, 4, T], BF16, r=BR[1])
                return
            BR[3] = k.sb("BR3", [128, 4, T], BF16, nb=NT)
            if not self.dbg.get("skip_ssd"):
                self.ssd(b, l, HM, BR[3], not last)
            if self.dbg.get("stop") == "ssd":
                self.dump("ssd", BR[3][:], [128, 4, T], BF16, r=BR[3])
                return
            BR[2] = k.sb("BR2", [128, 4, T], BF16, nb=NT)
            if not self.dbg.get("skip_lru"):
                self.lru(b, l, HM, BR[2], not last)
            if self.dbg.get("stop") == "lru":
                for i, nm in enumerate(("ret", "mla", "lru", "ssd")):
                    self.dump(nm, BR[i][:], [128, 4, T], BF16, r=BR[i])
                return
            self.merge(b, l, HM, BR, not last)
            if self.dbg.get("stop") == "acc":
                return
        if self.dbg.get("stop") == "h1":
            self.dump("h1", self.HS[1], [T, D], F32, r=Buf("dummy"))
            return
        self.ffn(b, l, not last, last)
        if self.dbg.get("stop") == "h2" and l == self.nl - 1:
            self.dump("h2", self.HS[0], [T, D], F32, r=Buf("dummy2"))


def host_consts():
    c = {}
    c["k_ident"] = np.eye(128, dtype=np.float32)
    c["k_ones"] = np.ones((128, 128), np.float32)
    c["k_identb"] = np.eye(128).astype(ml_dtypes.bfloat16)
    c["k_onesb"] = np.ones((128, 128)).astype(ml_dtypes.bfloat16)

    def rope(dim):
        rows = SEQ // 64
        r, col = np.meshgrid(np.arange(rows, dtype=np.float32), np.arange(64, dtype=np.float32), indexing="ij")
        quarter = dim // 4
        inv = (np.float32(10000.0) ** (-np.arange(quarter, dtype=np.float32) / np.float32(quarter))).astype(np.float32)
        ang = np.concatenate([r.reshape(-1, 1) * inv, col.reshape(-1, 1) * inv], axis=-1).astype(np.float32)
        return np.cos(ang).astype(np.float32), np.sin(ang).astype(np.float32)

    cs, sn = rope(128)
    p = np.arange(128)
    c["k_retC"] = np.ascontiguousarray(cs[:, p % 64].T)
    c["k_retS"] = np.ascontiguousarray((sn[:, p % 64] * np.where(p < 64, -1.0, 1.0)[None, :]).T.astype(np.float32))
    cs, sn = rope(64)
    c["k_mlaC"] = np.ascontiguousarray(cs[:, p % 32].T)
    c["k_mlaS"] = np.ascontiguousarray((sn[:, p % 32] * np.where((p % 64) < 32, -1.0, 1.0)[None, :]).T.astype(np.float32))
    j = np.arange(128)[:, None]; i = np.arange(128)[None, :]
    c["k_mf"] = np.where(i >= j, 0.0, NEG).astype(np.float32)
    c["k_mb"] = np.where(i <= j, 0.0, NEG).astype(np.float32)
    c["k_perm"] = (j == (i + 64) % 128).astype(np.float32).astype(ml_dtypes.bfloat16)
    c["k_up"] = (j <= i).astype(np.float32)
    c["k_lo"] = (j >= i).astype(np.float32)
    t = np.arange(T)
    posf = (t + 1).astype(np.float32)
    posb = np.where(t < CTX, CTX - t, T - (t - CTX)).astype(np.float32)
    c["k_posf"] = np.ascontiguousarray(np.broadcast_to(posf[None, :], (128, T)))
    c["k_posb"] = np.ascontiguousarray(np.broadcast_to(posb[None, :], (128, T)))
    c["k_posf_tm"] = np.ascontiguousarray(posf.reshape(NT, 128).T)
    c["k_posb_tm"] = np.ascontiguousarray(posb.reshape(NT, 128).T)
    return c


def make_in_maps(inputs, n_cores, nb):
    consts = host_consts()
    f = lambda a: np.ascontiguousarray(np.asarray(a, dtype=np.float32))
    shared = {
        "ada_w": f(inputs["ada_w"]), "ada_b": f(inputs["ada_b"]), "w_in": f(inputs["w_in"]),
        "ret_decay": f(inputs["ret_decay"]).reshape(L, 8), "ret_gn_w": f(inputs["ret_gn_w"]), "ret_gn_b": f(inputs["ret_gn_b"]),
        "mla_q_norm": f(inputs["mla_q_norm"]), "mla_w_uq": f(inputs["mla_w_uq"]), "mla_kv_norm": f(inputs["mla_kv_norm"]), "mla_w_ukv": f(inputs["mla_w_ukv"]),
        "lru_conv_w": f(inputs["lru_conv_w"]), "lru_conv_b": f(inputs["lru_conv_b"]), "lru_gate_w": f(inputs["lru_gate_w"]),
        "lru_gate_b": f(inputs["lru_gate_b"]), "lru_lambda": f(inputs["lru_lambda"]),
        "ssd_conv_w": f(inputs["ssd_conv_w"]), "ssd_conv_b": f(inputs["ssd_conv_b"]), "ssd_dt_bias": f(inputs["ssd_dt_bias"]).reshape(L, 16),
        "ssd_a_log": f(inputs["ssd_a_log"]).reshape(L, 16), "ssd_d": f(inputs["ssd_d"]), "ssd_norm_w": f(inputs["ssd_norm_w"]),
        "w_branch": f(inputs["w_branch"]).reshape(L, 2048, D), "w_out": f(inputs["w_out"]), "ln1_w": f(inputs["ln1_w"]), "ln1_b": f(inputs["ln1_b"]),
        "ffn_w_up": f(inputs["ffn_w_up"]), "ffn_conv_w": f(inputs["ffn_conv_w"]), "ffn_conv_b": f(inputs["ffn_conv_b"]), "ffn_w_down": f(inputs["ffn_w_down"]),
        "ln2_w": f(inputs["ln2_w"]), "ln2_b": f(inputs["ln2_b"]),
    }
    shared.update(consts)
    x = f(inputs["x"]); ctx = f(inputs["ctx"]); c = f(inputs["c"]); cc = f(inputs["c_ctx"])
    maps = []
    for i in range(n_cores):
        sl = slice(i * nb, (i + 1) * nb)
        c5 = np.zeros((5, D), np.float32)
        c5[:nb] = c[sl]
        c5[4] = cc
        c5T = np.ascontiguousarray(c5.reshape(5, 8, 128).transpose(2, 1, 0).reshape(128, 40))
        m = dict(shared)
        m.update({"x": np.ascontiguousarray(x[sl]), "ctx": np.ascontiguousarray(ctx[sl]), "c5T": c5T})
        maps.append(m)
    return maps


def kernel(**inputs):
    n_cores = 8
    prog = Prog(NB, L)
    nc = prog.build()
    maps = make_in_maps(inputs, n_cores, NB)
    res = run_bass_kernel_spmd(nc, maps, core_ids=list(range(n_cores)))
    out = np.concatenate([np.asarray(r["out"]) for r in res.results], axis=0)
    return out.astype(np.float32)
```

```python
import os
from contextlib import ExitStack, contextmanager
import numpy as np
import ml_dtypes
import concourse.bass as bass
import concourse.mybir as mybir
from concourse.bass_utils import run_bass_kernel_spmd

F32 = mybir.dt.float32
BF16 = mybir.dt.bfloat16
AF = mybir.ActivationFunctionType
ALU = mybir.AluOpType
AX = mybir.AxisListType

D = 1024
SEQ = 2048
CTX = 256
T = SEQ + CTX
NT = T // 128
L = 2
NB = 4
INW = 9424
DFF = 2816
ALPHA = (2 * L) ** 0.25
O_GATE, O_RQ, O_RK, O_RV, O_RG = 0, 4096, 4608, 5120, 5632
O_CQ, O_CKV, O_KR = 6144, 6528, 6784
O_LX, O_LG = 6848, 7360
O_SZ, O_SX, O_SDT = 7872, 8384, 9408
NEG = -30000.0


class Sem:
    def __init__(self, h):
        self.h = h
        self.total = 0


class Buf:
    __slots__ = ("name", "w", "r", "dsem", "psum")

    def __init__(self, name):
        self.name = name
        self.w = None
        self.r = {}
        self.dsem = None
        self.psum = False


class TT:
    def __init__(self, t, nb=1, name=""):
        self.t = t
        self.b = [Buf(f"{name}{i}") for i in range(nb)]
        self.b2 = None

    def rb(self, n):
        return [self.b[n]] if self.b2 is None else [self.b[n], self.b2[n]]

    def __getitem__(self, k):
        return self.t[k]


class Eng:
    def __init__(self, name, e, sem):
        self.name = name
        self.e = e
        self.sem = sem
        self.cnt = 0
        self.seen = {}

    def wait(self, sem, val):
        if self.seen.get(sem, 0) >= val:
            return
        self.e.wait_ge(sem.h, val)
        self.seen[sem] = val


def _flat(x):
    out = []
    if isinstance(x, (Buf, TT)):
        x = [x]
    for a in x:
        if a is None:
            continue
        if isinstance(a, Buf):
            out.append(a)
        elif isinstance(a, TT):
            out.extend(a.b)
            if a.b2 is not None:
                out.extend(a.b2)
        elif isinstance(a, (list, tuple)):
            out.extend(_flat(a))
        else:
            raise TypeError(f"bad dep object {type(a)}")
    return out


class K:
    def __init__(self, nc, es):
        self.nc = nc
        self.es = es
        self.E = {}
        for name, e in (("pe", nc.tensor), ("act", nc.scalar), ("dve", nc.vector), ("pool", nc.gpsimd), ("sp", nc.sync)):
            self.E[name] = Eng(name, e, Sem(es.enter_context(nc.semaphore("s_" + name))))
        self.dfree = [Sem(es.enter_context(nc.semaphore(f"dq{i}"))) for i in range(80)]
        self.dall = list(self.dfree)
        self.scopes = []
        self.uid = 0
        self.nins = 0

    @contextmanager
    def scope(self):
        st = ExitStack()
        rec = {"st": st, "bufs": []}
        self.scopes.append(rec)
        try:
            yield
        finally:
            self.barrier()
            for b in rec["bufs"]:
                if b.dsem is not None:
                    self.dfree.append(b.dsem)
                    b.dsem = None
            self.scopes.pop()
            st.close()

    def sb(self, name, shape, dt, nb=1):
        self.uid += 1
        st = self.scopes[-1]["st"] if self.scopes else self.es
        t = st.enter_context(self.nc.sbuf_tensor(f"{name}_{self.uid}", list(shape), dt))
        tt = TT(t, nb, name)
        if self.scopes:
            self.scopes[-1]["bufs"].extend(tt.b)
        return tt

    def rot(self, name, shape, dt, n):
        return Rot([self.sb(f"{name}{i}", shape, dt) for i in range(n)])

    def _deps(self, E, r, w):
        deps = []
        for b in r:
            if b.w is not None:
                deps.append(b.w)
            if b.psum:
                deps.extend((s_, v_) for s_, v_ in b.r.items() if s_ is not E.sem)
        for b in w:
            if b.w is not None:
                deps.append(b.w)
            deps.extend(b.r.items())
        for s, v in deps:
            if E.name == "pe" and s is E.sem:
                continue
            E.wait(s, v)

    def _record(self, ev, r, w):
        s, v = ev
        for b in r:
            if b.r.get(s, 0) < v:
                b.r[s] = v
        for b in w:
            b.w = ev
            b.r = {}

    def op(self, eng, fn, r=(), w=(), ww=()):
        E = self.E[eng]
        r = _flat(r)
        w = _flat(w)
        self._deps(E, r, w)
        ins = fn(E.e)
        E.cnt += 1
        ins.then_inc(E.sem.h, 1)
        self._record((E.sem, E.cnt), r, w)
        for b in _flat(ww):
            b.w = (E.sem, E.cnt)
        self.nins += 1
        return ins

    def dma(self, out, in_, key, r=(), w=(), **kw):
        E = self.E["sp"]
        r = _flat(r)
        w = _flat(w)
        key = _flat([key])[0]
        self._deps(E, r, w)
        if key.dsem is None:
            key.dsem = self.dfree.pop()
        s = key.dsem
        ins = E.e.dma_start(out=out, in_=in_, **kw)
        s.total += 16
        ins.then_inc(s.h, 16)
        self._record((s, s.total), r, w)
        self.nins += 1

    def dma_group(self, pairs, key, r=(), w=()):
        E = self.E["sp"]
        r = _flat(r)
        w = _flat(w)
        key = _flat([key])[0]
        self._deps(E, r, w)
        if key.dsem is None:
            key.dsem = self.dfree.pop()
        s = key.dsem
        for out, in_ in pairs:
            ins = E.e.dma_start(out=out, in_=in_)
            s.total += 16
            ins.then_inc(s.h, 16)
            self.nins += 1
        self._record((s, s.total), r, w)

    def barrier(self):
        sp = self.E["sp"]
        for n, E in self.E.items():
            if n != "sp" and E.cnt > 0:
                sp.wait(E.sem, E.cnt)
        for s in self.dall:
            if s.total > 0:
                sp.wait(s, s.total)
        sp.cnt += 1
        sp.e.sem_inc(sp.sem.h, 1)
        for n, E in self.E.items():
            if n != "sp":
                E.wait(sp.sem, sp.cnt)

    def mm(self, out, lhsT, rhs, start, stop, r, w):
        return self.op("pe", lambda e: e.matmul(out, lhsT=lhsT, rhs=rhs, start=start, stop=stop), r=r, w=w)

    def tr(self, out, in_, ident, r, w):
        return self.op("pe", lambda e: e.transpose(out, in_, ident), r=r, w=w)

    def act(self, out, in_, func, r, w, scale=1.0, bias=0.0, eng="act", ww=()):
        return self.op(eng, lambda e: e.activation(out=out, in_=in_, func=func, scale=scale, bias=bias), r=r, w=w, ww=ww)

    def tt(self, eng, out, in0, in1, op, r, w):
        return self.op(eng, lambda e: e.tensor_tensor(out=out, in0=in0, in1=in1, op=op), r=r, w=w)

    def ts(self, eng, out, in0, s1, s2, op0, op1, r, w, ww=()):
        if s2 is None:
            return self.op(eng, lambda e: e.tensor_scalar(out=out, in0=in0, scalar1=s1, scalar2=None, op0=op0), r=r, w=w, ww=ww)
        return self.op(eng, lambda e: e.tensor_scalar(out=out, in0=in0, scalar1=s1, scalar2=s2, op0=op0, op1=op1), r=r, w=w, ww=ww)

    def stt(self, out, in0, scalar, in1, op0, op1, r, w, eng="dve"):
        return self.op(eng, lambda e: e.scalar_tensor_tensor(out=out, in0=in0, scalar=scalar, in1=in1, op0=op0, op1=op1), r=r, w=w)

    def preload_lnexp(self):
        return

    def cp(self, eng, out, in_, r, w):
        if eng == "act":
            return self.op("act", lambda e: e.copy(out=out, in_=in_), r=r, w=w)
        return self.op(eng, lambda e: e.tensor_copy(out=out, in_=in_), r=r, w=w)


class Rot:
    def __init__(self, items):
        self.items = items
        self.i = 0

    def next(self):
        x = self.items[self.i % len(self.items)]
        self.i += 1
        return x


def run_pipe(steps, depth=2):
    st = {}
    n = len(steps)
    for i in range(min(depth, n)):
        st[i] = steps[i][0]()
    for i in range(n):
        if i + depth < n:
            st[i + depth] = steps[i + depth][0]()
        steps[i][1](st.pop(i))


def tiles_of(t0, n):
    return list(range(t0 // 128, (t0 + n + 127) // 128))


BLOCKS = [(0, 256)] + [(256 + 512 * m, 512) for m in range(4)]


class Prog:
    def __init__(self, nb=NB, nl=L, dbg=None):
        self.nb, self.nl, self.dbg = nb, nl, dbg or {}
        self.nc = nc = bass.Bass("TRN2", target_bir_lowering=False)
        self.es = ExitStack()
        self.k = K(nc, self.es)
        self.din = {}
        self.dbg_out = {}

        def inp(name, shape, dt=F32):
            self.din[name] = nc.dram_tensor(name, list(shape), dt, kind="ExternalInput").ap()
            return self.din[name]

        inp("x", [nb, SEQ, D]); inp("ctx", [nb, CTX, D]); inp("c5T", [128, 8 * 5])
        inp("ada_w", [L, D, 6 * D]); inp("ada_b", [L, 6 * D]); inp("w_in", [L, D, INW])
        inp("ret_decay", [L, 8]); inp("ret_gn_w", [L, 512]); inp("ret_gn_b", [L, 512])
        inp("mla_q_norm", [L, 384]); inp("mla_w_uq", [L, 384, 768]); inp("mla_kv_norm", [L, 256]); inp("mla_w_ukv", [L, 256, 1024])
        inp("lru_conv_w", [L, 4, 512]); inp("lru_conv_b", [L, 512]); inp("lru_gate_w", [L, 2, 2, 8, 64, 64])
        inp("lru_gate_b", [L, 2, 2, 512]); inp("lru_lambda", [L, 2, 512])
        inp("ssd_conv_w", [L, 4, 1024]); inp("ssd_conv_b", [L, 1024]); inp("ssd_dt_bias", [L, 16]); inp("ssd_a_log", [L, 16])
        inp("ssd_d", [L, 8]); inp("ssd_norm_w", [L, 512])
        inp("w_branch", [L, 2048, D]); inp("w_out", [L, D, D]); inp("ln1_w", [L, D]); inp("ln1_b", [L, D])
        inp("ffn_w_up", [L, D, 2 * DFF]); inp("ffn_conv_w", [L, 3, 2 * DFF]); inp("ffn_conv_b", [L, 2 * DFF]); inp("ffn_w_down", [L, DFF, D])
        inp("ln2_w", [L, D]); inp("ln2_b", [L, D])
        inp("k_ident", [128, 128]); inp("k_ones", [128, 128]); inp("k_identb", [128, 128], BF16); inp("k_onesb", [128, 128], BF16)
        inp("k_retC", [128, SEQ]); inp("k_retS", [128, SEQ]); inp("k_mlaC", [128, SEQ]); inp("k_mlaS", [128, SEQ])
        inp("k_mf", [128, 128]); inp("k_mb", [128, 128]); inp("k_up", [128, 128]); inp("k_lo", [128, 128])
        inp("k_perm", [128, 128], BF16)
        inp("k_posf", [128, T]); inp("k_posb", [128, T]); inp("k_posf_tm", [128, NT]); inp("k_posb_tm", [128, NT])
        self.out = nc.dram_tensor("out", [nb, SEQ, D], F32, kind="ExternalOutput").ap()

        def scr(name, shape, dt):
            return nc.dram_tensor(name, list(shape), dt, kind="Internal").ap()

        self.WB = {
            "in": scr("wb_in", [L, D, INW], BF16), "uq": scr("wb_uq", [L, 384, 768], BF16), "ukv": scr("wb_ukv", [L, 256, 1024], BF16),
            "br": scr("wb_br", [L, 2048, D], BF16), "out": scr("wb_out", [L, D, D], BF16), "up": scr("wb_up", [L, D, 2 * DFF], BF16),
            "dn": scr("wb_dn", [L, DFF, D], BF16),
        }
        self.MOD = scr("modscr", [L, 5, 6 * D], F32)
        self.HS = [scr("h_a", [T, D], F32), scr("h_b", [T, D], F32)]

    def dump(self, name, tt_or_ap, shape, dt, r):
        k = self.k
        o = self.nc.dram_tensor("dbg_" + name, list(shape), dt, kind="ExternalOutput").ap()
        self.dbg_out[name] = o
        key = _flat([r])[0]
        k.dma(o, tt_or_ap, key=key, r=r)

    def build(self):
        k = self.k
        nc = self.nc
        with self.es:
            self.setup_consts()
            self.stage0()
            if self.dbg.get("stop") in ("ada",):
                k.barrier()
                return nc
            for b in range(self.nb):
                for l in range(self.nl):
                    self.layer(b, l)
            k.barrier()
        return nc

    def setup_consts(self):
        k = self.k
        self.ident = k.sb("ident", [128, 128], F32)
        self.ones = k.sb("ones", [128, 128], F32)
        self.identb = k.sb("identb", [128, 128], BF16)
        self.onesb = k.sb("onesb", [128, 128], BF16)
        for tt, nm in ((self.ident, "k_ident"), (self.ones, "k_ones"), (self.identb, "k_identb"), (self.onesb, "k_onesb")):
            k.dma(tt[:], self.din[nm], key=tt, w=tt)
        self.ps = [TT(self.es.enter_context(self.nc.psum_tensor(f"ps{i}", [128, 512], F32)), 1, f"ps{i}") for i in range(8)]
        self.psi = 0
        for p in self.ps:
            p.b[0].psum = True

    def psum(self):
        p = self.ps[self.psi % 8]
        self.psi += 1
        return p

    def stage0(self):
        k = self.k
        with k.scope():
            stb = k.rot("cv_out", [128, 2048], BF16, 3)
            engs = ["dve", "pool", "act"]
            srcs = [("in", "w_in", D, INW), ("uq", "mla_w_uq", 384, 768), ("ukv", "mla_w_ukv", 256, 1024), ("br", "w_branch", 2048, D),
                    ("out", "w_out", D, D), ("up", "ffn_w_up", D, 2 * DFF), ("dn", "ffn_w_down", DFF, D)]
            conv_jobs = []
            for l in range(self.nl):
                for key, nm, R, C in srcs:
                    for r0 in range(0, R, 128):
                        for c0 in range(0, C, 2048):
                            conv_jobs.append((l, key, nm, r0, c0, min(2048, C - c0)))

            LA = 3
            stg = k.rot("cv_in2", [128, 2048], F32, LA + 2)
            loaded = {}

            def ld(i):
                if i < len(conv_jobs):
                    l, key, nm, r0, c0, cw = conv_jobs[i]
                    a = stg.next()
                    k.dma(a[:, :cw], self.din[nm][l, r0:r0 + 128, c0:c0 + cw], key=a, w=a)
                    loaded[i] = a

            def conv(i, job):
                l, key, nm, r0, c0, cw = job
                ld(i + LA)
                a = loaded.pop(i); bb = stb.next()
                k.cp(engs[i % 3], bb[:, :cw], a[:, :cw], r=a, w=bb)
                k.dma(self.WB[key][l, r0:r0 + 128, c0:c0 + cw], bb[:, :cw], key=bb, r=bb)

            c5 = k.sb("c5", [128, 40], F32)
            sc = k.sb("sc", [128, 40], F32)
            k.dma(c5[:], self.din["c5T"], key=c5, w=c5)
            k.act(sc[:], c5[:], AF.Silu, r=c5, w=sc)
            aw = k.rot("aw", [128, 8, 512], F32, 3)
            ab = k.rot("ab", [1, 512], F32, 3)
            mo = k.rot("mo", [5, 512], F32, 3)
            ada_jobs = [(l, cb) for l in range(self.nl) for cb in range(12)]
            ada_ld = {}

            def ada_load(j):
                l, cb = ada_jobs[j]
                w_ = aw.next(); b_ = ab.next()
                k.dma(w_[:], self.din["ada_w"][l, :, cb * 512:(cb + 1) * 512].rearrange("(kc p) n -> p kc n", p=128), key=w_, w=w_)
                k.dma(b_[:], self.din["ada_b"][l:l + 1, cb * 512:(cb + 1) * 512], key=b_, w=b_)
                ada_ld[j] = (w_, b_)

            def ada_compute(j):
                l, cb = ada_jobs[j]
                w_, b_ = ada_ld.pop(j)
                m_ = mo.next()
                p = self.psum()
                for kc in range(8):
                    k.mm(p[0:5, :], sc[:, kc * 5:(kc + 1) * 5], w_[:, kc, :], kc == 0, False, r=[sc, w_], w=p)
                k.mm(p[0:5, :], self.ones[0:1, 0:5], b_[:], False, True, r=[self.ones, b_], w=p)
                add1 = 1.0 if cb // 2 in (1, 4) else 0.0
                k.act(m_[:], p[0:5, :], AF.Identity, r=p, w=m_, bias=add1)
                return m_

            def ada_store(j, m_):
                l, cb = ada_jobs[j]
                k.dma(self.MOD[l, :, cb * 512:(cb + 1) * 512], m_[:], key=m_, r=m_)

            for i in range(LA):
                ld(i)
            every = max(1, len(conv_jobs) // max(1, len(ada_jobs)))
            sched = {}
            for j in range(len(ada_jobs)):
                i0 = min(j * every, len(conv_jobs) - 1)
                sched.setdefault(i0, []).append(("load", j))
                sched.setdefault(min(i0 + 3, len(conv_jobs) - 1), []).append(("compute", j))
                sched.setdefault(min(i0 + 5, len(conv_jobs) - 1), []).append(("store", j))
            computed = {}
            for i, job in enumerate(conv_jobs):
                conv(i, job)
                for kind, j in sched.get(i, []):
                    if kind == "load":
                        ada_load(j)
                    elif kind == "compute":
                        computed[j] = ada_compute(j)
                    else:
                        ada_store(j, computed.pop(j))
            assert not computed and not ada_ld and not loaded

    def hsrc(self, b, l, n, which):
        if which == 0 and l == 0:
            if n < 2:
                return self.din["ctx"][b, n * 128:(n + 1) * 128, :]
            return self.din["x"][b, (n - 2) * 128:(n - 1) * 128, :]
        return self.HS[which][n * 128:(n + 1) * 128, :]

    def load_modF(self, b, l, i0, dst):
        k = self.k
        m8 = k.sb("m8", [32, 128], F32)
        for j, row in ((0, b), (1, 4)):
            for q in range(2):
                src = self.MOD[l, row, (i0 + q) * D:(i0 + q + 1) * D].rearrange("(c p) -> c p", p=128)
                o = (j * 2 + q) * 8
                k.dma(m8[o:o + 8, :], src, key=m8, w=m8)
        p = self.psum()
        k.tr(p[:, 0:32], m8[0:32, :], self.ident[0:32, 0:32], r=[m8, self.ident], w=p)
        k.cp("dve", dst[:, 0:32], p[:, 0:32], r=p, w=dst)

    def ln_mod_phase(self, b, l, which, HM, tiles):
        k = self.k
        with k.scope():
            mf = k.sb("modF", [128, 32], F32)
            i0 = 0 if which == 0 else 3
            self.load_modF(b, l, i0, mf)
            LA = 4
            hts = k.rot("ht", [128, D], F32, LA + 3)
            xns = k.rot("xn", [128, D], F32, 2)
            sts = k.rot("st", [128, 16], F32, 4)
            tl = list(tiles)
            loads = {}

            def Ld(i):
                if i < len(tl):
                    ht = hts.next()
                    k.dma(ht[:], self.hsrc(b, l, tl[i], which), key=ht, w=ht)
                    loads[tl[i]] = ht

            for i in range(LA):
                Ld(i)

            def mkA(n):
                def A():
                    Ld(tl.index(n) + LA)
                    ht = loads.pop(n); st = sts.next()
                    self.ln_stats(ht, st)
                    return ht, st
                return A

            def mkB(n):
                def B(state):
                    ht, st = state
                    xn = xns.next()
                    self.ln_apply(ht, xn, st)
                    j = 1 if n < 2 else 0
                    for half in range(2):
                        p = self.psum()
                        for q in range(4):
                            c = half * 4 + q
                            k.tr(p[:, q * 128:(q + 1) * 128], xn[:, c * 128:(c + 1) * 128], self.ident[:], r=[xn, self.ident], w=p)
                        for q in range(4):
                            c = half * 4 + q
                            if half == 0:
                                k.ts("dve", HM[:, c, n * 128:(n + 1) * 128], p[:, q * 128:(q + 1) * 128], mf[:, j * 16 + 8 + c:j * 16 + 9 + c],
                                     mf[:, j * 16 + c:j * 16 + c + 1], ALU.mult, ALU.add, r=[p, mf], w=[], ww=HM.b[n])
                            else:
                                k.act(HM[:, c, n * 128:(n + 1) * 128], p[:, q * 128:(q + 1) * 128], AF.Identity, r=[p, mf], w=[], ww=HM.b2[n],
                                      scale=mf[:, j * 16 + 8 + c:j * 16 + 9 + c], bias=mf[:, j * 16 + c:j * 16 + c + 1])
                return B

            run_pipe([(mkA(n), mkB(n)) for n in tiles], 2)

    def ln_stats(self, ht, st, eps=1e-6):
        k = self.k
        k.op("dve", lambda e: e.bn_stats(out=st[:, 0:6], in_=ht[:, 0:512]), r=ht, w=st)
        k.op("dve", lambda e: e.bn_stats(out=st[:, 6:12], in_=ht[:, 512:1024]), r=ht, w=st)
        k.op("dve", lambda e: e.bn_aggr(out=st[:, 12:14], in_=st[:, 0:12]), r=st, w=st)
        k.act(st[:, 14:15], st[:, 13:14], AF.Sqrt, r=st, w=st, bias=eps)

    def ln_apply(self, ht, xn, st):
        k = self.k
        k.op("dve", lambda e: e.reciprocal(out=st[:, 14:15], in_=st[:, 14:15]), r=st, w=st)
        k.stt(st[:, 15:16], st[:, 12:13], -1.0, st[:, 14:15], ALU.mult, ALU.mult, r=st, w=st)
        k.act(xn[:], ht[:], AF.Identity, r=[ht, st], w=xn, scale=st[:, 14:15], bias=st[:, 15:16])

    def wload(self, pool, key, l, kc_n, segs, rows0=0):
        k = self.k
        w = pool.next()
        src = self.WB[key][l]
        pairs = []
        o = 0
        for c0, n in segs:
            pairs.append((w[:, 0:kc_n, o:o + n], src[rows0:rows0 + kc_n * 128, c0:c0 + n].rearrange("(kc p) n -> p kc n", p=128)))
            o += n
        k.dma_group(pairs, key=w, w=w)
        return w

    def proj(self, p, N, w, wc0, src, t0, kc_n, extra_r=()):
        k = self.k
        rb = [src.rb(n) for n in tiles_of(t0, N)]
        for kc in range(kc_n):
            k.mm(p[:, :N], w[:, kc, wc0:wc0 + 128], src[:, kc, t0:t0 + N], kc == 0, kc == kc_n - 1, r=[w, rb, extra_r], w=p)

    def fm_params(self, dst, rows):
        k = self.k
        tot = sum(r.shape[0] for r in rows)
        st = k.sb("fmst", [tot, 128], F32)
        pairs = []
        o = 0
        for r in rows:
            pairs.append((st[o:o + r.shape[0], :], r))
            o += r.shape[0]
        k.dma_group(pairs, key=st, w=st)
        p = self.psum()
        k.tr(p[:, 0:tot], st[0:tot, :], self.ident[0:tot, 0:tot], r=[st, self.ident], w=p)
        k.cp("dve", dst[:, 0:tot], p[:, 0:tot], r=p, w=dst)

    def bcast_load(self, dst, row_ap, n):
        self.k.dma(dst[:, 0:n], row_ap.to_broadcast([128, n]), key=dst, w=dst)

    def log1p_small(self, tt, out, y, t):
        k = self.k
        k.ts("dve", t, y, -0.2, 0.25, ALU.mult, ALU.add, r=tt, w=tt)
        for cst in (1.0 / 3.0, 0.5, 1.0):
            k.tt("dve", t, t, y, ALU.mult, r=tt, w=tt)
            k.ts("dve", t, t, -1.0, cst, ALU.mult, ALU.add, r=tt, w=tt)
        k.tt("dve", out, t, y, ALU.mult, r=tt, w=tt)

    def retention(self, b, l, HM, BR, do_ctx):
        k = self.k
        lnscale = float(np.log(128.0 ** -0.5))
        with k.scope():
            QK = k.sb("QK", [128, 8, T], BF16, nb=NT)
            Vt = k.sb("Vt", [128, NT, 512], BF16, nb=NT)
            with k.scope():
                rc = k.sb("rc", [128, SEQ], F32); rs = k.sb("rs", [128, SEQ], F32)
                k.dma(rc[:], self.din["k_retC"], key=rc, w=rc)
                k.dma(rs[:], self.din["k_retS"], key=rs, w=rs)
                wp = k.rot("wqk", [128, 8, 128], BF16, 2)
                t1s = k.rot("rt1", [128, 512], F32, 2); t2s = k.rot("rt2", [128, 512], F32, 2)
                qas = k.rot("rqa", [128, 512], BF16, 2)
                perm = k.sb("perm", [128, 128], BF16)
                k.dma(perm[:], self.din["k_perm"], key=perm, w=perm)
                for qi in range(8):
                    base = (O_RQ if qi < 4 else O_RK) + (qi % 4) * 128
                    w = self.wload(wp, "in", l, 8, [(base, 128)])
                    for t0, N in BLOCKS:
                        tl = [QK.b[n] for n in tiles_of(t0, N)]
                        pa = self.psum()
                        self.proj(pa, N, w, 0, HM, t0, 8)
                        if t0 < CTX:
                            k.cp("act", QK[:, qi, t0:t0 + N], pa[:, :N], r=pa, w=tl)
                        else:
                            qa = qas.next()
                            k.cp("act", qa[:, :N], pa[:, :N], r=pa, w=qa)
                            pb = self.psum()
                            k.mm(pb[:, :N], perm[:], qa[:, :N], True, True, r=[perm, qa], w=pb)
                            s0 = t0 - CTX
                            t1 = t1s.next(); t2 = t2s.next()
                            k.tt("dve", t1[:, :N], pa[:, :N], rc[:, s0:s0 + N], ALU.mult, r=[pa, rc], w=t1)
                            k.tt("dve", t2[:, :N], pb[:, :N], rs[:, s0:s0 + N], ALU.mult, r=[pb, rs], w=t2)
                            k.tt("pool", QK[:, qi, t0:t0 + N], t1[:, :N], t2[:, :N], ALU.add, r=[t1, t2], w=tl)
                wv = self.wload(k.rot("wv", [128, 8, 512], BF16, 1), "in", l, 8, [(O_RV, 512)])
                for n in range(NT):
                    p = self.psum()
                    for kc in range(8):
                        k.mm(p[:, :], HM[:, kc, n * 128:(n + 1) * 128], wv[:, kc, :], kc == 0, kc == 7, r=[wv, HM.rb(n)], w=p)
                    k.cp("act", Vt[:, n, :], p[:, :], r=p, w=Vt.b[n])
            sm = k.sb("rsm", [128, 64], F32)
            self.bcast_load(sm, self.din["ret_decay"][l:l + 1, :], 8)
            k.act(sm[:, 8:16], sm[:, 0:8], AF.Exp, r=sm, w=sm, scale=-1.0)
            self.log1p_small(sm, sm[:, 16:24], sm[:, 8:16], sm[:, 24:32])
            k.ts("dve", sm[:, 32:40], sm[:, 16:24], -1.0, None, ALU.mult, None, r=sm, w=sm)
            gp = k.sb("rgp", [128, 8], F32)
            self.fm_params(gp, [self.din["ret_gn_w"][l].rearrange("(c p) -> c p", p=128), self.din["ret_gn_b"][l].rearrange("(c p) -> c p", p=128)])
            posf = k.sb("posf", [128, T], F32); posb = k.sb("posb", [128, T], F32)
            ptm = k.sb("ptm", [128, 2, NT], F32)
            k.dma(posf[:], self.din["k_posf"], key=posf, w=posf)
            k.dma(posb[:], self.din["k_posb"], key=posb, w=posb)
            k.dma(ptm[:, 0, :], self.din["k_posf_tm"], key=ptm, w=ptm)
            k.dma(ptm[:, 1, :], self.din["k_posb_tm"], key=ptm, w=ptm)
            mf = k.sb("mfm", [128, 128], F32); mb = k.sb("mbm", [128, 128], F32)
            k.dma(mf[:], self.din["k_mf"], key=mf, w=mf)
            k.dma(mb[:], self.din["k_mb"], key=mb, w=mb)
            Qf = k.sb("Qf", [128, T], F32); Qb = k.sb("Qb", [128, T], F32)
            bias = k.sb("rbias", [128, 2, NT], F32)
            Ed = k.sb("Ed", [128, 128], F32)
            Es = k.rot("E", [128, 512], F32, 3); E2s = k.rot("E2", [128, 512], F32, 2)
            Ws = k.rot("W", [128, 512], BF16, 3)
            wg = k.rot("wg", [128, 8, 128], BF16, 2)
            fin = [k.rot(f"rf{i}", [128, 512], F32, 2) for i in range(3)]
            ysbs = k.rot("ysb", [128, 512], BF16, 2); ysqbs = k.rot("ysqb", [128, 512], BF16, 2)
            SG = k.sb("SG", [128, T], F32)
            accb = Rot(self.ps[0:2]); rotb = Rot(self.ps[2:8])
            for h in range(4):
                lgf = sm[:, 32 + h:33 + h]; lgb = sm[:, 36 + h:37 + h]
                k.act(Qf[:], posf[:], AF.Copy, r=[posf, sm], w=Qf, scale=lgf)
                k.act(Qb[:], posb[:], AF.Copy, r=[posb, sm], w=Qb, scale=lgb)
                k.ts("dve", bias[:, 0, :], ptm[:, 0, :], lgf, -1.0, ALU.mult, ALU.mult, r=[ptm, sm], w=bias)
                k.ts("dve", bias[:, 1, :], ptm[:, 1, :], lgb, -1.0, ALU.mult, ALU.mult, r=[ptm, sm], w=bias)
                k.ts("dve", bias[:, :, :], bias[:, :, :], lnscale, None, ALU.add, None, r=bias, w=bias)
                e1 = Es.next(); e2 = E2s.next()
                k.tt("dve", e1[:, 0:128], Qf[:, 256:384], mf[:], ALU.add, r=[Qf, mf], w=e1)
                k.act(e1[:, 0:128], e1[:, 0:128], AF.Exp, r=[e1, bias], w=e1, bias=bias[:, 0, 2:3])
                k.tt("dve", e2[:, 0:128], Qb[:, 256:384], mb[:], ALU.add, r=[Qb, mb], w=e2)
                k.act(e2[:, 0:128], e2[:, 0:128], AF.Exp, r=[e2, bias], w=e2, bias=bias[:, 1, 2:3])
                k.tt("pool", Ed[:], e1[:, 0:128], e2[:, 0:128], ALU.add, r=[e1, e2], w=Ed)
                wgt = self.wload(wg, "in", l, 8, [(O_RG + h * 128, 128)])
                for t0, N in BLOCKS:
                    if t0 < CTX and not do_ctx:
                        continue
                    G = rotb.next()
                    self.proj(G, N, wgt, 0, HM, t0, 8)
                    k.act(SG[:, t0:t0 + N], G[:, :N], AF.Silu, r=G, w=SG)
                k.preload_lnexp()
                pending = None
                for t0, N in BLOCKS:
                    if t0 < CTX and not do_ctx:
                        continue
                    I0 = t0 // 128
                    nI = N // 128
                    keys = [0, 1] if t0 < CTX else list(range(NT))
                    Y = accb.next()
                    def mkA(J):
                        def A():
                            S = rotb.next()
                            k.mm(S[:, :N], QK[:, 4 + h, J * 128:(J + 1) * 128], QK[:, h, t0:t0 + N], True, True,
                                 r=[QK.b[J]] + [QK.b[n] for n in tiles_of(t0, N)], w=S)
                            return S
                        return A

                    def mkB(ji, J):
                        def B(S):
                            E = Es.next()
                            if J < 2 and t0 >= CTX:
                                E2 = E2s.next()
                                k.act(E[:, :N], Qf[:, t0:t0 + N], AF.Exp, r=[Qf, bias], w=E, bias=bias[:, 0, J:J + 1])
                                k.act(E2[:, :N], Qb[:, t0:t0 + N], AF.Exp, r=[Qb, bias], w=E2, bias=bias[:, 1, J:J + 1])
                                k.tt("pool", E[:, :N], E[:, :N], E2[:, :N], ALU.add, r=[E, E2], w=E)
                            else:
                                d = J - I0
                                lo = min(max(d * 128, 0), N); hi = min(max((d + 1) * 128, 0), N)
                                if lo > 0:
                                    k.act(E[:, 0:lo], Qb[:, t0:t0 + lo], AF.Exp, r=[Qb, bias], w=E, bias=bias[:, 1, J:J + 1])
                                if hi < N:
                                    k.act(E[:, hi:N], Qf[:, t0 + hi:t0 + N], AF.Exp, r=[Qf, bias], w=E, bias=bias[:, 0, J:J + 1])
                                if lo < hi:
                                    k.cp("pool", E[:, lo:hi], Ed[:], r=Ed, w=E)
                            W = Ws.next()
                            k.tt("dve", W[:, :N], S[:, :N], E[:, :N], ALU.mult, r=[S, E], w=W)
                            k.mm(Y[:, :N], Vt[:, J, h * 128:(h + 1) * 128], W[:, :N], ji == 0, ji == len(keys) - 1, r=[Vt.b[J], W], w=Y)
                        return B

                    steps = [(mkA(J), mkB(ji, J)) for ji, J in enumerate(keys)]
                    if pending is not None:
                        if len(steps) > 3:
                            a3, b3 = steps[2]
                            steps[2] = (a3, (lambda S_, b3=b3, fp=pending: (b3(S_), fp())))
                        else:
                            pending()
                        pending = None
                    run_pipe(steps, 2)

                    def mkfin(Y=Y, N=N, t0=t0, h=h):
                        def finish():
                            mean, t3, t4 = [f.next() for f in fin]
                            ysb = ysbs.next(); ysqb = ysqbs.next()
                            k.cp("act", ysb[:, :N], Y[:, :N], r=Y, w=ysb)
                            k.act(ysqb[:, :N], Y[:, :N], AF.Square, r=Y, w=ysqb)
                            P1 = rotb.next(); P2 = rotb.next()
                            k.mm(P1[:, :N], self.onesb[:], ysb[:, :N], True, True, r=[self.onesb, ysb], w=P1)
                            k.mm(P2[:, :N], self.onesb[:], ysqb[:, :N], True, True, r=[self.onesb, ysqb], w=P2)
                            k.act(mean[:, :N], P1[:, :N], AF.Copy, r=P1, w=mean, scale=1.0 / 128)
                            k.act(t3[:, :N], P1[:, :N], AF.Square, r=P1, w=t3, scale=1.0 / 128)
                            k.stt(t3[:, :N], P2[:, :N], 1.0 / 128, t3[:, :N], ALU.mult, ALU.subtract, r=[P2, t3], w=t3)
                            k.act(t3[:, :N], t3[:, :N], AF.Ln, r=t3, w=t3, bias=1e-6)
                            k.act(t3[:, :N], t3[:, :N], AF.Exp, r=t3, w=t3, scale=-0.5)
                            k.tt("dve", t4[:, :N], Y[:, :N], mean[:, :N], ALU.subtract, r=[Y, mean], w=t4)
                            k.tt("dve", t4[:, :N], t4[:, :N], t3[:, :N], ALU.mult, r=[t4, t3], w=t4)
                            k.ts("dve", t4[:, :N], t4[:, :N], gp[:, h:h + 1], gp[:, 4 + h:5 + h], ALU.mult, ALU.add, r=[t4, gp], w=t4)
                            k.tt("pool", BR[:, h, t0:t0 + N], SG[:, t0:t0 + N], t4[:, :N], ALU.mult, r=[SG, t4], w=[BR.b[n] for n in tiles_of(t0, N)])
                        return finish

                    pending = mkfin()
                if pending is not None:
                    pending()
                    pending = None

    def rms_block(self, p_list, N, nch, dim, normw, out_bf, cbuf, sqbuf, rbuf, rotb, nw_tt):
        k = self.k
        for c in range(nch):
            k.cp("act", cbuf[:, c, :N], p_list[c][:, :N], r=p_list[c], w=cbuf)
            k.tt("pool", sqbuf[:, c, :N], cbuf[:, c, :N], cbuf[:, c, :N], ALU.mult, r=cbuf, w=sqbuf)
        ss = rotb.next()
        for c in range(nch):
            k.mm(ss[:, :N], self.ones[:], sqbuf[:, c, :N], c == 0, c == nch - 1, r=[self.ones, sqbuf], w=ss)
        k.act(rbuf[:, :N], ss[:, :N], AF.Ln, r=ss, w=rbuf, scale=1.0 / dim, bias=1e-6)
        k.act(rbuf[:, :N], rbuf[:, :N], AF.Exp, r=rbuf, w=rbuf, scale=-0.5)
        for c in range(nch):
            k.tt("pool", cbuf[:, c, :N], cbuf[:, c, :N], rbuf[:, :N], ALU.mult, r=[cbuf, rbuf], w=cbuf)
            k.ts("dve", out_bf[:, c, :N], cbuf[:, c, :N], normw[:, c:c + 1], None, ALU.mult, None, r=[cbuf, nw_tt], w=out_bf)

    def rope_combine(self, dst_ap, dst_bufs, pa, pb, N, cT, sT, s0, t1, t2):
        k = self.k
        k.tt("dve", t1[:, :N], pa[:, :N], cT[:, s0:s0 + N], ALU.mult, r=[pa, cT], w=t1)
        k.tt("dve", t2[:, :N], pb[:, :N], sT[:, s0:s0 + N], ALU.mult, r=[pb, sT], w=t2)
        k.tt("pool", dst_ap, t1[:, :N], t2[:, :N], ALU.add, r=[t1, t2], w=dst_bufs)

    def mla(self, b, l, HM, BR, do_ctx):
        k = self.k
        scale = 192.0 ** -0.5
        with k.scope():
            KN = k.sb("KN", [128, 4, T], BF16, nb=NT)
            KR = k.sb("KR", [128, T], BF16, nb=NT)
            Vt = k.sb("Vt", [128, NT, 512], BF16, nb=NT)
            mc = k.sb("mc", [128, SEQ], F32); ms = k.sb("ms", [128, SEQ], F32)
            k.dma(mc[:], self.din["k_mlaC"], key=mc, w=mc)
            k.dma(ms[:], self.din["k_mlaS"], key=ms, w=ms)
            nw = k.sb("mnw", [128, 8], F32)
            self.fm_params(nw, [self.din["mla_q_norm"][l].rearrange("(c p) -> c p", p=128), self.din["mla_kv_norm"][l].rearrange("(c p) -> c p", p=128)])
            cb = k.sb("mcb", [128, 3, 512], F32); sq = k.sb("msq", [128, 3, 512], F32); rb = k.sb("mrb", [128, 512], F32)
            t1s = k.rot("mt1", [128, 512], F32, 2); t2s = k.rot("mt2", [128, 512], F32, 2)
            rotb = Rot(self.ps[4:8])
            k.preload_lnexp()
            with k.scope():
                one = lambda nm, shp: k.rot(nm, shp, BF16, 1)
                wckv = self.wload(one("wckv", [128, 8, 256]), "in", l, 8, [(O_CKV, 256)])
                wkr = self.wload(one("wkr", [128, 8, 256]), "in", l, 8, [(O_KR, 64), (O_KR, 64), (O_KR + 32, 32), (O_KR, 32), (O_KR + 32, 32), (O_KR, 32)])
                wk = self.wload(one("wukvk", [128, 2, 512]), "ukv", l, 2, [(256 * h, 128) for h in range(4)])
                wv = self.wload(one("wukvv", [128, 2, 512]), "ukv", l, 2, [(256 * h + 128, 128) for h in range(4)])
                ckvn = k.rot("ckvn", [128, 2, 512], BF16, 2)
                for t0, N in BLOCKS:
                    tl = tiles_of(t0, N)
                    ps_ = [rotb.next() for _ in range(2)]
                    for c in range(2):
                        self.proj(ps_[c], N, wckv, c * 128, HM, t0, 8)
                    cn = ckvn.next()
                    self.rms_block(ps_, N, 2, 256.0, nw[:, 3:5], cn, cb, sq, rb, rotb, nw)
                    for h in range(4):
                        p = rotb.next()
                        for kc in range(2):
                            k.mm(p[:, :N], wk[:, kc, h * 128:(h + 1) * 128], cn[:, kc, :N], kc == 0, kc == 1, r=[wk, cn], w=p)
                        k.cp("act", KN[:, h, t0:t0 + N], p[:, :N], r=p, w=[KN.b[n] for n in tl])
                    for n in tl:
                        p = rotb.next()
                        o = n * 128 - t0
                        for kc in range(2):
                            k.mm(p[:, :], cn[:, kc, o:o + 128], wv[:, kc, :], kc == 0, kc == 1, r=[wv, cn], w=p)
                        k.cp("act", Vt[:, n, :], p[:, :], r=p, w=Vt.b[n])
                    pa = rotb.next()
                    self.proj(pa, N, wkr, 0, HM, t0, 8)
                    if t0 < CTX:
                        k.cp("act", KR[:, t0:t0 + N], pa[:, :N], r=pa, w=[KR.b[n] for n in tl])
                    else:
                        pb = rotb.next()
                        self.proj(pb, N, wkr, 128, HM, t0, 8)
                        self.rope_combine(KR[:, t0:t0 + N], [KR.b[n] for n in tl], pa, pb, N, mc, ms, t0 - CTX, t1s.next(), t2s.next())
            one = lambda nm, shp: k.rot(nm, shp, BF16, 1)
            wcq = self.wload(one("wcq", [128, 8, 384]), "in", l, 8, [(O_CQ, 384)])
            wqn = self.wload(one("wuqn", [128, 3, 512]), "uq", l, 3, [(192 * h, 128) for h in range(4)])
            segs = [(192 * h + 128, 64) for h in range(4)]
            for h in range(4):
                segs += [(192 * h + 128 + 32, 32), (192 * h + 128, 32)]
            wqr = self.wload(one("wuqr", [128, 3, 512]), "uq", l, 3, segs)
            cqn = k.sb("cqn", [128, 3, 512], BF16)
            QN = k.sb("QN", [128, 4, 512], BF16); QR = k.sb("QR", [128, 2, 512], BF16)
            Ps = k.rot("P", [128, 512], BF16, 3)
            rinv = k.rot("rinv", [128, 512], F32, 2)
            paccs = k.rot("pacc", [128, 512], F32, 2)
            accb = Rot(self.ps[0:4])
            for t0, N in BLOCKS:
                if t0 < CTX and not do_ctx:
                    continue
                ps_ = [rotb.next() for _ in range(3)]
                for c in range(3):
                    self.proj(ps_[c], N, wcq, c * 128, HM, t0, 8)
                self.rms_block(ps_, N, 3, 384.0, nw[:, 0:3], cqn, cb, sq, rb, rotb, nw)
                for h in range(4):
                    p = rotb.next()
                    for kc in range(3):
                        k.mm(p[:, :N], wqn[:, kc, h * 128:(h + 1) * 128], cqn[:, kc, :N], kc == 0, kc == 2, r=[wqn, cqn], w=p)
                    k.cp("act", QN[:, h, :N], p[:, :N], r=p, w=QN)
                for a in range(2):
                    pa = rotb.next()
                    for kc in range(3):
                        k.mm(pa[:, :N], wqr[:, kc, a * 128:(a + 1) * 128], cqn[:, kc, :N], kc == 0, kc == 2, r=[wqr, cqn], w=pa)
                    if t0 < CTX:
                        k.cp("act", QR[:, a, :N], pa[:, :N], r=pa, w=QR)
                    else:
                        pb = rotb.next()
                        for kc in range(3):
                            k.mm(pb[:, :N], wqr[:, kc, 256 + a * 128:256 + (a + 1) * 128], cqn[:, kc, :N], kc == 0, kc == 2, r=[wqr, cqn], w=pb)
                        self.rope_combine(QR[:, a, :N], QR, pa, pb, N, mc, ms, t0 - CTX, t1s.next(), t2s.next())
                keys = [0, 1] if t0 < CTX else list(range(NT))
                pending = None
                for h in range(4):
                    O = accb.next(); R = accb.next()
                    hp = 64 * (h % 2)
                    def mkA(J, h=h, hp=hp):
                        def A():
                            S = rotb.next()
                            k.mm(S[:, :N], KN[:, h, J * 128:(J + 1) * 128], QN[:, h, :N], True, False, r=[KN.b[J], QN], w=S)
                            k.mm(S[:, :N], KR[hp:hp + 64, J * 128:(J + 1) * 128], QR[hp:hp + 64, h // 2, :N], False, True, r=[KR.b[J], QR], w=S)
                            return S
                        return A

                    Pacc = paccs.next()

                    def mkB(ji, J, h=h, O=O, R=R, Pacc=Pacc):
                        def B(S):
                            P = Ps.next()
                            k.act(P[:, :N], S[:, :N], AF.Exp, r=S, w=P, scale=scale)
                            k.mm(O[:, :N], Vt[:, J, h * 128:(h + 1) * 128], P[:, :N], ji == 0, ji == len(keys) - 1, r=[Vt.b[J], P], w=O)
                            k.mm(R[:, :N], self.onesb[:], P[:, :N], ji == 0, ji == len(keys) - 1, r=[self.onesb, P], w=R)
                        return B

                    steps = [(mkA(J), mkB(ji, J)) for ji, J in enumerate(keys)]
                    if pending is not None:
                        if len(steps) > 3:
                            a3, b3 = steps[2]
                            steps[2] = (a3, (lambda S_, b3=b3, fp=pending: (b3(S_), fp())))
                        else:
                            pending()
                        pending = None
                    run_pipe(steps, 2)

                    def mkfin(O=O, R=R, N=N, t0=t0, h=h):
                        def finish():
                            ri = rinv.next()
                            k.act(ri[:, :N], R[:, :N], AF.Ln, r=R, w=ri)
                            k.act(ri[:, :N], ri[:, :N], AF.Exp, r=ri, w=ri, scale=-1.0)
                            k.tt("dve", BR[:, h, t0:t0 + N], O[:, :N], ri[:, :N], ALU.mult, r=[O, ri], w=[BR.b[n] for n in tiles_of(t0, N)])
                        return finish

                    pending = mkfin()
                if pending is not None:
                    pending()
                    pending = None

    def conv4(self, u, U0, pr, wcol, bcol, eng="dve"):
        k = self.k
        k.ts(eng, u[:, :], U0[:, :], wcol(1), bcol, ALU.mult, ALU.add, r=[U0, pr], w=u)
        for s_, e_ in ((0, CTX), (CTX, T)):
            k.stt(u[:, s_ + 1:e_], U0[:, s_:e_ - 1], wcol(0), u[:, s_ + 1:e_], ALU.mult, ALU.add, r=[U0, u, pr], w=u)
            k.stt(u[:, s_:e_ - 1], U0[:, s_ + 1:e_], wcol(2), u[:, s_:e_ - 1], ALU.mult, ALU.add, r=[U0, u, pr], w=u)
            k.stt(u[:, s_:e_ - 2], U0[:, s_ + 2:e_], wcol(3), u[:, s_:e_ - 2], ALU.mult, ALU.add, r=[U0, u, pr], w=u)

    def lru(self, b, l, HM, BR, do_ctx):
        k = self.k
        with k.scope():
            pr = k.sb("lpr", [128, 48], F32)
            self.fm_params(pr, [self.din["lru_conv_w"][l].rearrange("k (c p) -> (k c) p", p=128),
                                self.din["lru_conv_b"][l].rearrange("(c p) -> c p", p=128),
                                self.din["lru_gate_b"][l].rearrange("d g (c p) -> (d g c) p", p=128),
                                self.din["lru_lambda"][l].rearrange("d (c p) -> (d c) p", p=128)])
            sm = k.sb("lsm", [128, 32], F32)
            k.act(sm[:, 0:8], pr[:, 36:44], AF.Exp, r=pr, w=sm, scale=-1.0)
            self.log1p_small(sm, sm[:, 8:16], sm[:, 0:8], sm[:, 16:24])
            k.ts("dve", sm[:, 24:32], sm[:, 8:16], -8.0, None, ALU.mult, None, r=sm, w=sm)
            GW = k.sb("GW", [128, 16, 128], BF16)
            with k.scope():
                stg = k.sb("gwst", [128, 16, 128], F32)
                k.op("pool", lambda e: e.memset(stg[:], 0.0), w=stg)
                pairs = []
                for d in range(2):
                    for g in range(2):
                        for cc in range(4):
                            for j in range(2):
                                pairs.append((stg[64 * j:64 * j + 64, (d * 2 + g) * 4 + cc, 64 * j:64 * j + 64], self.din["lru_gate_w"][l, d, g, 2 * cc + j]))
                k.dma_group(pairs, key=stg, w=stg)
                k.cp("dve", GW[:], stg[:], r=stg, w=GW)
            wp = k.rot("lw", [128, 8, 256], BF16, 2)
            U0 = k.sb("lU0", [128, T], F32); u = k.sb("lu", [128, T], F32)
            aa = [k.sb(f"la{d}", [128, T], F32) for d in range(2)]
            iis = [k.sb(f"li{d}", [128, T], F32) for d in range(2)]
            hh = [k.sb(f"lh{d}", [128, T], F32) for d in range(2)]
            ub = k.sb("lub", [128, T], BF16)
            for cc in range(4):
                w = self.wload(wp, "in", l, 8, [(O_LX + cc * 128, 128), (O_LG + cc * 128, 128)])
                for t0, N in BLOCKS:
                    p = self.psum()
                    self.proj(p, N, w, 0, HM, t0, 8)
                    k.cp("act", U0[:, t0:t0 + N], p[:, :N], r=p, w=U0)
                self.conv4(u, U0, pr, lambda kk: pr[:, kk * 4 + cc:kk * 4 + cc + 1], pr[:, 16 + cc:17 + cc])
                k.cp("act", ub[:], u[:], r=u, w=ub)
                for d in range(2):
                    a = aa[d]; ii = iis[d]; h_ = hh[d]
                    for g, dst in ((0, a), (1, ii)):
                        gi = (d * 2 + g) * 4 + cc
                        for t0, N in BLOCKS:
                            p = self.psum()
                            k.mm(p[:, :N], GW[:, gi, :], ub[:, t0:t0 + N], True, True, r=[GW, ub], w=p)
                            k.act(dst[:, t0:t0 + N], p[:, :N], AF.Sigmoid, r=[p, pr], w=dst, bias=pr[:, 20 + gi:21 + gi])
                    k.act(a[:], a[:], AF.Exp, r=[a, sm], w=a, scale=sm[:, 24 + d * 4 + cc:25 + d * 4 + cc])
                    k.act(h_[:], a[:], AF.Square, r=a, w=h_)
                    k.act(h_[:], h_[:], AF.Sqrt, r=h_, w=h_, scale=-1.0, bias=1.0)
                    k.tt("dve", ii[:], ii[:], h_[:], ALU.mult, r=[h_, ii], w=ii)
                    k.tt("dve", ii[:], ii[:], u[:], ALU.mult, r=[ii, u], w=ii)
                    if d == 0:
                        k.op("dve", lambda e: e.tensor_tensor_scan(out=h_[:], data0=a[:], data1=ii[:], initial=0.0, op0=ALU.mult, op1=ALU.add), r=[a, ii], w=h_)
                    else:
                        k.op("dve", lambda e: e.tensor_tensor_scan(out=h_[:, 0:CTX][:, ::-1], data0=a[:, 0:CTX][:, ::-1], data1=ii[:, 0:CTX][:, ::-1],
                                                                   initial=0.0, op0=ALU.mult, op1=ALU.add), r=[a, ii], w=h_)
                        k.op("dve", lambda e: e.tensor_tensor_scan(out=h_[:, CTX:T][:, ::-1], data0=a[:, CTX:T][:, ::-1], data1=ii[:, CTX:T][:, ::-1],
                                                                   initial=h_[:, 0:1], op0=ALU.mult, op1=ALU.add), r=[a, ii, h_], w=h_)
                t = iis[0]; hs = aa[0]
                for t0, N in BLOCKS:
                    p = self.psum()
                    self.proj(p, N, w, 128, HM, t0, 8)
                    k.cp("act", U0[:, t0:t0 + N], p[:, :N], r=p, w=U0)
                k.act(t[:], U0[:], AF.Square, r=U0, w=t)
                k.act(t[:], t[:], AF.Identity, r=t, w=t, scale=0.044715, bias=1.0)
                k.tt("dve", t[:], t[:], U0[:], ALU.mult, r=[t, U0], w=t)
                k.act(t[:], t[:], AF.Sigmoid, r=t, w=t, scale=1.5957691216)
                k.tt("dve", hs[:], hh[0][:], hh[1][:], ALU.add, r=[hh[0], hh[1]], w=hs)
                k.tt("dve", hs[:], hs[:], U0[:], ALU.mult, r=[hs, U0], w=hs)
                k.tt("dve", BR[:, cc, :], hs[:], t[:], ALU.mult, r=[hs, t], w=BR)

    def ssd(self, b, l, HM, BR, do_ctx):
        k = self.k
        tq0 = 0 if do_ctx else CTX
        with k.scope():
            BT = k.sb("BT", [128, 2, T], BF16, nb=NT); CT = k.sb("CT", [128, 2, T], BF16, nb=NT)
            Xt = k.sb("Xt", [128, NT, 512], BF16, nb=NT)
            DT = k.sb("DT", [128, NT, 16], F32); Vc = k.sb("Vc", [128, NT, 16], F32); bia = k.sb("bia", [128, NT, 16], F32)
            pr = k.sb("spr", [128, 48], F32)
            self.fm_params(pr, [self.din["ssd_conv_w"][l].rearrange("k (c p) -> (k c) p", p=128),
                                self.din["ssd_conv_b"][l].rearrange("(c p) -> c p", p=128),
                                self.din["ssd_norm_w"][l].rearrange("(c p) -> c p", p=128)])
            sm = k.sb("ssm", [128, 48], F32)
            self.bcast_load(sm, self.din["ssd_a_log"][l:l + 1, :], 16)
            k.act(sm[:, 0:16], sm[:, 0:16], AF.Exp, r=sm, w=sm)
            k.ts("dve", sm[:, 0:16], sm[:, 0:16], -1.0, None, ALU.mult, None, r=sm, w=sm)
            k.dma(sm[:, 16:32], self.din["ssd_dt_bias"][l:l + 1, :].to_broadcast([128, 16]), key=sm, w=sm)
            k.dma(sm[:, 32:40], self.din["ssd_d"][l:l + 1, :].to_broadcast([128, 8]), key=sm, w=sm)
            dI = k.sb("dI", [128, 8, 128], BF16)
            for h in range(8):
                k.ts("dve", dI[:, h, :], self.ident[:], sm[:, 32 + h:33 + h], None, ALU.mult, None, r=[self.ident, sm], w=dI)
            mf = k.sb("smf", [128, 128], F32); mb = k.sb("smb", [128, 128], F32)
            k.dma(mf[:], self.din["k_mf"], key=mf, w=mf)
            k.dma(mb[:], self.din["k_mb"], key=mb, w=mb)
            with k.scope():
                U0 = k.sb("sU0", [128, T], F32); u = k.sb("su", [128, T], F32); xs = k.sb("sxs", [128, T], BF16)
                wp = k.rot("sw", [128, 8, 128], BF16, 2)
                up = k.sb("sup", [128, 128], F32); lo = k.sb("slo", [128, 128], F32)
                k.dma(up[:], self.din["k_up"], key=up, w=up)
                k.dma(lo[:], self.din["k_lo"], key=lo, w=lo)
                for c in range(8):
                    w = self.wload(wp, "in", l, 8, [(O_SX + c * 128, 128)])
                    for t0, N in BLOCKS:
                        p = self.psum()
                        self.proj(p, N, w, 0, HM, t0, 8)
                        k.cp("act", U0[:, t0:t0 + N], p[:, :N], r=p, w=U0)
                    self.conv4(u, U0, pr, lambda kk: pr[:, kk * 8 + c:kk * 8 + c + 1], pr[:, 32 + c:33 + c])
                    if c < 4:
                        k.act(xs[:], u[:], AF.Silu, r=u, w=xs)
                        for n in range(NT):
                            pt = self.psum()
                            ptb = pt[:].bitcast(BF16)
                            k.tr(ptb[:, 0:128], xs[:, n * 128:(n + 1) * 128], self.identb[:], r=[xs, self.identb], w=pt)
                            k.cp("pool" if False else "dve", Xt[:, n, c * 128:(c + 1) * 128], ptb[:, 0:128], r=pt, w=Xt.b[n])
                    elif c < 6:
                        k.act(BT[:, c - 4, :], u[:], AF.Silu, r=u, w=BT)
                    else:
                        k.act(CT[:, c - 6, :], u[:], AF.Silu, r=u, w=CT)
                wdt = self.wload(k.rot("swdt", [128, 8, 16], BF16, 1), "in", l, 8, [(O_SDT, 16)])
                for n in range(NT):
                    p = self.psum()
                    for kc in range(8):
                        k.mm(p[:, 0:16], HM[:, kc, n * 128:(n + 1) * 128], wdt[:, kc, :], kc == 0, kc == 7, r=[wdt, HM.rb(n)], w=p)
                    k.tt("dve", DT[:, n, :], p[:, 0:16], sm[:, 16:32], ALU.add, r=[p, sm], w=DT)
                tA = k.sb("stA", [128, NT, 16], F32); tB = k.sb("stB", [128, NT, 16], F32)
                k.ts("dve", tA[:], DT[:], -1.0, None, ALU.mult, None, r=DT, w=tA)
                k.tt("dve", tA[:], tA[:], DT[:], ALU.min, r=[tA, DT], w=tA)
                k.act(tA[:], tA[:], AF.Exp, r=tA, w=tA)
                k.act(tA[:], tA[:], AF.Ln, r=tA, w=tA, bias=1.0)
                k.ts("dve", tB[:], DT[:], 0.0, None, ALU.max, None, r=DT, w=tB)
                k.tt("dve", DT[:], tA[:], tB[:], ALU.add, r=[tA, tB], w=DT)
                k.act(bia[:], DT[:], AF.Ln, r=DT, w=bia)
                for n in range(NT):
                    k.tt("dve", tA[:, n, :], DT[:, n, :], sm[:, 0:16], ALU.mult, r=[DT, sm], w=tA)
                for n in range(NT):
                    p = self.psum()
                    k.mm(p[:, 0:16], self.ones[:], tA[:, n, :], True, True, r=[self.ones, tA], w=p)
                    k.cp("act", tB[:, n, :], p[:, 0:16], r=p, w=tB)
                car = k.sb("scar", [128, NT, 16], F32)
                k.op("pool", lambda e: e.memset(car[:], 0.0), w=car)
                for n in range(1, NT):
                    k.tt("dve", car[:, n, 0:8], car[:, n - 1, 0:8], tB[:, n - 1, 0:8], ALU.add, r=[car, tB], w=car)
                k.cp("dve", car[:, 0, 8:16], tB[:, 1, 8:16], r=tB, w=car)
                k.tt("dve", car[:, NT - 1, 8:16], tB[:, 0, 8:16], tB[:, 1, 8:16], ALU.add, r=tB, w=car)
                for n in range(NT - 2, 1, -1):
                    k.tt("dve", car[:, n, 8:16], car[:, n + 1, 8:16], tB[:, n + 1, 8:16], ALU.add, r=[car, tB], w=car)
                for n in range(NT):
                    p = self.psum()
                    k.mm(p[:, 0:8], up[:], tA[:, n, 0:8], True, True, r=[up, tA], w=p)
                    k.mm(p[:, 8:16], lo[:], tA[:, n, 8:16], True, True, r=[lo, tA], w=p)
                    k.tt("dve", Vc[:, n, :], p[:, 0:16], car[:, n, :], ALU.add, r=[p, car], w=Vc)
                k.tt("dve", bia[:], bia[:], Vc[:], ALU.subtract, r=[bia, Vc], w=bia)
            Q = [k.sb(f"sQ{d}", [128, T], F32) for d in range(2)]
            YZ = k.sb("sYZ", [128, T], F32); SS = k.sb("sSS", [128, T], F32)
            Es = k.rot("sE", [128, 512], F32, 3); Ws = k.rot("sW", [128, 512], BF16, 3)
            tds = k.rot("std", [128, 128], F32, 2)
            szs = k.rot("ssz", [128, 512], F32, 2); sqs = k.rot("ssq", [128, 512], F32, 2)
            wzp = k.rot("swz", [128, 8, 128], BF16, 2)
            accb = Rot(self.ps[0:2]); rotb = Rot(self.ps[2:8])
            for h in range(8):
                g = h // 4
                hp = 64 * (h % 2)
                for d in range(2):
                    for n0 in range(0, NT, 4):
                        p = rotb.next()
                        nn = min(4, NT - n0)
                        for q in range(nn):
                            k.mm(p[:, q * 128:(q + 1) * 128], Vc[:, n0 + q, d * 8 + h:d * 8 + h + 1].to_broadcast([128, 128]), self.ident[:],
                                 True, True, r=[Vc, self.ident], w=p)
                        k.cp("act", Q[d][:, n0 * 128:(n0 + nn) * 128], p[:, 0:nn * 128], r=p, w=Q[d])
                k.preload_lnexp()
                for t0, N in BLOCKS:
                    if t0 < CTX and not do_ctx:
                        continue
                    I0 = t0 // 128
                    nI = N // 128
                    keys = [0, 1] if t0 < CTX else list(range(NT))
                    Y = accb.next()
                    first = True
                    for q in range(nI):
                        k.op("pe", lambda e: e.matmul(Y[hp:hp + 64, q * 128:(q + 1) * 128], lhsT=Xt[:, I0 + q, h * 64:(h + 1) * 64], rhs=dI[:, h, :],
                                                      start=first, stop=False, skip_group_check=True), r=[Xt.b[I0 + q], dI], w=Y)
                        first = False
                    def mkA(J):
                        def A():
                            CBp = rotb.next()
                            k.mm(CBp[:, :N], BT[:, g, J * 128:(J + 1) * 128], CT[:, g, t0:t0 + N], True, True, r=[BT, CT], w=CBp)
                            return CBp
                        return A

                    def mkB(J):
                        def B(CBp):
                            for d in range(2):
                                mk = mf if d == 0 else mb
                                if J < 2 and t0 >= CTX:
                                    lo_, hi_, dlo = 0, N, None
                                else:
                                    dd = J - I0
                                    if d == 0:
                                        lo_, hi_ = max(dd * 128, 0), N
                                    else:
                                        lo_, hi_ = 0, min((dd + 1) * 128, N)
                                    dlo = dd * 128 if 0 <= dd < nI else None
                                if lo_ >= hi_:
                                    continue
                                bcol = bia[:, J, d * 8 + h:d * 8 + h + 1]
                                E = Es.next(); W = Ws.next()
                                segs = [(lo_, hi_)]
                                if dlo is not None:
                                    segs = [(a_, b_) for a_, b_ in ((lo_, dlo), (dlo + 128, hi_)) if a_ < b_]
                                    td = tds.next()
                                    k.tt("pool", td[:], Q[d][:, t0 + dlo:t0 + dlo + 128], mk[:], ALU.add, r=[Q[d], mk], w=td)
                                    k.act(E[:, dlo:dlo + 128], td[:], AF.Exp, r=[td, bia], w=E, bias=bcol)
                                for a_, b_ in segs:
                                    k.act(E[:, a_:b_], Q[d][:, t0 + a_:t0 + b_], AF.Exp, r=[Q[d], bia], w=E, bias=bcol)
                                k.tt("dve", W[:, lo_:hi_], E[:, lo_:hi_], CBp[:, lo_:hi_], ALU.mult, r=[E, CBp], w=W)
                                k.op("pe", lambda e: e.matmul(Y[hp:hp + 64, lo_:hi_], lhsT=Xt[:, J, h * 64:(h + 1) * 64], rhs=W[:, lo_:hi_],
                                                              start=False, stop=False, skip_group_check=True), r=[Xt.b[J], W], w=Y)
                        return B

                    run_pipe([(mkA(J), mkB(J)) for J in keys], 2)
                    k.cp("act", YZ[hp:hp + 64, t0:t0 + N], Y[hp:hp + 64, :N], r=Y, w=YZ)
                if h % 2 == 1:
                    c = h // 2
                    wz = self.wload(wzp, "in", l, 8, [(O_SZ + c * 128, 128)])
                    for t0, N in BLOCKS:
                        if t0 < CTX and not do_ctx:
                            continue
                        p = rotb.next()
                        self.proj(p, N, wz, 0, HM, t0, 8)
                        sz = szs.next(); sq = sqs.next()
                        k.act(sz[:, :N], p[:, :N], AF.Silu, r=p, w=sz)
                        k.tt("pool", YZ[:, t0:t0 + N], YZ[:, t0:t0 + N], sz[:, :N], ALU.mult, r=[YZ, sz], w=YZ)
                        k.tt("pool", sq[:, :N], YZ[:, t0:t0 + N], YZ[:, t0:t0 + N], ALU.mult, r=YZ, w=sq)
                        p2 = rotb.next()
                        k.mm(p2[:, :N], self.ones[:], sq[:, :N], True, True, r=[self.ones, sq], w=p2)
                        if c == 0:
                            k.cp("act", SS[:, t0:t0 + N], p2[:, :N], r=p2, w=SS)
                        else:
                            k.tt("dve", SS[:, t0:t0 + N], SS[:, t0:t0 + N], p2[:, :N], ALU.add, r=[SS, p2], w=SS)
                        k.cp("pool", BR[:, c, t0:t0 + N], YZ[:, t0:t0 + N], r=YZ, w=[BR.b[n] for n in tiles_of(t0, N)])
            k.preload_lnexp()
            k.act(SS[:, tq0:T], SS[:, tq0:T], AF.Ln, r=SS, w=SS, scale=1.0 / 512, bias=1e-6)
            k.act(SS[:, tq0:T], SS[:, tq0:T], AF.Exp, r=SS, w=SS, scale=-0.5)
            for c in range(4):
                k.stt(BR[:, c, tq0:T], BR[:, c, tq0:T], pr[:, 40 + c:41 + c], SS[:, tq0:T], ALU.mult, ALU.mult, r=[BR, pr, SS], w=BR)

    def postnorm_tiles(self, b, l, tiles, mm_fn, src_which, mod_i, lnw, lnb, dst_fn, depth=2, la=None):
        k = self.k
        with k.scope():
            bc = k.sb("bc", [128, 4, D], F32)
            k.dma(bc[:, 0, :], self.MOD[l, b:b + 1, mod_i * D:(mod_i + 1) * D].to_broadcast([128, D]), key=bc, w=bc)
            k.dma(bc[:, 1, :], self.MOD[l, 4:5, mod_i * D:(mod_i + 1) * D].to_broadcast([128, D]), key=bc, w=bc)
            k.dma(bc[:, 2, :], self.din[lnw][l:l + 1, :].to_broadcast([128, D]), key=bc, w=bc)
            k.dma(bc[:, 3, :], self.din[lnb][l:l + 1, :].to_broadcast([128, D]), key=bc, w=bc)
            LA = la or (depth + 1)
            hts = k.rot("pht", [128, D], F32, LA + 1); t1s = k.rot("pt1", [128, D], F32, depth + 2)
            sts = k.rot("pst", [128, 16], F32, 4)
            tl = list(tiles)
            loads = {}

            def L(i):
                if i < len(tl):
                    ht = hts.next()
                    k.dma(ht[:], self.hsrc(b, l, tl[i], src_which), key=ht, w=ht)
                    loads[i] = ht

            for i in range(LA):
                L(i)

            def mkA(i, n):
                def A():
                    L(i + LA)
                    j = 1 if n < 2 else 0
                    o = [self.psum(), self.psum()]
                    mm_fn(n, o)
                    ht = loads.pop(i); t1 = t1s.next(); st = sts.next()
                    for half in range(2):
                        k.tt("dve", t1[:, half * 512:(half + 1) * 512], o[half][:, :], bc[:, j, half * 512:(half + 1) * 512], ALU.mult, r=[o[half], bc], w=t1)
                    k.stt(t1[:], ht[:], ALPHA, t1[:], ALU.mult, ALU.add, r=[ht, t1], w=t1)
                    self.ln_stats(t1, st)
                    return t1, st
                return A

            def mkB(n):
                def B(state):
                    t1, st = state
                    self.ln_apply(t1, t1, st)
                    k.tt("pool", t1[:], t1[:], bc[:, 2, :], ALU.mult, r=[t1, bc], w=t1)
                    k.tt("pool", t1[:], t1[:], bc[:, 3, :], ALU.add, r=[t1, bc], w=t1)
                    k.dma(dst_fn(n), t1[:], key=t1, r=t1)
                return B

            run_pipe([(mkA(i, n), mkB(n)) for i, n in enumerate(tl)], depth)

    def merge(self, b, l, HM, BR, do_ctx):
        k = self.k
        blocks = [bl for bl in BLOCKS if do_ctx or bl[0] >= CTX]
        tiles = list(range(0 if do_ctx else 2, NT))
        with k.scope():
            ACC = k.sb("ACC", [128, 8, T], BF16, nb=NT)
            with k.scope():
                wgp = k.rot("wgate", [128, 8, 512], BF16, 2)
                wbp = k.rot("wbr", [128, 4, 512], BF16, 2)
                sgs = k.rot("msg", [128, 512], F32, 3); accs = k.rot("macc", [128, 512], F32, 2); tms = k.rot("mtm", [128, 512], F32, 2)
                rotb = Rot(self.ps)
                for oc in range(8):
                    wg = self.wload(wgp, "in", l, 8, [(1024 * i + 128 * oc, 128) for i in range(4)])
                    wb = wbp.next()
                    k.dma_group([(wb[:, 0:4, i * 128:(i + 1) * 128],
                                  self.WB["br"][l][512 * i:512 * (i + 1), oc * 128:(oc + 1) * 128].rearrange("(kc p) n -> p kc n", p=128)) for i in range(4)],
                                key=wb, w=wb)
                    for t0, N in blocks:
                        acc = accs.next()
                        tl = [ACC.b[n] for n in tiles_of(t0, N)]
                        for i in range(4):
                            G = rotb.next()
                            self.proj(G, N, wg, i * 128, HM, t0, 8)
                            Pj = rotb.next()
                            self.proj(Pj, N, wb, i * 128, BR[i], t0, 4)
                            sg = sgs.next()
                            k.act(sg[:, :N], G[:, :N], AF.Sigmoid, r=G, w=sg)
                            if i == 0:
                                k.tt("dve", acc[:, :N], Pj[:, :N], sg[:, :N], ALU.mult, r=[Pj, sg], w=acc)
                            else:
                                tm = tms.next()
                                k.tt("dve", tm[:, :N], Pj[:, :N], sg[:, :N], ALU.mult, r=[Pj, sg], w=tm)
                                if i < 3:
                                    k.tt("pool", acc[:, :N], acc[:, :N], tm[:, :N], ALU.add, r=[acc, tm], w=acc)
                                else:
                                    k.tt("pool", ACC[:, oc, t0:t0 + N], acc[:, :N], tm[:, :N], ALU.add, r=[acc, tm], w=tl)
            if self.dbg.get("stop") == "acc":
                self.dump("acc", ACC[:], [128, 8, T], BF16, r=ACC)
                return
            with k.scope():
                wo = self.wload(k.rot("wo", [128, 8, 1024], BF16, 1), "out", l, 8, [(0, 1024)])

                def mmf(n, o):
                    for half in range(2):
                        for kc in range(8):
                            k.mm(o[half][:, :], ACC[:, kc, n * 128:(n + 1) * 128], wo[:, kc, half * 512:(half + 1) * 512], kc == 0, kc == 7,
                                 r=[ACC.b[n], wo], w=o[half])

                self.postnorm_tiles(b, l, tiles, mmf, 0, 2, "ln1_w", "ln1_b", lambda n: self.HS[1][n * 128:(n + 1) * 128, :], depth=1)

    def ffn(self, b, l, do_ctx, is_last):
        k = self.k
        tq0 = 0 if do_ctx else CTX
        blocks = [bl for bl in BLOCKS if do_ctx or bl[0] >= CTX]
        tiles = list(range(0 if do_ctx else 2, NT))
        segs = [(CTX, T)] if not do_ctx else [(0, CTX), (CTX, T)]
        with k.scope():
            ACTT = k.sb("ACTT", [128, 22, T], BF16, nb=NT)
            with k.scope():
                HM2 = k.sb("HM2", [128, 8, T], BF16, nb=NT)
                HM2.b2 = [Buf(f"HM2x{i}") for i in range(NT)]
                self.ln_mod_phase(b, l, 1, HM2, tiles)
                prA = k.sb("fprA", [128, 88], F32); prB = k.sb("fprB", [128, 88], F32)
                self.fm_params(prA, [self.din["ffn_conv_w"][l, 0:2, :].rearrange("k (c p) -> (k c) p", p=128)])
                self.fm_params(prB, [self.din["ffn_conv_w"][l, 2, :].rearrange("(c p) -> c p", p=128), self.din["ffn_conv_b"][l].rearrange("(c p) -> c p", p=128)])
                wup = k.rot("wup", [128, 8, 256], BF16, 2)
                U = [k.sb(f"fU{i}", [128, T], F32) for i in range(2)]
                Yv = [k.sb(f"fY{i}", [128, T], F32) for i in range(2)]
                for c in range(22):
                    w = self.wload(wup, "in" if False else "up", l, 8, [(c * 128, 128), (DFF + c * 128, 128)])
                    for part in range(2):
                        cc = part * 22 + c
                        Up = U[part]; Y = Yv[part]
                        for t0, N in blocks:
                            p = self.psum()
                            self.proj(p, N, w, part * 128, HM2, t0, 8)
                            k.cp("act", Up[:, t0:t0 + N], p[:, :N], r=p, w=Up)
                        eng = "dve" if part == 0 else "pool"
                        k.ts(eng, Y[:, tq0:T], Up[:, tq0:T], prA[:, 44 + cc:45 + cc], prB[:, 44 + cc:45 + cc], ALU.mult, ALU.add, r=[Up, prA, prB], w=Y)
                        for s_, e_ in segs:
                            k.stt(Y[:, s_ + 1:e_], Up[:, s_:e_ - 1], prA[:, cc:cc + 1], Y[:, s_ + 1:e_], ALU.mult, ALU.add, r=[Up, Y, prA], w=Y)
                            k.stt(Y[:, s_:e_ - 1], Up[:, s_ + 1:e_], prB[:, cc:cc + 1], Y[:, s_:e_ - 1], ALU.mult, ALU.add, r=[Up, Y, prB], w=Y)
                    k.act(Yv[0][:, tq0:T], Yv[0][:, tq0:T], AF.Silu, r=Yv[0], w=Yv[0])
                    k.tt("pool", ACTT[:, c, tq0:T], Yv[0][:, tq0:T], Yv[1][:, tq0:T], ALU.mult, r=[Yv[0], Yv[1]], w=ACTT)
            with k.scope():
                wd = k.sb("wd", [128, 22, D], BF16)
                srcw = self.WB["dn"][l].rearrange("(kc p) n -> p kc n", p=128)
                k.dma_group([(wd[:, 0:8, :], srcw[:, 0:8, :]), (wd[:, 8:16, :], srcw[:, 8:16, :]), (wd[:, 16:22, :], srcw[:, 16:22, :])], key=wd, w=wd)

                def mmf(n, o):
                    for half in range(2):
                        for c in range(22):
                            k.mm(o[half][:, :], ACTT[:, c, n * 128:(n + 1) * 128], wd[:, c, half * 512:(half + 1) * 512], c == 0, c == 21,
                                 r=[ACTT, wd], w=o[half])

                if is_last:
                    dst = lambda n: self.out[b, (n - 2) * 128:(n - 1) * 128, :]
                else:
                    dst = lambda n: self.HS[0][n * 128:(n + 1) * 128, :]
                self.postnorm_tiles(b, l, tiles, mmf, 1, 5, "ln2_w", "ln2_b", dst, depth=1, la=4)

    def layer(self, b, l):
        k = self.k
        last = (l == L - 1)
        with k.scope():
            HM = k.sb("HM", [128, 8, T], BF16, nb=NT)
            HM.b2 = [Buf(f"HMx{i}") for i in range(NT)]
            self.ln_mod_phase(b, l, 0, HM, range(NT))
            if self.dbg.get("stop") == "hm":
                self.dump("hm", HM[:], [128, 8, T], BF16, r=HM)
                return
            BR = [None] * 4
            BR[0] = k.sb("BR0", [128, 4, T], BF16, nb=NT)
            if not self.dbg.get("skip_ret"):
                self.retention(b, l, HM, BR[0], not last)
            if self.dbg.get("stop") == "ret":
                self.dump("ret", BR[0][:], [128, 4, T], BF16, r=BR[0])
                return
            BR[1] = k.sb("BR1", [128, 4, T], BF16, nb=NT)
            if not self.dbg.get("skip_mla"):
                self.mla(b, l, HM, BR[1], not last)
            if self.dbg.get("stop") == "mla":
                self.dump("mla", BR[1][:], [128, 4, T], BF16, r=BR[1])
                return
            BR[3] = k.sb("BR3", [128, 4, T], BF16, nb=NT)
            if not self.dbg.get("skip_ssd"):
                self.ssd(b, l, HM, BR[3], not last)
            if self.dbg.get("stop") == "ssd":
                self.dump("ssd", BR[3][:], [128, 4, T], BF16, r=BR[3])
                return
            BR[2] = k.sb("BR2", [128, 4, T], BF16, nb=NT)
            if not self.dbg.get("skip_lru"):
                self.lru(b, l, HM, BR[2], not last)
            if self.dbg.get("stop") == "lru":
                for i, nm in enumerate(("ret", "mla", "lru", "ssd")):
                    self.dump(nm, BR[i][:], [128, 4, T], BF16, r=BR[i])
                return
            self.merge(b, l, HM, BR, not last)
            if self.dbg.get("stop") == "acc":
                return
        if self.dbg.get("stop") == "h1":
            self.dump("h1", self.HS[1], [T, D], F32, r=Buf("dummy"))
            return
        self.ffn(b, l, not last, last)
        if self.dbg.get("stop") == "h2" and l == self.nl - 1:
            self.dump("h2", self.HS[0], [T, D], F32, r=Buf("dummy2"))


def host_consts():
    c = {}
    c["k_ident"] = np.eye(128, dtype=np.float32)
    c["k_ones"] = np.ones((128, 128), np.float32)
    c["k_identb"] = np.eye(128).astype(ml_dtypes.bfloat16)
    c["k_onesb"] = np.ones((128, 128)).astype(ml_dtypes.bfloat16)

    def rope(dim):
        rows = SEQ // 64
        r, col = np.meshgrid(np.arange(rows, dtype=np.float32), np.arange(64, dtype=np.float32), indexing="ij")
        quarter = dim // 4
        inv = (np.float32(10000.0) ** (-np.arange(quarter, dtype=np.float32) / np.float32(quarter))).astype(np.float32)
        ang = np.concatenate([r.reshape(-1, 1) * inv, col.reshape(-1, 1) * inv], axis=-1).astype(np.float32)
        return np.cos(ang).astype(np.float32), np.sin(ang).astype(np.float32)

    cs, sn = rope(128)
    p = np.arange(128)
    c["k_retC"] = np.ascontiguousarray(cs[:, p % 64].T)
    c["k_retS"] = np.ascontiguousarray((sn[:, p % 64] * np.where(p < 64, -1.0, 1.0)[None, :]).T.astype(np.float32))
    cs, sn = rope(64)
    c["k_mlaC"] = np.ascontiguousarray(cs[:, p % 32].T)
    c["k_mlaS"] = np.ascontiguousarray((sn[:, p % 32] * np.where((p % 64) < 32, -1.0, 1.0)[None, :]).T.astype(np.float32))
    j = np.arange(128)[:, None]; i = np.arange(128)[None, :]
    c["k_mf"] = np.where(i >= j, 0.0, NEG).astype(np.float32)
    c["k_mb"] = np.where(i <= j, 0.0, NEG).astype(np.float32)
    c["k_perm"] = (j == (i + 64) % 128).astype(np.float32).astype(ml_dtypes.bfloat16)
    c["k_up"] = (j <= i).astype(np.float32)
    c["k_lo"] = (j >= i).astype(np.float32)
    t = np.arange(T)
    posf = (t + 1).astype(np.float32)
    posb = np.where(t < CTX, CTX - t, T - (t - CTX)).astype(np.float32)
    c["k_posf"] = np.ascontiguousarray(np.broadcast_to(posf[None, :], (128, T)))
    c["k_posb"] = np.ascontiguousarray(np.broadcast_to(posb[None, :], (128, T)))
    c["k_posf_tm"] = np.ascontiguousarray(posf.reshape(NT, 128).T)
    c["k_posb_tm"] = np.ascontiguousarray(posb.reshape(NT, 128).T)
    return c


def make_in_maps(inputs, n_cores, nb):
    consts = host_consts()
    f = lambda a: np.ascontiguousarray(np.asarray(a, dtype=np.float32))
    shared = {
        "ada_w": f(inputs["ada_w"]), "ada_b": f(inputs["ada_b"]), "w_in": f(inputs["w_in"]),
        "ret_decay": f(inputs["ret_decay"]).reshape(L, 8), "ret_gn_w": f(inputs["ret_gn_w"]), "ret_gn_b": f(inputs["ret_gn_b"]),
        "mla_q_norm": f(inputs["mla_q_norm"]), "mla_w_uq": f(inputs["mla_w_uq"]), "mla_kv_norm": f(inputs["mla_kv_norm"]), "mla_w_ukv": f(inputs["mla_w_ukv"]),
        "lru_conv_w": f(inputs["lru_conv_w"]), "lru_conv_b": f(inputs["lru_conv_b"]), "lru_gate_w": f(inputs["lru_gate_w"]),
        "lru_gate_b": f(inputs["lru_gate_b"]), "lru_lambda": f(inputs["lru_lambda"]),
        "ssd_conv_w": f(inputs["ssd_conv_w"]), "ssd_conv_b": f(inputs["ssd_conv_b"]), "ssd_dt_bias": f(inputs["ssd_dt_bias"]).reshape(L, 16),
        "ssd_a_log": f(inputs["ssd_a_log"]).reshape(L, 16), "ssd_d": f(inputs["ssd_d"]), "ssd_norm_w": f(inputs["ssd_norm_w"]),
        "w_branch": f(inputs["w_branch"]).reshape(L, 2048, D), "w_out": f(inputs["w_out"]), "ln1_w": f(inputs["ln1_w"]), "ln1_b": f(inputs["ln1_b"]),
        "ffn_w_up": f(inputs["ffn_w_up"]), "ffn_conv_w": f(inputs["ffn_conv_w"]), "ffn_conv_b": f(inputs["ffn_conv_b"]), "ffn_w_down": f(inputs["ffn_w_down"]),
        "ln2_w": f(inputs["ln2_w"]), "ln2_b": f(inputs["ln2_b"]),
    }
    shared.update(consts)
    x = f(inputs["x"]); ctx = f(inputs["ctx"]); c = f(inputs["c"]); cc = f(inputs["c_ctx"])
    maps = []
    for i in range(n_cores):
        sl = slice(i * nb, (i + 1) * nb)
        c5 = np.zeros((5, D), np.float32)
        c5[:nb] = c[sl]
        c5[4] = cc
        c5T = np.ascontiguousarray(c5.reshape(5, 8, 128).transpose(2, 1, 0).reshape(128, 40))
        m = dict(shared)
        m.update({"x": np.ascontiguousarray(x[sl]), "ctx": np.ascontiguousarray(ctx[sl]), "c5T": c5T})
        maps.append(m)
    return maps


def kernel(**inputs):
    n_cores = 8
    prog = Prog(NB, L)
    nc = prog.build()
    maps = make_in_maps(inputs, n_cores, NB)
    res = run_bass_kernel_spmd(nc, maps, core_ids=list(range(n_cores)))
    out = np.concatenate([np.asarray(r["out"]) for r in res.results], axis=0)
    return out.astype(np.float32)
```
